# Optimizing a Trainium2 kernel written in Bass

```python
import math
import jax, jax.numpy as jnp
from jax import lax
import numpy as np


D_MODEL = 2048
BATCH = 4
SEQ = 2048
DEPTH = 2
DEC_BATCH = 32
DEC_SEQ = 8
PAST_LEN = 16384
PAGE_SIZE = 128

N_A_LAYERS = DEPTH // 2
N_B_LAYERS = DEPTH - N_A_LAYERS
GLA_HEADS = 4
GLA_KEY_DIM = D_MODEL // 2
GLA_VALUE_DIM = D_MODEL
GLA_DK = GLA_KEY_DIM // GLA_HEADS
GLA_DV = GLA_VALUE_DIM // GLA_HEADS
GLA_GATE_RANK = 16
GLA_GATE_TEMP = 16.0
GLA_CHUNK = 64
SWA_HEAD_DIM = 64
SWA_HEADS = D_MODEL // SWA_HEAD_DIM
SWA_KV_HEADS = 4
SWA_WIDTH = SWA_HEADS * SWA_HEAD_DIM
WINDOW = 128
ATTN_BLOCK = WINDOW
RMS_EPS = 1e-6

kernel_name = 'gla_swa_sink_yoco_hybrid_step'


def rmsnorm(x, g):
    x32 = x.astype(jnp.float32)
    y = x32 * lax.rsqrt(jnp.mean(x32 * x32, axis=-1, keepdims=True) + RMS_EPS)
    return (y * g.astype(jnp.float32)).astype(x.dtype)


def alibi_slopes(n_heads):
    return jnp.exp2(-8.0 * jnp.arange(1, n_heads + 1, dtype=jnp.float32) / n_heads)


def gla_chunked(q, k, v, logg, s0):
    B, L, H, DK = q.shape
    DV = v.shape[-1]
    C = math.gcd(L, GLA_CHUNK)
    n = L // C

    def blocks(t):
        return t.astype(jnp.float32).reshape(B, n, C, H, t.shape[-1]).transpose(1, 0, 3, 2, 4)

    causal = jnp.tril(jnp.ones((C, C), dtype=bool))[:, :, None]

    def step(S, inp):
        qc, kc, vc, gc = inp
        b = jnp.cumsum(gc, axis=2)
        o_inter = jnp.einsum('bhtd,bhde->bhte', qc * jnp.exp(b), S)
        diff = b[:, :, :, None, :] - b[:, :, None, :, :]
        decay = jnp.exp(jnp.where(causal, diff, -jnp.inf))
        scores = jnp.einsum('bhtd,bhsd,bhtsd->bhts', qc, kc, decay)
        o_intra = jnp.einsum('bhts,bhse->bhte', scores, vc)
        b_last = b[:, :, -1:, :]
        S_new = (jnp.exp(b_last[:, :, 0, :])[..., None] * S
                 + jnp.einsum('bhsd,bhse->bhde', kc * jnp.exp(b_last - b), vc))
        return S_new, o_inter + o_intra

    S_fin, o = lax.scan(step, s0.astype(jnp.float32),
                        (blocks(q), blocks(k), blocks(v), blocks(logg)))
    o = o.transpose(1, 0, 3, 2, 4).reshape(B, L, H, DV)
    return o, S_fin


def gla_layer(h, s0, g_norm, w_in, w_gate_up, b_gate, g_onorm, w_out):
    B, L, _ = h.shape
    xn = rmsnorm(h, g_norm)
    proj = xn @ w_in
    q, k, v, r, glow = jnp.split(
        proj, [GLA_KEY_DIM, 2 * GLA_KEY_DIM, 2 * GLA_KEY_DIM + GLA_VALUE_DIM,
               2 * GLA_KEY_DIM + 2 * GLA_VALUE_DIM], axis=-1)
    logg = jax.nn.log_sigmoid((glow @ w_gate_up + b_gate).astype(jnp.float32)) / GLA_GATE_TEMP
    q = q.reshape(B, L, GLA_HEADS, GLA_DK).astype(jnp.float32) * (GLA_DK ** -0.5)
    k = k.reshape(B, L, GLA_HEADS, GLA_DK)
    v = v.reshape(B, L, GLA_HEADS, GLA_DV)
    logg = logg.reshape(B, L, GLA_HEADS, GLA_DK)
    o, S = gla_chunked(q, k, v, logg, s0)
    o = rmsnorm(o, g_onorm.reshape(GLA_HEADS, GLA_DV))
    o = o.reshape(B, L, GLA_VALUE_DIM).astype(h.dtype) * jax.nn.silu(r)
    return h + o @ w_out, S.astype(s0.dtype)


def sink_attention(q, k, v, qpos, kpos, sinks):
    B, N, Tq, H, HD = q.shape
    KVH = k.shape[3]
    G = H // KVH
    qg = q.reshape(B, N, Tq, KVH, G, HD).astype(jnp.float32)
    s = jnp.einsum('bnqkgd,bnskd->bkgnqs', qg, k.astype(jnp.float32)) * (HD ** -0.5)
    dist = qpos[:, :, None] - kpos[:, None, :]
    allowed = (dist >= 0) & (dist <= WINDOW) & (kpos >= 0)[:, None, :]
    slopes = alibi_slopes(H).reshape(KVH, G)[:, :, None, None, None]
    s = s - slopes * dist.astype(jnp.float32)
    s = jnp.where(allowed, s, -jnp.inf)
    sink = sinks.astype(jnp.float32).reshape(KVH, G)[:, :, None, None, None]
    m = jnp.maximum(jnp.max(s, axis=-1, keepdims=True), sink)
    p = jnp.exp(s - m)
    denom = jnp.sum(p, axis=-1, keepdims=True) + jnp.exp(sink - m)
    o = jnp.einsum('bkgnqs,bnskd->bnqkgd', p / denom, v.astype(jnp.float32))
    return o.reshape(B, N * Tq, H * HD)


def swa_layer(h, k, v, k_buf, v_buf, g_norm, w_in, sinks, w_out):
    B, L, _ = h.shape
    xn = rmsnorm(h, g_norm)
    q, z = jnp.split(xn @ w_in, 2, axis=-1)
    q = q.reshape(B, L, SWA_HEADS, SWA_HEAD_DIM)
    if k_buf is None:
        nb = L // ATTN_BLOCK
        qb = q.reshape(B, nb, ATTN_BLOCK, SWA_HEADS, SWA_HEAD_DIM)

        def band(t):
            prev = jnp.pad(t, ((0, 0), (ATTN_BLOCK, 0), (0, 0), (0, 0)))[:, :L]
            shp = (B, nb, ATTN_BLOCK, SWA_KV_HEADS, SWA_HEAD_DIM)
            return jnp.concatenate([prev.reshape(shp), t.reshape(shp)], axis=2)

        kb, vb = band(k), band(v)
        qpos = jnp.arange(L).reshape(nb, ATTN_BLOCK)
        kpos = (jnp.arange(nb)[:, None] - 1) * ATTN_BLOCK + jnp.arange(2 * ATTN_BLOCK)[None, :]
    else:
        Wb = k_buf.shape[1]
        qb = q[:, None]
        kb = jnp.concatenate([k_buf.astype(k.dtype), k], axis=1)[:, None]
        vb = jnp.concatenate([v_buf.astype(v.dtype), v], axis=1)[:, None]
        qpos = (Wb + jnp.arange(L))[None]
        kpos = jnp.arange(Wb + L)[None]
    o = sink_attention(qb, kb, vb, qpos, kpos, sinks)
    o = o.astype(h.dtype) * jax.nn.silu(z)
    return h + o @ w_out


def trunk(x, gla_init, k_buf, v_buf, g_norm_a, w_in_a, w_gate_up, b_gate, g_onorm_a, w_out_a,
          g_norm_kv, w_kv, g_norm_b, w_in_b, sinks, w_out_b, g_final):
    B, L, _ = x.shape
    h = x
    new_states = []
    k = v = None
    for layer in range(DEPTH):
        if layer < N_A_LAYERS:
            h, S = gla_layer(h, gla_init[layer], g_norm_a[layer], w_in_a[layer], w_gate_up[layer],
                             b_gate[layer], g_onorm_a[layer], w_out_a[layer])
            new_states.append(S)
        else:
            if layer == N_A_LAYERS:
                kv = rmsnorm(h, g_norm_kv) @ w_kv
                k, v = jnp.split(kv, 2, axis=-1)
                k = k.reshape(B, L, SWA_KV_HEADS, SWA_HEAD_DIM)
                v = v.reshape(B, L, SWA_KV_HEADS, SWA_HEAD_DIM)
            j = layer - N_A_LAYERS
            h = swa_layer(h, k, v, k_buf, v_buf, g_norm_b[j], w_in_b[j], sinks[j], w_out_b[j])
    y = rmsnorm(h, g_final)
    return y, jnp.stack(new_states, axis=0), k, v


def setup_inputs(seed: int = 0) -> dict:
    key = jax.random.key(seed)
    ks = jax.random.split(key, 20)
    f32 = jnp.float32
    D = D_MODEL
    n_in_a = 2 * GLA_KEY_DIM + 2 * GLA_VALUE_DIM + GLA_GATE_RANK
    win_rows = min(WINDOW, PAST_LEN)
    nrm = lambda k_, shp, sc: jax.random.normal(k_, shp, f32) * sc
    return {
        'x_prompt': nrm(ks[0], (BATCH, SEQ, D), 1.0),
        'x_sample': nrm(ks[1], (DEC_BATCH, DEC_SEQ, D), 1.0),
        'state_gla': nrm(ks[2], (N_A_LAYERS, DEC_BATCH, GLA_HEADS, GLA_DK, GLA_DV), 0.3),
        'cache_k_win': nrm(ks[3], (DEC_BATCH, win_rows, SWA_KV_HEADS, SWA_HEAD_DIM), 1.0),
        'cache_v_win': nrm(ks[4], (DEC_BATCH, win_rows, SWA_KV_HEADS, SWA_HEAD_DIM), 1.0),
        'g_norm_a': 1.0 + nrm(ks[5], (N_A_LAYERS, D), 0.02),
        'w_in_a': nrm(ks[6], (N_A_LAYERS, D, n_in_a), D ** -0.5),
        'w_gate_up': nrm(ks[7], (N_A_LAYERS, GLA_GATE_RANK, GLA_KEY_DIM), GLA_GATE_RANK ** -0.5),
        'b_gate': nrm(ks[8], (N_A_LAYERS, GLA_KEY_DIM), 0.1),
        'g_onorm_a': 1.0 + nrm(ks[9], (N_A_LAYERS, GLA_VALUE_DIM), 0.02),
        'w_out_a': nrm(ks[10], (N_A_LAYERS, GLA_VALUE_DIM, D), GLA_VALUE_DIM ** -0.5),
        'g_norm_kv': 1.0 + nrm(ks[11], (D,), 0.02),
        'w_kv': nrm(ks[12], (D, 2 * SWA_KV_HEADS * SWA_HEAD_DIM), D ** -0.5),
        'g_norm_b': 1.0 + nrm(ks[13], (N_B_LAYERS, D), 0.02),
        'w_in_b': nrm(ks[14], (N_B_LAYERS, D, 2 * SWA_WIDTH), D ** -0.5),
        'sinks': nrm(ks[15], (N_B_LAYERS, SWA_HEADS), 1.0),
        'w_out_b': nrm(ks[16], (N_B_LAYERS, SWA_WIDTH, D), SWA_WIDTH ** -0.5),
        'g_final': 1.0 + nrm(ks[17], (D,), 0.02),
    }


def reference(x_prompt, x_sample, state_gla, cache_k_win, cache_v_win, g_norm_a, w_in_a,
              w_gate_up, b_gate, g_onorm_a, w_out_a, g_norm_kv, w_kv, g_norm_b, w_in_b, sinks,
              w_out_b, g_final):
    gla0 = jnp.zeros((N_A_LAYERS, x_prompt.shape[0], GLA_HEADS, GLA_DK, GLA_DV), x_prompt.dtype)
    y_prompt, gla_prompt, k_p, v_p = trunk(
        x_prompt, gla0, None, None, g_norm_a, w_in_a, w_gate_up, b_gate, g_onorm_a, w_out_a,
        g_norm_kv, w_kv, g_norm_b, w_in_b, sinks, w_out_b, g_final)
    y_sample, gla_sample, k_s, v_s = trunk(
        x_sample, state_gla, cache_k_win, cache_v_win, g_norm_a, w_in_a, w_gate_up, b_gate,
        g_onorm_a, w_out_a, g_norm_kv, w_kv, g_norm_b, w_in_b, sinks, w_out_b, g_final)
    win_p = min(WINDOW, k_p.shape[1])
    k_win_prompt = k_p[:, -win_p:]
    v_win_prompt = v_p[:, -win_p:]
    wb = cache_k_win.shape[1]
    k_win_sample = jnp.concatenate([cache_k_win, k_s.astype(cache_k_win.dtype)], axis=1)[:, -wb:]
    v_win_sample = jnp.concatenate([cache_v_win, v_s.astype(cache_v_win.dtype)], axis=1)[:, -wb:]
    return (y_prompt, y_sample, gla_prompt, gla_sample, k_win_prompt, v_win_prompt,
            k_win_sample, v_win_sample)
```

```python
import numpy as np
from contextlib import ExitStack
import concourse.bass as bass
import concourse.mybir as mybir
from concourse.bass_utils import run_bass_kernel_spmd

F32 = mybir.dt.float32
BF16 = mybir.dt.bfloat16
AF = mybir.ActivationFunctionType
ALU = mybir.AluOpType
AX = mybir.AxisListType


class Buf:
    __slots__ = ("name", "w", "r", "dsem", "excl")

    def __init__(self, name, dsem=None, excl=False):
        self.name = name
        self.w = None
        self.r = {}
        self.dsem = dsem
        self.excl = excl


class Sched:
    ENG = ("pe", "act", "dve", "pool", "sp")

    def __init__(self, nc, stack):
        self.nc = nc
        self.stack = stack
        self.sems = {}
        self.cnt = {}
        self.streams = {e: [] for e in self.ENG}
        self.waited = {e: {} for e in self.ENG}
        for e in self.ENG:
            self._sem("e_" + e)
        self.nwaits = 0

    def _sem(self, key):
        if key not in self.sems:
            self.sems[key] = self.stack.enter_context(self.nc.semaphore(key))
            self.cnt[key] = 0
        return key

    def _need(self, eng, tok, waits):
        key, val = tok
        if key == "e_pe" and eng == "pe":
            return
        if key == "e_sp" and eng == "sp":
            return
        if self.waited[eng].get(key, 0) >= val:
            return
        self.waited[eng][key] = val
        waits.append((key, val))

    def _deps(self, eng, reads, writes):
        waits = []
        for b in reads:
            if b.w is not None:
                self._need(eng, b.w, waits)
        for b in writes:
            if b.w is not None:
                self._need(eng, b.w, waits)
            for k, v in b.r.items():
                self._need(eng, (k, v), waits)
        self.nwaits += len(waits)
        return waits

    def _mark(self, tok, reads, writes):
        k, v = tok
        for b in reads:
            if b.r.get(k, 0) < v:
                b.r[k] = v
        for b in writes:
            b.w = tok
            b.r = {}

    def op(self, eng, fn, reads=(), writes=(), inc=True):
        ex = [b for b in reads if b.excl]
        if ex:
            writes = list(writes) + [b for b in ex if b not in writes]
            reads = [b for b in reads if not b.excl]
        waits = self._deps(eng, reads, writes)
        key = "e_" + eng
        if inc:
            self.cnt[key] += 1
            tok = (key, self.cnt[key])
        else:
            tok = (key, self.cnt[key] + 1)
        self._mark(tok, reads, writes)
        self.streams[eng].append((waits, fn, (key, 1) if inc else None))

    def dma(self, eng, out, in_, reads=(), writes=(), sem=None):
        waits = self._deps(eng, reads, writes)
        if sem is None:
            for b in list(writes) + list(reads):
                if b.dsem is not None:
                    sem = b.dsem
                    break
        assert sem is not None
        key = self._sem("d_" + sem)
        self.cnt[key] += 16
        tok = (key, self.cnt[key])
        self._mark(tok, reads, writes)
        self.streams[eng].append(
            (waits, lambda e, o=out, i=in_: e.dma_start(out=o, in_=i), (key, 16)))

    def finish(self, eng="sp"):
        waits = []
        for key, c in self.cnt.items():
            if key.startswith("d_") and c > 0:
                self._need(eng, (key, c), waits)
        self.streams[eng].append((waits, None, None))

    def emit(self):
        nc = self.nc
        streams = self.streams
        sems = self.sems

        def replay(name, e):
            for waits, fn, inc in streams[name]:
                for key, val in waits:
                    e.wait_ge(sems[key], val)
                if fn is None:
                    continue
                ins = fn(e)
                if inc is not None:
                    ins.then_inc(sems[inc[0]], inc[1])

        with nc.Block() as block:
            @block.tensor
            def _(e):
                replay("pe", e)

            @block.scalar
            def _(e):
                replay("act", e)

            @block.vector
            def _(e):
                replay("dve", e)

            @block.gpsimd
            def _(e):
                replay("pool", e)

            @block.sync
            def _(e):
                replay("sp", e)

D = 2048
NEG = -30000.0
EPS = 1e-6
KB = 1024


def _prod(xs):
    r = 1
    for v in xs:
        r *= v
    return r


class Arena:
    def __init__(self, nc, st, nbytes):
        self.n = nbytes
        self.t = st.enter_context(nc.sbuf_tensor("arena", [128, nbytes // 2], BF16))
        self.lo = 0
        self.hi = nbytes

    def at(self, off, shape, dt):
        esz = 4 if dt == F32 else 2
        n = _prod(shape[1:]) * esz
        assert off % 4 == 0 and off + n <= self.n, (off, n, self.n)
        a = self.t[:, off // 2:(off + n) // 2]
        if dt != BF16:
            a = a.bitcast(dt)
        if len(shape) == 3:
            a = a.rearrange("p (a b) -> p a b", b=shape[2])
        elif len(shape) == 4:
            a = a.rearrange("p (a b c) -> p a b c", b=shape[2], c=shape[3])
        elif len(shape) == 5:
            a = a.rearrange("p (a b c d) -> p a b c d", b=shape[2], c=shape[3], d=shape[4])
        return a

    def setrange(self, lo, hi):
        self.lo, self.hi = lo, hi

    def alloc(self, shape, dt):
        esz = 4 if dt == F32 else 2
        n = _prod(shape[1:]) * esz
        n = (n + 63) // 64 * 64
        assert self.lo + n <= self.hi, ("arena overflow", self.lo, n, self.hi)
        a = self.at(self.lo, shape, dt)
        self.lo += n
        return a


class _Stop(Exception):
    pass


def build_program():
    import os
    STOP = os.environ.get('MK_STOP', '')

    def stop_check(p):
        if STOP == p:
            raise _Stop()
    nc = bass.Bass("TRN2", target_bir_lowering=False)

    def din(name, shape, dt=F32):
        return nc.dram_tensor(name, list(shape), dt, kind="ExternalInput").ap()

    def dout(name, shape, dt=F32):
        return nc.dram_tensor(name, list(shape), dt, kind="ExternalOutput").ap()

    xw = din("xw", [2048, D]); xs = din("xs", [32, D])
    st_in = din("st_in", [4, 4, 256, 512])
    ck = din("ck", [4, 128, 256]); cv = din("cv", [4, 128, 256])
    w_in_a = din("w_in_a", [D, 6160]); w_out_a = din("w_out_a", [D, D])
    w_kv = din("w_kv", [D, 512]); w_in_b = din("w_in_b", [D, 4096]); w_out_b = din("w_out_b", [D, D])
    wgu_d = din("wgu", [16, 1024])
    gon_d = din("gon_bc", [128, D]); gfin_d = din("gfin_bc", [128, D])
    NCP = 768 + 8 + 4 + 1024 + 48 + 32 + 1 + 256 + 3
    cpack_d = din("cpack", [128, NCP]); cpack16_d = din("cpack16", [128, 160], BF16)
    biasp_d = din("bias_p", [4, 128, 2, 1024]); biass_d = din("bias_s", [4, 128, 5, 256])

    y_o = dout("y", [1024, D]); ys_o = dout("ys", [32, D])
    sfin_o = dout("sfin", [4, 256, 512]); ssamp_o = dout("ssamp", [4, 4, 256, 512])
    kwin_o = dout("kwin", [128, 256]); vwin_o = dout("vwin", [128, 256])
    kwins_o = dout("kwin_s", [4, 128, 256]); vwins_o = dout("vwin_s", [4, 128, 256])
    h1s = nc.dram_tensor("h1s", [1056, D], F32).ap()

    with ExitStack() as st:
        S = Sched(nc, st)
        sbt = lambda name, shape, dt: st.enter_context(nc.sbuf_tensor("sb_" + name, shape, dt))
        NW = 2
        wbuf = [sbt(f"wbuf{i}", [128, 16, 512], BF16) for i in range(NW)]
        bwbuf = [Buf(f"wbuf{i}", f"w{i}") for i in range(NW)]
        b_const = Buf("const", "cst")
        cpack = sbt("cpack", [128, NCP], F32); cpack16 = sbt("cpack16", [128, 160], BF16)
        ident = cpack16[:, 0:128]; foldm = cpack16[:, 128:160]
        sfold = [sbt(f"sfold{i}", [128, 512], BF16) for i in range(2)]; b_sfold = [Buf(f"sfold{i}") for i in range(2)]
        _o = 0
        glac = cpack[:, _o:_o + 768].rearrange("p (a b c) -> p a b c", a=2, b=3); _o += 768
        segi = cpack[:, _o:_o + 8].rearrange("p (a b) -> p a b", a=2); _o += 8
        segm = cpack[:, _o:_o + 4]; _o += 4
        bgate = cpack[:, _o:_o + 1024]; _o += 1024
        wgu = sbt("wgu", [16, 1024], BF16); b_wgu = Buf("wgu", "wgu")
        gvec = cpack[:, _o:_o + 48].rearrange("p (a b) -> p a b", a=3); _o += 48
        sink = cpack[:, _o:_o + 32]; _o += 32
        pmask = cpack[:, _o:_o + 1]; _o += 1
        wglow_f = cpack[:, _o:_o + 256].rearrange("p (a b) -> p a b", a=16); _o += 256
        wglow = sbt("wglow", [128, 16, 16], BF16); b_wglow = Buf("wglow")
        rs0 = sbt("rs0", [128, 8], F32); rs1 = sbt("rs1", [128, 10], F32)
        rs2 = sbt("rs2", [128, 10], F32); rs3 = sbt("rs3", [128, 9], F32)
        ss0 = sbt("ss0", [128, 8], F32); ss1 = sbt("ss1", [128, 10], F32)
        ss2 = sbt("ss2", [128, 10, 4], F32); ss3 = sbt("ss3", [128, 9, 4], F32)
        b_rs0 = Buf("rs0"); b_rs1 = Buf("rs1"); b_rs2 = Buf("rs2"); b_rs3 = Buf("rs3")
        small = sbt("small", [128, 64], F32); b_small = Buf("small")
        hrs2 = sbt("hrs2", [128, 10], F32)
        ARENA = nc.sbuf_bytes_remaining - 1024
        ARENA = ARENA // 64 * 64
        A = Arena(nc, st, ARENA)
        psum = [st.enter_context(nc.psum_tensor(f"ps{i}", [128, 512], F32)) for i in range(8)]
        psb = [p[:].bitcast(BF16) for p in psum]
        bps = [Buf(f"ps{i}", excl=True) for i in range(8)]

        S.dma("act", cpack[:], cpack_d, writes=[b_const])
        S.dma("sp", cpack16[:], cpack16_d, writes=[b_const])
        S.op("dve", lambda e: e.tensor_copy(out=wglow[:], in_=wglow_f[:]), reads=[b_const], writes=[b_wglow])
        S.dma("pool", wgu[:], wgu_d, writes=[b_wgu])

        rr = {"n": 0}

        def MM(out, lhsT, rhs, start, stop, R, W, inc, tp=None):
            if tp is None:
                S.op("pe", lambda e: e.matmul(out, lhsT=lhsT, rhs=rhs, start=start, stop=stop), R, W, inc)
            else:
                S.op("pe", lambda e: e.matmul(out, lhsT=lhsT, rhs=rhs, start=start, stop=stop, tile_position=tp), R, W, inc)

        foldn = {"n": 0}

        def PROJ16(pb, T, ncols, lhs_of_k, rhs_of_k, R):
            if T == 128:
                for k in range(16):
                    MM(psum[pb][:T, 0:ncols], lhs_of_k(k), rhs_of_k(k), k == 0, k == 15, R, [bps[pb]], inc=(k == 15))
                return
            groups = [[0, 4, 8, 12], [1, 5, 9, 13], [2, 6, 10, 14], [3, 7, 11, 15]]
            order = []
            for step in range(4):
                for j in range(4):
                    if step < len(groups[j]):
                        order.append((j, step))
            for n_, (j, step) in enumerate(order):
                k = groups[j][step]
                MM(psum[pb][32 * j:32 * j + 32, 0:ncols], lhs_of_k(k), rhs_of_k(k), step == 0, step == len(groups[j]) - 1, R, [bps[pb]],
                   inc=(n_ == len(order) - 1), tp=(0, 32 * j))
            fi = foldn["n"] % 2
            foldn["n"] += 1
            CP(sfold[fi][:, 0:ncols], psum[pb][:, 0:ncols], [bps[pb]], [b_sfold[fi]])
            MM(psum[pb][0:32, 0:ncols], foldm[:, :], sfold[fi][:, 0:ncols], True, True, [b_sfold[fi], b_const], [bps[pb]], True)

        def TR(out, in_, R, W, inc, idn=None):
            idn_ap = ident[:in_.shape[0], :in_.shape[0]]
            S.op("pe", lambda e: e.transpose(out=out, in_=in_, identity=idn_ap), list(R) + [b_const], W, inc)

        def ACT(out, in_, func, R, W, **kw):
            S.op("act", lambda e: e.activation(out=out, in_=in_, func=func, **kw), R, W)

        def TT(out, in0, in1, op, R, W, eng="dve"):
            S.op(eng, lambda e: e.tensor_tensor(out=out, in0=in0, in1=in1, op=op), R, W)

        def TS(out, in0, s1, op0, R, W, s2=None, op1=None, eng="dve", accum=None):
            kw = {}
            if op1 is not None:
                kw["op1"] = op1
            if accum is not None:
                kw["accum_out"] = accum
            S.op(eng, lambda e: e.tensor_scalar(out=out, in0=in0, scalar1=s1, scalar2=s2, op0=op0, **kw), R, W)

        def STT(out, in0, scalar, in1, op0, op1, R, W, accum=None):
            kw = {}
            if accum is not None:
                kw["accum_out"] = accum
            S.op("dve", lambda e: e.scalar_tensor_tensor(out=out, in0=in0, scalar=scalar, in1=in1, op0=op0, op1=op1, **kw), R, W)

        def CP(out, in_, R, W, eng=None):
            if eng is None:
                rr["n"] += 1
                eng = "act" if rr["n"] % 2 else "dve"
            if eng == "act":
                ACT(out, in_, AF.Copy, R, W)
            else:
                S.op(eng, lambda e: e.tensor_copy(out=out, in_=in_), R, W)

        def SCALE(out, in_, sc_ap, R, W, mul=None, eng=None):
            if eng is None:
                rr["n"] += 1
                eng = "act" if rr["n"] % 2 else "dve"
            if eng == "act" and mul is None:
                ACT(out, in_, AF.Copy, R, W, scale=sc_ap)
            else:
                if mul is None:
                    TS(out, in_, sc_ap, ALU.mult, R, W)
                else:
                    TS(out, in_, sc_ap, ALU.mult, R, W, s2=mul, op1=ALU.mult)

        def MEMSET(ap, val, W, eng="dve"):
            S.op(eng, lambda e: e.memset(ap, val), (), W)

        def barrier():
            engs = ("pe", "act", "dve", "sp")
            for e in engs:
                waits = []
                for k, c in S.cnt.items():
                    if c > 0 and k != "e_" + e and k != "e_sp" and k != "e_pool" and not k.startswith("d_w"):
                        S._need(e, (k, c), waits)
                S.streams[e].append((waits, None, None))

        def RSTD(rs_ap, ss_ap, R, W):
            ACT(rs_ap, ss_ap, AF.Ln, R, W, scale=1.0 / D, bias=EPS)
            ACT(rs_ap, rs_ap, AF.Exp, W, W, scale=-0.5)

        wstate = {"n": 0}
        b_early = Buf("early")
        wlist = [(wbuf[i], bwbuf[i]) for i in range(NW)]

        def WLOAD(parts):
            i = wstate["n"] % len(wlist)
            first = wstate["n"] == 0
            wstate["n"] += 1
            wb_, bwb_ = wlist[i]
            for ap, off in parts:
                nco = ap.shape[1]
                src = ap.rearrange("(k p) n -> p k n", p=128)
                for kh in range(2):
                    S.dma("pool", wb_[:, kh * 8:(kh + 1) * 8, off:off + nco], src[:, kh * 8:(kh + 1) * 8, :], reads=[b_const, b_early] if first else [b_const], writes=[bwb_])
            return wb_, bwb_

        def slotT(i):
            return 32 if i == 9 else 128

        def xrows(i):
            return xs if i == 9 else xw[(7 + i) * 128:(8 + i) * 128, :]

        pstate = {"n": 0}

        def proj_bank():
            pstate["n"] += 1
            return pstate["n"] % len(P_PROJ)

        R1 = 0
        R2 = 37888
        R3 = 75776
        xT = A.at(R1, [128, 16, 1184], BF16); b_xT = [Buf(f"xT{i}") for i in range(10)]
        ogT = A.at(R2, [128, 16, 1184], BF16); b_ogT = [Buf(f"ogT{i}") for i in range(10)]
        A.setrange(R2, R3)
        xf = [A.alloc([128, D], F32) for _ in range(2)]; b_xf = [Buf(f"xf{i}", f"xf{i}") for i in range(2)]
        xb = A.alloc([128, D], BF16); b_xb = Buf("xb")
        junk = A.alloc([128, D], BF16); b_junk = Buf("junk")
        A.setrange(R3, ARENA)
        qkd = A.alloc([128, 10, 3, 256], BF16); b_qkd = [Buf(f"qkd{i}") for i in range(10)]
        qdT = A.alloc([128, 10, 2, 128], BF16); b_qdT = [Buf(f"qdT{i}") for i in range(10)]
        scm = A.alloc([128, 10, 128], BF16); b_scm = [Buf(f"scm{i}") for i in range(10)]
        v_s = A.alloc([128, 10, 512], BF16); b_v = [Buf(f"v{i}") for i in range(10)]
        glowT = A.alloc([128, 1184], BF16); b_glow = [Buf(f"glow{i}") for i in range(10)]
        Sf = A.alloc([128, 4, 2, 512], F32); b_Sf = [Buf(f"Sf{h}", "sfin") for h in range(4)]
        Sb = A.alloc([128, 2, 512], BF16); b_Sb = [Buf("Sb0"), Buf("Sb1")]
        sSf = [A.alloc([128, 2, 512], F32) for _ in range(2)]; b_sSf = [Buf(f"sSf{i}", f"sS{i}") for i in range(2)]
        sSb = [A.alloc([128, 2, 512], BF16)] * 2; b_sSb = [[Buf("sSb")] * 2] * 2
        gon = A.alloc([128, 512], F32); b_gon = Buf("gon", "gon")
        NG = 3
        nl = [A.alloc([128, 256], F32) for _ in range(NG)]; b_nl = [Buf(f"nl{i}") for i in range(NG)]
        eb = [A.alloc([128, 3, 256], F32) for _ in range(2)]; b_eb = [Buf(f"eb{i}") for i in range(2)]
        ebl = A.alloc([128, 10, 8], F32); b_ebl = [Buf(f"ebl{i}") for i in range(10)]
        kdT = [A.alloc([128, 2, 128], BF16) for _ in range(2)]; b_kdT = [Buf(f"kdT{i}") for i in range(2)]
        qTm = A.alloc([128, 4, 2, 32], BF16); b_qTm = Buf("qTm")
        klm = A.alloc([128, 4, 256], BF16); b_klm = Buf("klm")
        sso = [A.alloc([128, 2], F32) for _ in range(2)]; b_sso = [Buf(f"sso{i}") for i in range(2)]
        srt = [A.alloc([128, 512], BF16) for _ in range(2)]; b_srt = [Buf(f"srt{i}") for i in range(2)]
        ogt = [A.alloc([128, 512], BF16) for _ in range(2)]; b_ogt = [Buf(f"ogt{i}") for i in range(2)]
        sqj = A.alloc([128, 512], BF16); b_sqj = Buf("sqj")
        print("phase01 arena used", A.lo, "of", ARENA)

        P_PROJ = [0, 1]; P_TR = (2, 3); P_G = 4; P_BC = 5; P_O = 6; P_SU = 7
        b_p4a = b_p4b = b_p4c = bps[4]
        trstate = {"n": 0}

        def tr_bank():
            trstate["n"] += 1
            return P_TR[trstate["n"] % 2]

        def load_transpose(slots, rs, ss, b_rs, idx0=0):
            n = len(slots)
            for idx, (slot, rows, T) in enumerate(slots):
                f = (idx0 + idx) % 2
                S.dma("sp", xf[f][:T, :], rows, reads=[b_const], writes=[b_xf[f]])
                if idx == 1 and b_early.w is None:
                    b_early.w = b_xf[f].w
                ACT(junk[:T, :], xf[f][:T, :], AF.Square, [b_xf[f]], [b_junk, b_rs], accum_out=ss[:T, slot:slot + 1])
                ACT(xb[:T, :], xf[f][:T, :], AF.Copy, [b_xf[f]], [b_xb])
                c0 = slot * 128
                for half in range(2):
                    pb = tr_bank()
                    pv = psb[pb][:, 0:8 * T].rearrange("p (a b) -> p a b", b=T)
                    for kk in range(8):
                        k = half * 8 + kk
                        TR(pv[:, kk, :], xb[:T, k * 128:(k + 1) * 128], [b_xb], [bps[pb]], inc=(kk == 7))
                    gb = gvec[:, 0, half * 8:(half + 1) * 8].unsqueeze(2).broadcast_to([128, 8, T])
                    TT(xT[:, half * 8:(half + 1) * 8, c0:c0 + T], pv, gb, ALU.mult, [bps[pb], b_const], [b_xT[slot]])
            RSTD(rs[:, 0:10 if rs is rs1 else 8], ss[:, 0:10 if ss is ss1 else 8], [b_rs], [b_rs])

        def glow_for(slot, T):
            c0 = slot * 128
            pb = P_G
            for k in range(16):
                MM(psum[pb][0:16, 0:T], wglow[:, k, :], xT[:, k, c0:c0 + T], k == 0, k == 15, [b_wglow, b_xT[slot]], [b_p4a], inc=(k == 15))
            CP(glowT[0:16, c0:c0 + T], psum[pb][0:16, 0:T], [b_p4a], [b_glow[slot]])

        def proj(wt, bw, wcols, ncols, slots, xTt, b_x, evac):
            for slot, T in slots:
                c0 = slot * 128
                pb = P_PROJ[proj_bank()]
                PROJ16(pb, T, ncols, lambda k: xTt[:, k, c0:c0 + T], lambda k: wt[:, k, wcols:wcols + ncols], [b_x[slot], bw])
                evac(slot, T, psum[pb][:T, 0:ncols], bps[pb])

        LN16 = -2.772588722239781
        P_SUB = (7, 5)

        def head_passes(h, slots, rs, b_rs, full, wq_parts, wv_part, wr_part, post_pv=None):
            n = len(slots)

            def G1(i):
                slot, T, sample = slots[i]
                c0 = slot * 128
                gi = i % NG
                pb = 4 + (i % 2)
                MM(psum[pb][:T, 0:256], glowT[0:16, c0:c0 + T], wgu[:, h * 256:(h + 1) * 256], True, True, [b_glow[slot], b_wgu], [bps[pb]], True)
                STT(nl[gi][:T, :], psum[pb][:T, 0:256], rs[:T, slot:slot + 1], bgate[:T, h * 256:(h + 1) * 256], ALU.mult, ALU.add,
                    [bps[pb], b_rs, b_const], [b_nl[gi]])
                ACT(nl[gi][:T, :], nl[gi][:T, :], AF.Exp, [b_nl[gi]], [b_nl[gi]], scale=-1.0)
                ACT(nl[gi][:T, :], nl[gi][:T, :], AF.Ln, [b_nl[gi]], [b_nl[gi]], bias=1.0)

            def G2(i):
                slot, T, sample = slots[i]
                gi = i % NG
                ci = 1 if sample else 0
                nseg = 4 if sample else 1
                pb = 6 + (i % 2)
                pg = 4 + (i % 2)
                if full:
                    MM(psum[pb][:T, 0:256], glac[:T, ci, 0, :T], nl[gi][:T, :], True, True, [b_nl[gi], b_const], [bps[pb]], False)
                MM(psum[pb][:T, 256:512], glac[:T, ci, 1, :T], nl[gi][:T, :], True, True, [b_nl[gi], b_const], [bps[pb]], True)
                for dc in range(2):
                    MM(psum[pg][:, 256 + dc * nseg:256 + (dc + 1) * nseg], nl[gi][:T, dc * 128:(dc + 1) * 128], segi[:T, ci, 0:nseg], True, True,
                       [b_nl[gi], b_const], [bps[pg]], dc == 1)
                if full:
                    ACT(eb[i % 2][:T, 0, :], psum[pb][:T, 0:256], AF.Exp, [bps[pb]], [b_eb[i % 2]], bias=LN16)
                    ACT(eb[i % 2][:T, 1, :], psum[pb][:T, 0:256], AF.Exp, [bps[pb]], [b_eb[i % 2]], scale=-1.0)
                ACT(eb[i % 2][:T, 2, :], psum[pb][:T, 256:512], AF.Exp, [bps[pb]], [b_eb[i % 2]])
                ACT(ebl[:, slot, 0:2 * nseg], psum[pg][:, 256:256 + 2 * nseg], AF.Exp, [bps[pg]], [b_ebl[slot]])

            p1bank = {}

            def P1_mm(i):
                slot, T, sample = slots[i]
                c0 = slot * 128
                pb = P_PROJ[proj_bank()]
                p1bank[i] = pb
                wc, nco = (0, 512) if full else (256, 256)
                PROJ16(pb, T, nco, lambda k: xT[:, k, c0:c0 + T], lambda k: wt_qk[:, k, wc:wc + nco], [b_xT[slot], bw_qk])

            def P1_ev(i):
                slot, T, sample = slots[i]
                pb = p1bank[i]
                rsc = rs[:T, slot:slot + 1]
                if full:
                    STT(qkd[:T, slot, 0, :], psum[pb][:T, 0:256], rsc, eb[i % 2][:T, 0, :], ALU.mult, ALU.mult, [bps[pb], b_rs, b_eb[i % 2]], [b_qkd[slot]])
                    STT(qkd[:T, slot, 1, :], psum[pb][:T, 256:512], rsc, eb[i % 2][:T, 1, :], ALU.mult, ALU.mult, [bps[pb], b_rs, b_eb[i % 2]], [b_qkd[slot]])
                    STT(qkd[:T, slot, 2, :], psum[pb][:T, 256:512], rsc, eb[i % 2][:T, 2, :], ALU.mult, ALU.mult, [bps[pb], b_rs, b_eb[i % 2]], [b_qkd[slot]])
                else:
                    STT(qkd[:T, slot, 2, :], psum[pb][:T, 0:256], rsc, eb[i % 2][:T, 2, :], ALU.mult, ALU.mult, [bps[pb], b_rs, b_eb[i % 2]], [b_qkd[slot]])

            wt_qk, bw_qk = WLOAD(wq_parts)
            G1(0)
            if n > 1:
                G1(1)
            P1_mm(0)
            G2(0)
            for i in range(n):
                if i + 2 < n:
                    G1(i + 2)
                if i + 1 < n:
                    G2(i + 1)
                P1_ev(i)
                if i + 1 < n:
                    P1_mm(i + 1)

            wt_v, bw_v = WLOAD([wv_part])

            def TRQ(i):
                slot, T, sample = slots[i]
                ki = i % 2
                pb = tr_bank()
                pv = psb[pb][:, 0:4 * T].rearrange("p (a b) -> p a b", b=T)
                for a in range(2):
                    for dc in range(2):
                        TR(pv[:, a * 2 + dc, :], qkd[:T, slot, a, dc * 128:(dc + 1) * 128], [b_qkd[slot]], [bps[pb]], inc=(a == 1 and dc == 1))
                CP(qdT[:, slot, :, :T], pv[:, 0:2, :], [bps[pb]], [b_qdT[slot]], eng="act")
                CP(kdT[ki][:, :, :T], pv[:, 2:4, :], [bps[pb]], [b_kdT[ki]], eng="dve")

            def SC(i):
                slot, T, sample = slots[i]
                ki = i % 2
                ci = 1 if sample else 0
                pb = 4 + (i % 2)
                for dc in range(2):
                    MM(psum[pb][:T, 0:T], kdT[ki][:, dc, :T], qdT[:, slot, dc, :T], dc == 0, dc == 1, [b_kdT[ki], b_qdT[slot]], [bps[pb]], dc == 1)
                TT(scm[:T, slot, :T], psum[pb][:T, 0:T], glac[:T, ci, 2, :T], ALU.mult, [bps[pb], b_const], [b_scm[slot]])

            def PV(i):
                slot, T, sample = slots[i]
                c0 = slot * 128
                pb = P_PROJ[proj_bank()]
                PROJ16(pb, T, 512, lambda k: xT[:, k, c0:c0 + T], lambda k: wt_v[:, k, :], [b_xT[slot], bw_v])
                SCALE(v_s[:T, slot, :], psum[pb][:T, :], rs[:T, slot:slot + 1], [bps[pb], b_rs], [b_v[slot]])

            if full:
                TRQ(0)
            for i in range(n):
                if full and i + 1 < n:
                    TRQ(i + 1)
                PV(i)
                if full:
                    SC(i)
                if post_pv is not None:
                    post_pv(i)

            if full:
                wt_r, bw_r = WLOAD([wr_part])

            def RP_mm(i):
                slot, T, sample = slots[i]
                c0 = slot * 128
                ti = i % 2
                pb = P_PROJ[proj_bank()]
                PROJ16(pb, T, 512, lambda k: xT[:, k, c0:c0 + T], lambda k: wt_r[:, k, :], [b_xT[slot], bw_r])
                ACT(srt[ti][:T, :], psum[pb][:T, :], AF.Silu, [bps[pb], b_rs], [b_srt[ti]], scale=rs[:T, slot:slot + 1])
                TT(ogt[ti][:T, :], v_s[:T, slot, :], srt[ti][:T, :], ALU.mult, [b_v[slot], b_srt[ti]], [b_ogt[ti]])

            def RP_tr(i):
                slot, T, sample = slots[i]
                c0 = slot * 128
                ti = i % 2
                pbt = tr_bank()
                pv = psb[pbt][:, 0:4 * T].rearrange("p (a b) -> p a b", b=T)
                for jj in range(4):
                    TR(pv[:, jj, :], ogt[ti][:T, jj * 128:(jj + 1) * 128], [b_ogt[ti]], [bps[pbt]], inc=(jj == 3))
                CP(ogT[:, 4 * h:4 * h + 4, c0:c0 + T], pv, [bps[pbt]], [b_ogT[slot]])

            def ST(i):
                slot, T, sample = slots[i]
                nseg = 4 if sample else 1
                si = i % 2
                if sample:
                    if full:
                        MEMSET(qTm[:], 0.0, [b_qTm])
                        for j in range(4):
                            S.op("dve", lambda e, j=j, slot=slot: e.tensor_copy(out=qTm[:, j, :, 8 * j:8 * j + 8], in_=qdT[:, slot, :, 8 * j:8 * j + 8]), [b_qdT[slot]], [b_qTm])
                    for j in range(4):
                        TS(klm[:T, j, :], qkd[:T, slot, 2, :], segm[:T, j:j + 1], ALU.mult, [b_qkd[slot], b_const], [b_klm])
                for j in range(nseg):
                    if sample:
                        sj = j % 2
                        CP(sSb[sj][:], sSf[sj][:], [b_sSf[sj]], [b_sSb[sj][0]], eng="act")
                        Sfj, b_Sfj, Sbj, b_Sbj = sSf[sj], b_sSf[sj], sSb[sj], b_sSb[sj]
                        kl_ap = klm[:T, j, :]
                        b_kl = b_klm
                    else:
                        Sfj, b_Sfj, Sbj, b_Sbj = Sf[:, h], b_Sf[h], Sb, b_Sb
                        kl_ap = qkd[:T, slot, 2, :]
                        b_kl = b_qkd[slot]
                    if full:
                        for dc in range(2):
                            lh = qTm[:, j, dc, :T] if sample else qdT[:, slot, dc, :T]
                            MM(psum[6][:T, :], lh, Sbj[:, dc, :], (j == 0 and dc == 0), False, [b_qTm if sample else b_qdT[slot], b_Sbj[dc]], [bps[6]], False)
                        if j == nseg - 1:
                            MM(psum[6][:T, :], scm[:T, slot, :T], v_s[:T, slot, :], False, True, [b_scm[slot], b_v[slot]], [bps[6]], True)
                    for dc in range(2):
                        pbs = P_SUB[dc]
                        MM(psum[pbs][:, :], kl_ap[:, dc * 128:(dc + 1) * 128], v_s[:T, slot, :], True, True, [b_kl, b_v[slot]], [bps[pbs]], True)
                        STT(Sfj[:, dc, :], Sfj[:, dc, :], ebl[:, slot, dc * nseg + j:dc * nseg + j + 1], psum[pbs][:, :], ALU.mult, ALU.add,
                            [bps[pbs], b_ebl[slot], b_Sfj], [b_Sfj])
                        if full and not sample:
                            S.op("dve", lambda e, dc=dc: e.tensor_copy(out=Sb[:, dc, :], in_=Sf[:, h, dc, :]), [b_Sf[h]], [b_Sb[dc]])
                    if sample:
                        S.dma("sp", ssamp_o[j, h].rearrange("(c p) e -> p c e", p=128), Sfj[:], reads=[b_Sfj])
                        if j + 2 < 4:
                            S.dma("sp", sSf[sj][:], st_in[j + 2, h].rearrange("(c p) e -> p c e", p=128), writes=[b_sSf[sj]])
                if full:
                    ACT(sqj[:T, :], psum[6][:T, :], AF.Square, [bps[6]], [b_sqj, b_sso[si]], accum_out=sso[si][:T, 0:1])
                    ACT(sso[si][:T, 1:2], sso[si][:T, 0:1], AF.Ln, [b_sso[si]], [b_sso[si]], scale=1.0 / 512, bias=EPS)
                    ACT(sso[si][:T, 1:2], sso[si][:T, 1:2], AF.Exp, [b_sso[si]], [b_sso[si]], scale=-0.5)
                    STT(v_s[:T, slot, :], psum[6][:T, :], sso[si][:T, 1:2], gon[:T, :], ALU.mult, ALU.mult, [bps[6], b_sso[si], b_gon], [b_v[slot]])

            if any(sm for _, _, sm in slots):
                for j in range(2):
                    S.dma("sp", sSf[j][:], st_in[j, h].rearrange("(c p) e -> p c e", p=128), writes=[b_sSf[j]])
            for i in range(n):
                ST(i)
                if full and i >= 1:
                    RP_mm(i - 1)
                if full and i >= 2:
                    RP_tr(i - 2)
            if full:
                RP_mm(n - 1)
                RP_tr(n - 2)
                RP_tr(n - 1)

        try:
            MEMSET(ss0[:], 0.0, [b_rs0]); MEMSET(ss1[:], 0.0, [b_rs1]); MEMSET(ss2[:], 0.0, [b_rs2]); MEMSET(ss3[:], 0.0, [b_rs3])
            MEMSET(Sf[:], 0.0, b_Sf)
            for i_ in range(2):
                MEMSET(sso[i_][:], 0.0, [b_sso[i_]])
            pre = [(p, xw[p * 128:(p + 1) * 128, :], 128) for p in range(7)]
            load_transpose(pre, rs0, ss0, b_rs0)
            for p in range(7):
                glow_for(p, 128)
            pslots = [(p, 128, False) for p in range(7)]
            for h in range(4):
                if h == 3:
                    early = [(i, xrows(i), slotT(i)) for i in (7, 8, 9)]
                    load_transpose(early, rs1, ss1, b_rs1)
                    for i, _, T_ in early:
                        glow_for(i, T_)
                def main_tile_early(i):
                    load_transpose([(i, xrows(i), 128)], rs1, ss1, b_rs1, idx0=i + 1)
                    glow_for(i, 128)
                head_passes(h, pslots, rs0, b_rs0, False,
                            [(w_in_a[:, 1024 + 256 * h:1024 + 256 * (h + 1)], 256)],
                            (w_in_a[:, 2048 + 512 * h:2048 + 512 * (h + 1)], 0), None,
                            post_pv=main_tile_early if h == 3 else None)

            stop_check('p0')
            barrier()
            mslots = [(i, slotT(i)) for i in range(10)]
            mslots3 = [(i, slotT(i), i == 9) for i in range(10)]
            for h in range(4):
                S.dma("sp", gon[:], gon_d[:, h * 512:(h + 1) * 512], writes=[b_gon])
                for dc_ in range(2):
                    S.op("dve", lambda e, dc_=dc_, h=h: e.tensor_copy(out=Sb[:, dc_, :], in_=Sf[:, h, dc_, :]), [b_Sf[h]], [b_Sb[dc_]])
                head_passes(h, mslots3, rs1, b_rs1, True,
                            [(w_in_a[:, 256 * h:256 * (h + 1)], 0), (w_in_a[:, 1024 + 256 * h:1024 + 256 * (h + 1)], 256)],
                            (w_in_a[:, 2048 + 512 * h:2048 + 512 * (h + 1)], 0),
                            (w_in_a[:, 4096 + 512 * h:4096 + 512 * (h + 1)], 0))
                S.dma("sp", sfin_o[h].rearrange("(c p) e -> p c e", p=128), Sf[:, h], reads=[b_Sf[h]])
            barrier()

            stop_check('p1')
            P_PROJ[:] = [0, 1, 4, 5, 6, 7]
            HB0 = ARENA - 33792
            KV0 = HB0 - 21 * KB
            hbT = A.at(HB0, [128, 16, 1056], BF16); b_hbT = [Buf(f"hbT{i}") for i in range(9)]
            W3 = KV0
            hkvT = A.at(R1, [128, 16, 1184], BF16); b_hkvT = [Buf(f"hkvT{i}") for i in range(10)]
            A.setrange(R3, W3)
            NP = 6
            xpc = [A.alloc([128, 512], F32) for _ in range(NP)]; b_xpc = [Buf(f"xpc{i}", f"xpc{i}") for i in range(NP)]
            h1p = [A.alloc([128, 512], F32) for _ in range(NP)]; b_h1p = [Buf(f"h1p{i}", f"h1p{i}") for i in range(NP)]
            h1b = [A.alloc([128, 512], BF16) for _ in range(2)]; b_h1b = [Buf(f"h1b{i}") for i in range(2)]
            junk2 = A.alloc([128, 512], BF16); b_junk2 = Buf("junk2")
            b_h1s = [Buf(f"h1s{i}") for i in range(9)]
            cnt = 0
            wnext = WLOAD([(w_out_a[:, 0:512], 0)])
            for blk in range(4):
                wt, bw = wnext
                wnx = {}

                def o_evac2(slot, T, bi, blk=blk):
                    pb = tr_bank()
                    pv = psb[pb][:, 0:4 * T].rearrange("p (a b) -> p a b", b=T)
                    for jj in range(4):
                        TR(pv[:, jj, :], h1b[bi][:T, jj * 128:(jj + 1) * 128], [b_h1b[bi]], [bps[pb]], inc=(jj == 3))
                    c0 = slot * 128
                    gk = gvec[:, 1, 4 * blk:4 * blk + 4].unsqueeze(2).broadcast_to([128, 4, T])
                    TT(hkvT[:, 4 * blk:4 * blk + 4, c0:c0 + T], pv, gk, ALU.mult, [bps[pb], b_const], [b_hkvT[slot]])
                    if slot >= 1:
                        c1 = (slot - 1) * 128
                        gb2 = gvec[:, 2, 4 * blk:4 * blk + 4].unsqueeze(2).broadcast_to([128, 4, T])
                        TT(hbT[:, 4 * blk:4 * blk + 4, c1:c1 + T], pv, gb2, ALU.mult, [bps[pb], b_const], [b_hbT[slot - 1]])

                def o_evac(slot, T, ps_ap, bp, blk=blk):
                    nonlocal cnt
                    if slot == 2:
                        wnx["w"] = WLOAD([(w_out_a[:, 512 * (blk + 1):512 * (blk + 2)], 0)]) if blk < 3 else WLOAD([(w_kv, 0)])
                    pi = cnt % NP
                    bi = cnt % 2
                    cnt += 1
                    cs = slice(512 * blk, 512 * (blk + 1))
                    S.dma("sp", xpc[pi][:T, :], xrows(slot)[:, cs], writes=[b_xpc[pi]])
                    TT(h1p[pi][:T, :], ps_ap, xpc[pi][:T, :], ALU.add, [bp, b_xpc[pi]], [b_h1p[pi]])
                    if slot >= 1:
                        r0 = (slot - 1) * 128
                        S.dma("pool", h1s[r0:r0 + T, cs], h1p[pi][:T, :], reads=[b_h1p[pi]], writes=[b_h1s[slot - 1]])
                    ACT(junk2[:T, :], h1p[pi][:T, :], AF.Square, [b_h1p[pi]], [b_junk2, b_rs2], accum_out=ss2[:T, slot, blk:blk + 1])
                    CP(h1b[bi][:T, :], h1p[pi][:T, :], [b_h1p[pi]], [b_h1b[bi]], eng="act")
                    if pend2:
                        o_evac2(*pend2.pop())
                    pend2.append((slot, T, bi))
                pend2 = []
                proj(wt, bw, 0, 512, mslots, ogT, b_ogT, o_evac)
                o_evac2(*pend2.pop())
                wnext = wnx["w"]
            S.op("dve", lambda e: e.reduce_sum(out=ss1[:, 0:10], in_=ss2[:, :, :], axis=AX.X), [b_rs2], [b_rs1])
            RSTD(rs2[:, 0:10], ss1[:, 0:10], [b_rs1], [b_rs2])
            TS(hrs2[:, 0:10], rs2[:, 0:10], 0.5, ALU.mult, [b_rs2], [b_rs2])
            barrier()

            stop_check('p2')
            kT = A.at(KV0, [128, 4, 1184], BF16); b_kT = [Buf(f"kT{i}") for i in range(10)]
            o1 = KV0 + 4 * 1184 * 2
            vaug = A.at(o1, [128, 10, 4, 65], BF16); b_va = [Buf(f"va{i}") for i in range(10)]
            o2 = (o1 + 10 * 4 * 65 * 2 + 63) // 64 * 64
            kTc = A.at(o2, [128, 4, 512], BF16); b_kTc = Buf("kTc")
            o3 = o2 + 4 * 512 * 2
            vcaug = A.at(o3, [128, 4, 4, 65], BF16); b_vca = Buf("vca")
            assert o3 + 4 * 4 * 65 * 2 <= HB0
            A.setrange(R2, W3)
            kvf = [A.alloc([128, 512], F32) for _ in range(2)]; b_kvf = [Buf(f"kvf{i}", f"kvf{i}") for i in range(2)]
            kdup = [A.alloc([128, 4, 2, 64], BF16) for _ in range(2)]; b_kdup = [Buf(f"kdup{i}") for i in range(2)]
            ckf = [A.alloc([128, 256], F32) for _ in range(2)]; b_ckf = [Buf(f"ckf{i}", f"ckf{i}") for i in range(2)]
            cvf = [A.alloc([128, 256], F32) for _ in range(2)]; b_cvf = [Buf(f"cvf{i}", f"cvf{i}") for i in range(2)]
            MEMSET(vaug[:], 1.0, b_va); MEMSET(vcaug[:], 1.0, [b_vca])
            assert A.lo <= R2 + 18432, A.lo
            bp_t = A.at(R2 + 18432, [128, 2, 1024], F32); bs_t = A.at(R2 + 18432 + 8192, [128, 5, 256], F32)
            b_bias = Buf("bias", "bias")
            S.dma("sp", bp_t[:], biasp_d[0], writes=[b_bias])
            S.dma("sp", bs_t[:], biass_d[0], writes=[b_bias])

            pendk = []

            def kforms2(T, ki, kT_out, b_kT_out):
                pb = tr_bank()
                pv = psb[pb][:, 0:4 * T].rearrange("p (a b) -> p a b", b=T)
                for g in range(4):
                    TR(pv[:, g, :], kdup[ki][:T, g].rearrange("p a b -> p (a b)"), [b_kdup[ki]], [bps[pb]], inc=(g == 3))
                CP(kT_out, pv, [bps[pb]], [b_kT_out])

            def kforms(kf_ap, vf_ap, T, R, ki, kT_out, b_kT_out, va_out, b_va_out):
                kv4 = kf_ap.rearrange("p (g d) -> p g d", d=64)
                for r in range(2):
                    S.op("dve", lambda e, r=r: e.tensor_copy(out=kdup[ki][:T, :, r, :], in_=kv4), R, [b_kdup[ki]])
                CP(va_out[:, :, 0:64], vf_ap.rearrange("p (g d) -> p g d", d=64), R, [b_va_out], eng="act")
                if pendk:
                    kforms2(*pendk.pop())
                pendk.append((T, ki, kT_out, b_kT_out))

            wt, bw = wnext

            def kv_evac(slot, T, ps_ap, bp):
                ki = slot % 2
                SCALE(kvf[ki][:T, :], ps_ap, rs2[:T, slot:slot + 1], [bp, b_rs2], [b_kvf[ki]])
                c0 = slot * 128
                kforms(kvf[ki][:T, 0:256], kvf[ki][:T, 256:512], T, [b_kvf[ki]], ki, kT[:, :, c0:c0 + T], b_kT[slot], vaug[:T, slot], b_va[slot])
                if slot == 8:
                    S.dma("sp", kwin_o, kvf[ki][:, 0:256], reads=[b_kvf[ki]])
                    S.dma("sp", vwin_o, kvf[ki][:, 256:512], reads=[b_kvf[ki]])
                if slot == 9:
                    for j in range(4):
                        S.dma("sp", kwins_o[j, 120:128, :], kvf[ki][8 * j:8 * j + 8, 0:256], reads=[b_kvf[ki]])
                        S.dma("sp", vwins_o[j, 120:128, :], kvf[ki][8 * j:8 * j + 8, 256:512], reads=[b_kvf[ki]])
            proj(wt, bw, 0, 512, mslots, hkvT, b_hkvT, kv_evac)
            b_dd = Buf("dd", "dd")
            S.dma("sp", kwins_o[:, 0:120, :], ck[:, 8:128, :], writes=[b_dd])
            S.dma("sp", vwins_o[:, 0:120, :], cv[:, 8:128, :], writes=[b_dd])
            for j in range(4):
                ci_ = j % 2
                S.dma("sp", ckf[ci_][:], ck[j], writes=[b_ckf[ci_]])
                S.dma("sp", cvf[ci_][:], cv[j], writes=[b_cvf[ci_]])
                kforms(ckf[ci_][:], cvf[ci_][:], 128, [b_ckf[ci_], b_cvf[ci_]], ci_, kTc[:, :, j * 128:(j + 1) * 128], b_kTc, vcaug[:, j], b_vca)
            kforms2(*pendk.pop())
            barrier()

            stop_check('p25')
            P_PROJ[:] = [0, 1]
            ozT = A.at(R1, [128, 16, 1056], BF16); b_ozT = [Buf(f"ozT{i}") for i in range(9)]
            A.setrange(R2, W3)
            q_s = A.alloc([128, 9, 512], BF16); b_q = [Buf(f"q{i}") for i in range(9)]
            oat = A.alloc([128, 9, 512], BF16); b_oat = [Buf(f"oat{i}") for i in range(9)]
            assert A.lo == R2 + 18432, A.lo
            _bp = A.alloc([128, 2, 1024], F32); _bs = A.alloc([128, 5, 256], F32)
            qT = [A.alloc([128, 2, 4, 128], BF16) for _ in range(2)]; b_qT = [Buf(f"qT{i}") for i in range(2)]
            scs = [A.alloc([128, 512], BF16) for _ in range(2)]; b_scs = [Buf(f"scs{i}") for i in range(2)]
            Ep = A.alloc([128, 2, 1024], BF16); Ep0 = A.alloc([128, 1024], BF16); Es = A.alloc([128, 5, 256], BF16); b_E = Buf("Etab")
            pT = [A.alloc([128, 4, 512], BF16) for _ in range(2)]; b_pT = [Buf(f"pT{i}") for i in range(2)]
            pTn = A.alloc([128, 2, 128], BF16); b_pTn = Buf("pTn")
            szt = [A.alloc([128, 512], F32) for _ in range(2)]; b_szt = [Buf(f"szt{i}") for i in range(2)]
            ozt = [A.alloc([128, 512], BF16) for _ in range(2)]; b_ozt = [Buf(f"ozt{i}") for i in range(2)]
            rden = [A.alloc([128, 8], F32) for _ in range(2)]; b_rden = [Buf(f"rden{i}") for i in range(2)]
            print("phase3 arena used", A.lo, "of", W3)
            P_SC = (4, 5); P_PV = (6, 7)
            for i_ in range(2):
                MEMSET(qT[i_][:], 0.0, [b_qT[i_]])
            hslots = [(i, 32 if i == 8 else 128) for i in range(9)]
            scn = {"n": 0}
            for g in range(4):
                if g > 0:
                    S.dma("sp", bp_t[:], biasp_d[g], writes=[b_bias])
                    S.dma("sp", bs_t[:], biass_d[g], writes=[b_bias])
                for half in range(2):
                    for j in range(4):
                        si = 8 * g + 2 * j + half
                        c = half * 512 + j * 128
                        TS(bp_t[:, :, c:c + 128], bp_t[:, :, c:c + 128], sink[:, si:si + 1], ALU.subtract, [b_bias, b_const], [b_bias])
                        c2 = half * 128 + j * 32
                        TS(bs_t[:, :, c2:c2 + 32], bs_t[:, :, c2:c2 + 32], sink[:, si:si + 1], ALU.subtract, [b_bias, b_const], [b_bias])
                ACT(Ep[:], bp_t[:], AF.Exp, [b_bias], [b_E])
                ACT(Ep0[:], bp_t[:, 0, :], AF.Exp, [b_bias, b_const], [b_E], bias=pmask[:, 0:1])
                ACT(Es[:], bs_t[:], AF.Exp, [b_bias], [b_E])
                wt, bw = WLOAD([(w_in_b[:, 512 * g:512 * (g + 1)], 0)])
                proj(wt, bw, 0, 512, hslots, hbT, b_hbT,
                     lambda slot, T, ps_ap, bp: SCALE(q_s[:T, slot, :], ps_ap, rs2[:T, slot + 1:slot + 2], [bp, b_rs2], [b_q[slot]], mul=0.125, eng="dve"))
                def S1(i):
                    slot, T = hslots[i]
                    qi = slot % 2
                    pb = tr_bank()
                    pv = psb[pb][:, 0:4 * T].rearrange("p (a b) -> p a b", b=T)
                    for jj in range(4):
                        TR(pv[:, jj, :], q_s[:T, slot, jj * 128:(jj + 1) * 128], [b_q[slot]], [bps[pb]], inc=(jj == 3))
                    CP(qT[qi][0:64, 0, :, :T], pv[0:64], [bps[pb]], [b_qT[qi]], eng="act")
                    CP(qT[qi][64:128, 1, :, :T], pv[64:128], [bps[pb]], [b_qT[qi]], eng="dve")

                def kvblocks(slot):
                    kvs = slot + 1
                    return [(kT[:, g, (kvs - 1) * 128:kvs * 128], b_kT[kvs - 1], vaug[:, kvs - 1, g, :], b_va[kvs - 1], (Ep0[:, :] if slot == 0 else Ep[:, 0, :])),
                            (kT[:, g, kvs * 128:(kvs + 1) * 128], b_kT[kvs], vaug[:, kvs, g, :], b_va[kvs], Ep[:, 1, :])]

                def S2(i, part=None):
                    slot, T = hslots[i]
                    qi = slot % 2
                    pti = slot % 2
                    if slot >= 8 and part == 1:
                        return
                    if slot < 8:
                        for bi_, (kt_ap, bkt, va_ap, bva, bias_ap) in enumerate(kvblocks(slot)):
                            if part is not None and bi_ != part:
                                continue
                            for half in range(2):
                                pbk = P_SC[half]
                                MM(psum[pbk][:, :], kt_ap[:, :], qT[qi][:, half, :, :].rearrange("p a b -> p (a b)"),
                                   True, True, [bkt, b_qT[qi]], [bps[pbk]], True)
                                si_ = scn["n"] % 2
                                scn["n"] += 1
                                ACT(scs[si_][:, :], psum[pbk][:, :], AF.Exp, [bps[pbk]], [b_scs[si_]])
                                TT(pT[pti][:, bi_ * 2 + half, :], scs[si_][:, :], bias_ap[:, half * 512:(half + 1) * 512], ALU.mult, [b_scs[si_], b_E], [b_pT[pti]])
                    else:
                        for half in range(2):
                            pbk = P_SC[half]
                            rhs_q = qT[qi][half * 64:(half + 1) * 64, half, :, :T]
                            for jb in range(4):
                                MM(psum[pbk][:, jb * 128:(jb + 1) * 128].rearrange("p (a b) -> p a b", b=T), kTc[half * 64:(half + 1) * 64, g, jb * 128:(jb + 1) * 128], rhs_q,
                                   True, True, [b_kTc, b_qT[qi]], [bps[pbk]], jb == 3)
                            si_ = scn["n"] % 2
                            scn["n"] += 1
                            bias_c = Es[:, 0:4, half * 128:(half + 1) * 128]
                            ACT(scs[si_][:, :], psum[pbk][:, :], AF.Exp, [bps[pbk]], [b_scs[si_]])
                            TT(pT[pti][:, half, :].rearrange("p (a b) -> p a b", b=128), scs[si_][:, :].rearrange("p (a b) -> p a b", b=128), bias_c, ALU.mult,
                               [b_scs[si_], b_E], [b_pT[pti]])
                        c9 = 9 * 128
                        for half in range(2):
                            pbn = P_TR[half]
                            MM(psum[pbn][:T, 0:128].rearrange("p (a b) -> p a b", b=T), kT[half * 64:(half + 1) * 64, g, c9:c9 + T],
                               qT[qi][half * 64:(half + 1) * 64, half, :, :T], True, True, [b_kT[9], b_qT[qi]], [bps[pbn]], True)
                            si_ = scn["n"] % 2
                            scn["n"] += 1
                            ACT(scs[si_][:T, 0:128], psum[pbn][:T, 0:128], AF.Exp, [bps[pbn]], [b_scs[si_]])
                            TT(pTn[:T, half, :], scs[si_][:T, 0:128], Es[:T, 4, half * 128:(half + 1) * 128], ALU.mult, [b_scs[si_], b_E], [b_pTn])

                def S3(i):
                    slot, T = hslots[i]
                    qi = slot % 2
                    pti = slot % 2
                    if slot < 8:
                        blocks = kvblocks(slot)
                        for half in range(2):
                            for j in range(4):
                                for bi_, (kt_ap, bkt, va_ap, bva, bias_ap) in enumerate(blocks):
                                    MM(psum[P_PV[half]][:T, j * 65:(j + 1) * 65], pT[pti][:, bi_ * 2 + half, j * 128:(j + 1) * 128], va_ap,
                                       bi_ == 0, bi_ == 1, [b_pT[pti], bva], [bps[P_PV[half]]], (bi_ == 1 and j == 3))
                    else:
                        for half in range(2):
                            for j in range(4):
                                for jb in range(4):
                                    MM(psum[P_PV[half]][:T, j * 65:(j + 1) * 65], pT[pti][:, half, jb * 128 + j * T:jb * 128 + (j + 1) * T], vcaug[:, jb, g, :],
                                       jb == 0, False, [b_pT[pti], b_vca], [bps[P_PV[half]]], False)
                                MM(psum[P_PV[half]][:T, j * 65:(j + 1) * 65], pTn[:T, half, j * T:(j + 1) * T], vaug[:T, 9, g, :],
                                   False, True, [b_pTn, b_va[9]], [bps[P_PV[half]]], j == 3)
                    for half in range(2):
                        pvv = psum[P_PV[half]][:T, 0:260].rearrange("p (a b) -> p a b", b=65)
                        TS(rden[qi][:T, half * 4:(half + 1) * 4], pvv[:, :, 64], 1.0, ALU.add, [bps[P_PV[half]]], [b_rden[qi]])
                        S.op("dve", lambda e, half=half, qi=qi, T=T: e.reciprocal(out=rden[qi][:T, half * 4:(half + 1) * 4], in_=rden[qi][:T, half * 4:(half + 1) * 4]),
                             [b_rden[qi]], [b_rden[qi]])
                        ov = oat[:T, slot, :].rearrange("p (j h d) -> p j h d", h=2, d=64)[:, :, half, :]
                        rb = rden[qi][:T, half * 4:(half + 1) * 4].unsqueeze(2).broadcast_to([T, 4, 64])
                        TT(ov, pvv[:, :, 0:64], rb, ALU.mult, [bps[P_PV[half]], b_rden[qi]], [b_oat[slot]])

                wtz, bwz = WLOAD([(w_in_b[:, 2048 + 512 * g:2048 + 512 * (g + 1)], 0)])

                def z_evac2(slot, T, zi, g=g):
                    pb = tr_bank()
                    pv = psb[pb][:, 0:4 * T].rearrange("p (a b) -> p a b", b=T)
                    for jj in range(4):
                        TR(pv[:, jj, :], ozt[zi][:T, jj * 128:(jj + 1) * 128], [b_ozt[zi]], [bps[pb]], inc=(jj == 3))
                    c0 = slot * 128
                    CP(ozT[:, 4 * g:4 * g + 4, c0:c0 + T], pv, [bps[pb]], [b_ozT[slot]])

                def z_evac(slot, T, ps_ap, bp, g=g):
                    zi = slot % 2
                    ACT(szt[zi][:T, :], ps_ap, AF.Tanh, [bp, b_rs2], [b_szt[zi]], scale=hrs2[:T, slot + 1:slot + 2])
                    STT(szt[zi][:T, :], szt[zi][:T, :], 1.0, oat[:T, slot, :], ALU.add, ALU.mult, [b_szt[zi], b_oat[slot]], [b_szt[zi]])
                    STT(ozt[zi][:T, :], ps_ap, hrs2[:T, slot + 1:slot + 2], szt[zi][:T, :], ALU.mult, ALU.mult, [bp, b_rs2, b_szt[zi]], [b_ozt[zi]])
                    if pendz:
                        z_evac2(*pendz.pop())
                    pendz.append((slot, T, zi))
                pendz = []

                nh = len(hslots)
                S1(0)
                S1(1)
                S2(0)
                for i in range(nh):
                    if i + 2 < nh:
                        S1(i + 2)
                    if i + 1 < nh:
                        S2(i + 1, 0)
                    S3(i)
                    if i + 1 < nh:
                        S2(i + 1, 1)
                    proj(wtz, bwz, 0, 512, [hslots[i]], hbT, b_hbT, z_evac)
                z_evac2(*pendz.pop())
            barrier()

            stop_check('p3')
            P_PROJ[:] = [0, 1, 4, 5, 6, 7]
            A.setrange(R2, ARENA)
            h2 = A.alloc([128, 9, D], F32); b_h2 = [Buf(f"h2{i}", f"h2") for i in range(9)]
            gfin = A.alloc([128, D], F32); b_gfin = Buf("gfin", "gfin")
            hpc = [A.alloc([128, 512], F32) for _ in range(NP)]; b_hpc = [Buf(f"hpc{i}", f"hpc{i}") for i in range(NP)]
            junk3 = A.alloc([128, 512], BF16); b_junk3 = Buf("junk3")
            S.dma("sp", gfin[:], gfin_d, writes=[b_gfin])
            cnt4 = {"n": 0}
            b_rs3s = [Buf(f"rs3s{i}") for i in range(9)]
            for i_ in range(9):
                b_rs3s[i_].w = b_rs3.w
            for blk in range(4):
                wt, bw = WLOAD([(w_out_b[:, 512 * blk:512 * (blk + 1)], 0)])

                def f_evac(slot, T, ps_ap, bp, blk=blk):
                    pi = cnt4["n"] % NP
                    cnt4["n"] += 1
                    cs = slice(512 * blk, 512 * (blk + 1))
                    r0 = slot * 128
                    S.dma("sp", hpc[pi][:T, :], h1s[r0:r0 + T, cs], reads=[b_h1s[slot]], writes=[b_hpc[pi]])
                    TT(h2[:T, slot, cs], ps_ap, hpc[pi][:T, :], ALU.add, [bp, b_hpc[pi]], [b_h2[slot]])
                    ACT(junk3[:T, :], h2[:T, slot, cs], AF.Square, [b_h2[slot]], [b_junk3, b_rs3s[slot]], accum_out=ss3[:T, slot, blk:blk + 1])
                    if blk == 3:
                        S.op("dve", lambda e, slot=slot, T=T: e.reduce_sum(out=rs3[:T, slot:slot + 1], in_=ss3[:T, slot, :], axis=AX.X), [b_rs3s[slot]], [b_rs3s[slot]])
                        RSTD(rs3[:T, slot:slot + 1], rs3[:T, slot:slot + 1], [b_rs3s[slot]], [b_rs3s[slot]])
                        STT(h2[:T, slot, :], h2[:T, slot, :], rs3[:T, slot:slot + 1], gfin[:T, :], ALU.mult, ALU.mult, [b_h2[slot], b_rs3s[slot], b_gfin], [b_h2[slot]])
                        if slot < 8:
                            S.dma("pool", y_o[slot * 128:(slot + 1) * 128, :], h2[:, slot, :], reads=[b_h2[slot]])
                        else:
                            S.dma("pool", ys_o, h2[:32, slot, :], reads=[b_h2[slot]])
                proj(wt, bw, 0, 512, hslots, ozT, b_ozT, f_evac)
        except _Stop:
            pass
        S.finish()
        print("instr counts", {k: len(v) for k, v in S.streams.items()}, "waits", S.nwaits, "sem counts", {k: v for k, v in S.cnt.items() if k.startswith("e_")})
        S.emit()
    return nc


_CACHE = {}


def _consts():
    import ml_dtypes
    if "c" in _CACHE:
        return _CACHE["c"]
    c = {}
    c["ident"] = np.eye(128, dtype=np.float32).astype(ml_dtypes.bfloat16)
    fm = np.zeros((128, 32), np.float32)
    for p_ in range(128):
        fm[p_, p_ % 32] = 1.0
    c["fold"] = fm.astype(ml_dtypes.bfloat16)
    s = np.arange(128)[:, None]
    t = np.arange(128)[None, :]
    glac = np.zeros((128, 2, 3, 128), np.float32)
    glac[:, 0, 0, :] = np.where(s <= t, -1.0 / 16, 0.0)
    glac[:, 0, 1, :] = np.where(s > t, -1.0 / 16, 0.0)
    glac[:, 0, 2, :] = np.where(s <= t, 1.0, 0.0)
    same = (s // 8 == t // 8) & (s < 32) & (t < 32)
    glac[:, 1, 0, :] = np.where(same & (s <= t), -1.0 / 16, 0.0)
    glac[:, 1, 1, :] = np.where(same & (s > t), -1.0 / 16, 0.0)
    glac[:, 1, 2, :] = np.where(same & (s <= t), 1.0, 0.0)
    c["glac"] = glac
    segi = np.zeros((128, 2, 4), np.float32)
    segi[:, 0, 0] = -1.0 / 16
    segm = np.zeros((128, 4), np.float32)
    for r in range(32):
        segi[r, 1, r // 8] = -1.0 / 16
        segm[r, r // 8] = 1.0
    c["segi"] = segi
    c["segm"] = segm
    slopes = np.exp2(-8.0 * np.arange(1, 33, dtype=np.float64) / 32.0)
    bp = np.full((4, 128, 2, 1024), NEG, np.float64)
    bs = np.full((4, 128, 5, 256), NEG, np.float64)
    sv = np.arange(128)[:, None]
    tv = np.arange(128)[None, :]
    t32 = np.arange(32)[None, :]
    samp = t32 // 8
    ii = t32 % 8
    for g in range(4):
        for half in range(2):
            for j in range(4):
                sl = slopes[8 * g + 2 * j + half]
                cc = half * 512 + j * 128
                dist = tv + 128 - sv
                bp[g, :, 0, cc:cc + 128] = np.where(sv >= tv, -sl * dist, NEG)
                dist = tv - sv
                bp[g, :, 1, cc:cc + 128] = np.where(dist >= 0, -sl * dist, NEG)
                c2 = half * 128 + j * 32
                for jb in range(4):
                    ok = (samp == jb) & (sv >= ii)
                    bs[g, :, jb, c2:c2 + 32] = np.where(ok, -sl * (128 + ii - sv), NEG)
                ss_ = sv // 8
                is_ = sv % 8
                ok = (ss_ == samp) & (is_ <= ii) & (sv < 32)
                bs[g, :, 4, c2:c2 + 32] = np.where(ok, -sl * (ii - is_), NEG)
    c["bias_p"] = bp.astype(np.float32)
    c["bias_s"] = bs.astype(np.float32)
    _CACHE["c"] = c
    return c


def kernel(x_prompt, x_sample, state_gla, cache_k_win, cache_v_win, g_norm_a, w_in_a, w_gate_up, b_gate,
           g_onorm_a, w_out_a, g_norm_kv, w_kv, g_norm_b, w_in_b, sinks, w_out_b, g_final):
    f = lambda a: np.ascontiguousarray(np.asarray(a, dtype=np.float32))
    x_prompt = f(x_prompt); x_sample = f(x_sample); state_gla = f(state_gla)
    cache_k_win = f(cache_k_win); cache_v_win = f(cache_v_win)
    w_in_a = f(w_in_a)[0]; w_out_a = f(w_out_a)[0]; w_kv = f(w_kv); w_in_b = f(w_in_b)[0]; w_out_b = f(w_out_b)[0]
    if "nc" not in _CACHE:
        _CACHE["nc"] = build_program()
    nc = _CACHE["nc"]
    c = _consts()
    pk = lambda v: np.ascontiguousarray(f(v).reshape(16, 128).T)
    gvec = np.ascontiguousarray(np.stack([pk(g_norm_a), pk(g_norm_kv), pk(g_norm_b)], axis=1))
    bc = lambda v, n: np.ascontiguousarray(np.broadcast_to(f(v).reshape(1, n), (128, n)))
    shared = dict(
        w_in_a=w_in_a, w_out_a=w_out_a, w_kv=w_kv, w_in_b=w_in_b, w_out_b=w_out_b,
        wgu=f(w_gate_up)[0], gon_bc=bc(g_onorm_a, 2048), gfin_bc=bc(g_final, 2048),
        bias_p=c["bias_p"], bias_s=c["bias_s"],
        cpack16=np.ascontiguousarray(np.concatenate([c["ident"], c["fold"]], axis=1)),
    )
    wglow_h = np.ascontiguousarray(w_in_a[:, 6144:6160].reshape(16, 128, 16).transpose(1, 0, 2)).reshape(128, 256)

    def cpack_for(pm):
        parts = [c["glac"].reshape(128, 768), c["segi"].reshape(128, 8), c["segm"], bc(b_gate, 1024), gvec.reshape(128, 48),
                 bc(sinks, 32), np.full((128, 1), pm, np.float32), wglow_h, np.zeros((128, 3), np.float32)]
        return np.ascontiguousarray(np.concatenate(parts, axis=1).astype(np.float32))
    in_maps = []
    for core in range(8):
        b, half = core // 2, core % 2
        if half == 0:
            xw = np.concatenate([np.zeros((1024, 2048), np.float32), x_prompt[b, :1024]], axis=0)
        else:
            xw = x_prompt[b]
        m = dict(shared)
        m["xw"] = np.ascontiguousarray(xw)
        m["xs"] = np.ascontiguousarray(x_sample[4 * core:4 * core + 4].reshape(32, 2048))
        m["st_in"] = np.ascontiguousarray(state_gla[0, 4 * core:4 * core + 4])
        m["ck"] = np.ascontiguousarray(cache_k_win[4 * core:4 * core + 4].reshape(4, 128, 256))
        m["cv"] = np.ascontiguousarray(cache_v_win[4 * core:4 * core + 4].reshape(4, 128, 256))
        m["cpack"] = cpack_for(NEG if half == 0 else 0.0)
        in_maps.append(m)
    if _CACHE.get('hook') is not None:
        return _CACHE['hook'](nc, in_maps)
    res = run_bass_kernel_spmd(nc, in_maps, core_ids=list(range(8)))
    R = res.results
    y_prompt = np.zeros((4, 2048, 2048), np.float32)
    y_sample = np.zeros((32, 8, 2048), np.float32)
    gla_prompt = np.zeros((1, 4, 4, 256, 512), np.float32)
    gla_sample = np.zeros((1, 32, 4, 256, 512), np.float32)
    kwp = np.zeros((4, 128, 4, 64), np.float32); vwp = np.zeros((4, 128, 4, 64), np.float32)
    kws = np.zeros((32, 128, 4, 64), np.float32); vws = np.zeros((32, 128, 4, 64), np.float32)
    for core in range(8):
        b, half = core // 2, core % 2
        r = R[core]
        y_prompt[b, half * 1024:(half + 1) * 1024] = r["y"]
        y_sample[4 * core:4 * core + 4] = np.asarray(r["ys"]).reshape(4, 8, 2048)
        gla_sample[0, 4 * core:4 * core + 4] = r["ssamp"]
        kws[4 * core:4 * core + 4] = np.asarray(r["kwin_s"]).reshape(4, 128, 4, 64)
        vws[4 * core:4 * core + 4] = np.asarray(r["vwin_s"]).reshape(4, 128, 4, 64)
        if half == 1:
            gla_prompt[0, b] = r["sfin"]
            kwp[b] = np.asarray(r["kwin"]).reshape(128, 4, 64)
            vwp[b] = np.asarray(r["vwin"]).reshape(128, 4, 64)
    return (y_prompt, y_sample, gla_prompt, gla_sample, kwp, vwp, kws, vws)
```

```python
import numpy as np
from contextlib import ExitStack
import concourse.bass as bass
import concourse.mybir as mybir
from concourse.bass_utils import run_bass_kernel_spmd

F32 = mybir.dt.float32
BF16 = mybir.dt.bfloat16
AF = mybir.ActivationFunctionType
ALU = mybir.AluOpType
AX = mybir.AxisListType


class Buf:
    __slots__ = ("name", "w", "r", "dsem", "excl")

    def __init__(self, name, dsem=None, excl=False):
        self.name = name
        self.w = None
        self.r = {}
        self.dsem = dsem
        self.excl = excl


class Sched:
    ENG = ("pe", "act", "dve", "pool", "sp")

    def __init__(self, nc, stack):
        self.nc = nc
        self.stack = stack
        self.sems = {}
        self.cnt = {}
        self.streams = {e: [] for e in self.ENG}
        self.waited = {e: {} for e in self.ENG}
        for e in self.ENG:
            self._sem("e_" + e)
        self.nwaits = 0

    def _sem(self, key):
        if key not in self.sems:
            self.sems[key] = self.stack.enter_context(self.nc.semaphore(key))
            self.cnt[key] = 0
        return key

    def _need(self, eng, tok, waits):
        key, val = tok
        if key == "e_pe" and eng == "pe":
            return
        if key == "e_sp" and eng == "sp":
            return
        if self.waited[eng].get(key, 0) >= val:
            return
        self.waited[eng][key] = val
        waits.append((key, val))

    def _deps(self, eng, reads, writes):
        waits = []
        for b in reads:
            if b.w is not None:
                self._need(eng, b.w, waits)
        for b in writes:
            if b.w is not None:
                self._need(eng, b.w, waits)
            for k, v in b.r.items():
                self._need(eng, (k, v), waits)
        self.nwaits += len(waits)
        return waits

    def _mark(self, tok, reads, writes):
        k, v = tok
        for b in reads:
            if b.r.get(k, 0) < v:
                b.r[k] = v
        for b in writes:
            b.w = tok
            b.r = {}

    def op(self, eng, fn, reads=(), writes=(), inc=True):
        ex = [b for b in reads if b.excl]
        if ex:
            writes = list(writes) + [b for b in ex if b not in writes]
            reads = [b for b in reads if not b.excl]
        waits = self._deps(eng, reads, writes)
        key = "e_" + eng
        if inc:
            self.cnt[key] += 1
            tok = (key, self.cnt[key])
        else:
            tok = (key, self.cnt[key] + 1)
        self._mark(tok, reads, writes)
        self.streams[eng].append((waits, fn, (key, 1) if inc else None))

    def dma(self, eng, out, in_, reads=(), writes=(), sem=None):
        waits = self._deps(eng, reads, writes)
        if sem is None:
            for b in list(writes) + list(reads):
                if b.dsem is not None:
                    sem = b.dsem
                    break
        assert sem is not None
        key = self._sem("d_" + sem)
        self.cnt[key] += 16
        tok = (key, self.cnt[key])
        self._mark(tok, reads, writes)
        self.streams[eng].append(
            (waits, lambda e, o=out, i=in_: e.dma_start(out=o, in_=i), (key, 16)))

    def finish(self, eng="sp"):
        waits = []
        for key, c in self.cnt.items():
            if key.startswith("d_") and c > 0:
                self._need(eng, (key, c), waits)
        self.streams[eng].append((waits, None, None))

    def emit(self):
        nc = self.nc
        streams = self.streams
        sems = self.sems

        def replay(name, e):
            for waits, fn, inc in streams[name]:
                for key, val in waits:
                    e.wait_ge(sems[key], val)
                if fn is None:
                    continue
                ins = fn(e)
                if inc is not None:
                    ins.then_inc(sems[inc[0]], inc[1])

        with nc.Block() as block:
            @block.tensor
            def _(e):
                replay("pe", e)

            @block.scalar
            def _(e):
                replay("act", e)

            @block.vector
            def _(e):
                replay("dve", e)

            @block.gpsimd
            def _(e):
                replay("pool", e)

            @block.sync
            def _(e):
                replay("sp", e)

D = 2048
NEG = -30000.0
EPS = 1e-6
KB = 1024


def _prod(xs):
    r = 1
    for v in xs:
        r *= v
    return r


class Arena:
    def __init__(self, nc, st, nbytes):
        self.n = nbytes
        self.t = st.enter_context(nc.sbuf_tensor("arena", [128, nbytes // 2], BF16))
        self.lo = 0
        self.hi = nbytes

    def at(self, off, shape, dt):
        esz = 4 if dt == F32 else 2
        n = _prod(shape[1:]) * esz
        assert off % 4 == 0 and off + n <= self.n, (off, n, self.n)
        a = self.t[:, off // 2:(off + n) // 2]
        if dt != BF16:
            a = a.bitcast(dt)
        if len(shape) == 3:
            a = a.rearrange("p (a b) -> p a b", b=shape[2])
        elif len(shape) == 4:
            a = a.rearrange("p (a b c) -> p a b c", b=shape[2], c=shape[3])
        elif len(shape) == 5:
            a = a.rearrange("p (a b c d) -> p a b c d", b=shape[2], c=shape[3], d=shape[4])
        return a

    def setrange(self, lo, hi):
        self.lo, self.hi = lo, hi

    def alloc(self, shape, dt):
        esz = 4 if dt == F32 else 2
        n = _prod(shape[1:]) * esz
        n = (n + 63) // 64 * 64
        assert self.lo + n <= self.hi, ("arena overflow", self.lo, n, self.hi)
        a = self.at(self.lo, shape, dt)
        self.lo += n
        return a


class _Stop(Exception):
    pass


def build_program():
    import os
    STOP = os.environ.get('MK_STOP', '')

    def stop_check(p):
        if STOP == p:
            raise _Stop()
    nc = bass.Bass("TRN2", target_bir_lowering=False)

    def din(name, shape, dt=F32):
        return nc.dram_tensor(name, list(shape), dt, kind="ExternalInput").ap()

    def dout(name, shape, dt=F32):
        return nc.dram_tensor(name, list(shape), dt, kind="ExternalOutput").ap()

    xw = din("xw", [2048, D]); xs = din("xs", [32, D])
    st_in = din("st_in", [4, 4, 256, 512])
    ck = din("ck", [4, 128, 256]); cv = din("cv", [4, 128, 256])
    w_in_a = din("w_in_a", [D, 6160]); w_out_a = din("w_out_a", [D, D])
    w_kv = din("w_kv", [D, 512]); w_in_b = din("w_in_b", [D, 4096]); w_out_b = din("w_out_b", [D, D])
    wgu_d = din("wgu", [16, 1024])
    gon_d = din("gon_bc", [128, D]); gfin_d = din("gfin_bc", [128, D])
    NCP = 768 + 8 + 4 + 1024 + 48 + 32 + 1 + 256 + 3
    cpack_d = din("cpack", [128, NCP]); cpack16_d = din("cpack16", [128, 160], BF16)
    biasp_d = din("bias_p", [4, 128, 2, 1024]); biass_d = din("bias_s", [4, 128, 5, 256])

    y_o = dout("y", [1024, D]); ys_o = dout("ys", [32, D])
    sfin_o = dout("sfin", [4, 256, 512]); ssamp_o = dout("ssamp", [4, 4, 256, 512])
    kwin_o = dout("kwin", [128, 256]); vwin_o = dout("vwin", [128, 256])
    kwins_o = dout("kwin_s", [4, 128, 256]); vwins_o = dout("vwin_s", [4, 128, 256])
    h1s = nc.dram_tensor("h1s", [1056, D], F32).ap()

    with ExitStack() as st:
        S = Sched(nc, st)
        sbt = lambda name, shape, dt: st.enter_context(nc.sbuf_tensor("sb_" + name, shape, dt))
        NW = 2
        wbuf = [sbt(f"wbuf{i}", [128, 16, 512], BF16) for i in range(NW)]
        bwbuf = [Buf(f"wbuf{i}", f"w{i}") for i in range(NW)]
        b_const = Buf("const", "cst")
        cpack = sbt("cpack", [128, NCP], F32); cpack16 = sbt("cpack16", [128, 160], BF16)
        ident = cpack16[:, 0:128]; foldm = cpack16[:, 128:160]
        sfold = [sbt(f"sfold{i}", [128, 512], BF16) for i in range(2)]; b_sfold = [Buf(f"sfold{i}") for i in range(2)]
        _o = 0
        glac = cpack[:, _o:_o + 768].rearrange("p (a b c) -> p a b c", a=2, b=3); _o += 768
        segi = cpack[:, _o:_o + 8].rearrange("p (a b) -> p a b", a=2); _o += 8
        segm = cpack[:, _o:_o + 4]; _o += 4
        bgate = cpack[:, _o:_o + 1024]; _o += 1024
        wgu = sbt("wgu", [16, 1024], BF16); b_wgu = Buf("wgu", "wgu")
        gvec = cpack[:, _o:_o + 48].rearrange("p (a b) -> p a b", a=3); _o += 48
        sink = cpack[:, _o:_o + 32]; _o += 32
        pmask = cpack[:, _o:_o + 1]; _o += 1
        wglow_f = cpack[:, _o:_o + 256].rearrange("p (a b) -> p a b", a=16); _o += 256
        wglow = sbt("wglow", [128, 16, 16], BF16); b_wglow = Buf("wglow")
        rs0 = sbt("rs0", [128, 8], F32); rs1 = sbt("rs1", [128, 10], F32)
        rs2 = sbt("rs2", [128, 10], F32); rs3 = sbt("rs3", [128, 9], F32)
        ss0 = sbt("ss0", [128, 8], F32); ss1 = sbt("ss1", [128, 10], F32)
        ss2 = sbt("ss2", [128, 10, 4], F32); ss3 = sbt("ss3", [128, 9, 4], F32)
        b_rs0 = Buf("rs0"); b_rs1 = Buf("rs1"); b_rs2 = Buf("rs2"); b_rs3 = Buf("rs3")
        small = sbt("small", [128, 64], F32); b_small = Buf("small")
        hrs2 = sbt("hrs2", [128, 10], F32)
        ARENA = nc.sbuf_bytes_remaining - 1024
        ARENA = ARENA // 64 * 64
        A = Arena(nc, st, ARENA)
        psum = [st.enter_context(nc.psum_tensor(f"ps{i}", [128, 512], F32)) for i in range(8)]
        psb = [p[:].bitcast(BF16) for p in psum]
        bps = [Buf(f"ps{i}", excl=True) for i in range(8)]

        S.dma("act", cpack[:], cpack_d, writes=[b_const])
        S.dma("sp", cpack16[:], cpack16_d, writes=[b_const])
        S.op("dve", lambda e: e.tensor_copy(out=wglow[:], in_=wglow_f[:]), reads=[b_const], writes=[b_wglow])
        S.dma("pool", wgu[:], wgu_d, writes=[b_wgu])

        rr = {"n": 0}

        def MM(out, lhsT, rhs, start, stop, R, W, inc, tp=None):
            if tp is None:
                S.op("pe", lambda e: e.matmul(out, lhsT=lhsT, rhs=rhs, start=start, stop=stop), R, W, inc)
            else:
                S.op("pe", lambda e: e.matmul(out, lhsT=lhsT, rhs=rhs, start=start, stop=stop, tile_position=tp), R, W, inc)

        foldn = {"n": 0}

        def PROJ16(pb, T, ncols, lhs_of_k, rhs_of_k, R):
            if T == 128:
                for k in range(16):
                    MM(psum[pb][:T, 0:ncols], lhs_of_k(k), rhs_of_k(k), k == 0, k == 15, R, [bps[pb]], inc=(k == 15))
                return
            groups = [[0, 4, 8, 12], [1, 5, 9, 13], [2, 6, 10, 14], [3, 7, 11, 15]]
            order = []
            for step in range(4):
                for j in range(4):
                    if step < len(groups[j]):
                        order.append((j, step))
            for n_, (j, step) in enumerate(order):
                k = groups[j][step]
                MM(psum[pb][32 * j:32 * j + 32, 0:ncols], lhs_of_k(k), rhs_of_k(k), step == 0, step == len(groups[j]) - 1, R, [bps[pb]],
                   inc=(n_ == len(order) - 1), tp=(0, 32 * j))
            fi = foldn["n"] % 2
            foldn["n"] += 1
            CP(sfold[fi][:, 0:ncols], psum[pb][:, 0:ncols], [bps[pb]], [b_sfold[fi]])
            MM(psum[pb][0:32, 0:ncols], foldm[:, :], sfold[fi][:, 0:ncols], True, True, [b_sfold[fi], b_const], [bps[pb]], True)

        def TR(out, in_, R, W, inc, idn=None):
            idn_ap = ident[:in_.shape[0], :in_.shape[0]]
            S.op("pe", lambda e: e.transpose(out=out, in_=in_, identity=idn_ap), list(R) + [b_const], W, inc)

        def ACT(out, in_, func, R, W, **kw):
            S.op("act", lambda e: e.activation(out=out, in_=in_, func=func, **kw), R, W)

        def TT(out, in0, in1, op, R, W, eng="dve"):
            S.op(eng, lambda e: e.tensor_tensor(out=out, in0=in0, in1=in1, op=op), R, W)

        def TS(out, in0, s1, op0, R, W, s2=None, op1=None, eng="dve", accum=None):
            kw = {}
            if op1 is not None:
                kw["op1"] = op1
            if accum is not None:
                kw["accum_out"] = accum
            S.op(eng, lambda e: e.tensor_scalar(out=out, in0=in0, scalar1=s1, scalar2=s2, op0=op0, **kw), R, W)

        def STT(out, in0, scalar, in1, op0, op1, R, W, accum=None):
            kw = {}
            if accum is not None:
                kw["accum_out"] = accum
            S.op("dve", lambda e: e.scalar_tensor_tensor(out=out, in0=in0, scalar=scalar, in1=in1, op0=op0, op1=op1, **kw), R, W)

        def CP(out, in_, R, W, eng=None):
            if eng is None:
                rr["n"] += 1
                eng = "act" if rr["n"] % 2 else "dve"
            if eng == "act":
                ACT(out, in_, AF.Copy, R, W)
            else:
                S.op(eng, lambda e: e.tensor_copy(out=out, in_=in_), R, W)

        def SCALE(out, in_, sc_ap, R, W, mul=None, eng=None):
            if eng is None:
                rr["n"] += 1
                eng = "act" if rr["n"] % 2 else "dve"
            if eng == "act" and mul is None:
                ACT(out, in_, AF.Copy, R, W, scale=sc_ap)
            else:
                if mul is None:
                    TS(out, in_, sc_ap, ALU.mult, R, W)
                else:
                    TS(out, in_, sc_ap, ALU.mult, R, W, s2=mul, op1=ALU.mult)

        def MEMSET(ap, val, W, eng="dve"):
            S.op(eng, lambda e: e.memset(ap, val), (), W)

        def barrier():
            engs = ("pe", "act", "dve", "sp")
            for e in engs:
                waits = []
                for k, c in S.cnt.items():
                    if c > 0 and k != "e_" + e and k != "e_sp" and k != "e_pool" and not k.startswith("d_w"):
                        S._need(e, (k, c), waits)
                S.streams[e].append((waits, None, None))

        def RSTD(rs_ap, ss_ap, R, W):
            ACT(rs_ap, ss_ap, AF.Ln, R, W, scale=1.0 / D, bias=EPS)
            ACT(rs_ap, rs_ap, AF.Exp, W, W, scale=-0.5)

        wstate = {"n": 0}
        b_early = Buf("early")
        wlist = [(wbuf[i], bwbuf[i]) for i in range(NW)]

        def WLOAD(parts):
            i = wstate["n"] % len(wlist)
            first = wstate["n"] == 0
            wstate["n"] += 1
            wb_, bwb_ = wlist[i]
            for ap, off in parts:
                nco = ap.shape[1]
                src = ap.rearrange("(k p) n -> p k n", p=128)
                for kh in range(2):
                    S.dma("pool", wb_[:, kh * 8:(kh + 1) * 8, off:off + nco], src[:, kh * 8:(kh + 1) * 8, :], reads=[b_const, b_early] if first else [b_const], writes=[bwb_])
            return wb_, bwb_

        def slotT(i):
            return 32 if i == 9 else 128

        def xrows(i):
            return xs if i == 9 else xw[(7 + i) * 128:(8 + i) * 128, :]

        pstate = {"n": 0}

        def proj_bank():
            pstate["n"] += 1
            return pstate["n"] % 2

        R1 = 0
        R2 = 37888
        R3 = 75776
        xT = A.at(R1, [128, 16, 1184], BF16); b_xT = [Buf(f"xT{i}") for i in range(10)]
        ogT = A.at(R2, [128, 16, 1184], BF16); b_ogT = [Buf(f"ogT{i}") for i in range(10)]
        A.setrange(R2, R3)
        xf = [A.alloc([128, D], F32) for _ in range(2)]; b_xf = [Buf(f"xf{i}", f"xf{i}") for i in range(2)]
        xb = A.alloc([128, D], BF16); b_xb = Buf("xb")
        junk = A.alloc([128, D], BF16); b_junk = Buf("junk")
        A.setrange(R3, ARENA)
        qkd = A.alloc([128, 10, 3, 256], BF16); b_qkd = [Buf(f"qkd{i}") for i in range(10)]
        qdT = A.alloc([128, 10, 2, 128], BF16); b_qdT = [Buf(f"qdT{i}") for i in range(10)]
        scm = A.alloc([128, 10, 128], BF16); b_scm = [Buf(f"scm{i}") for i in range(10)]
        v_s = A.alloc([128, 10, 512], BF16); b_v = [Buf(f"v{i}") for i in range(10)]
        glowT = A.alloc([128, 1184], BF16); b_glow = [Buf(f"glow{i}") for i in range(10)]
        Sf = A.alloc([128, 4, 2, 512], F32); b_Sf = [Buf(f"Sf{h}", "sfin") for h in range(4)]
        Sb = A.alloc([128, 2, 512], BF16); b_Sb = [Buf("Sb0"), Buf("Sb1")]
        sSf = [A.alloc([128, 2, 512], F32) for _ in range(2)]; b_sSf = [Buf(f"sSf{i}", f"sS{i}") for i in range(2)]
        sSb = [A.alloc([128, 2, 512], BF16)] * 2; b_sSb = [[Buf("sSb")] * 2] * 2
        gon = A.alloc([128, 512], F32); b_gon = Buf("gon", "gon")
        NG = 3
        nl = [A.alloc([128, 256], F32) for _ in range(NG)]; b_nl = [Buf(f"nl{i}") for i in range(NG)]
        eb = [A.alloc([128, 3, 256], F32) for _ in range(2)]; b_eb = [Buf(f"eb{i}") for i in range(2)]
        ebl = A.alloc([128, 10, 8], F32); b_ebl = [Buf(f"ebl{i}") for i in range(10)]
        kdT = [A.alloc([128, 2, 128], BF16) for _ in range(2)]; b_kdT = [Buf(f"kdT{i}") for i in range(2)]
        qTm = A.alloc([128, 4, 2, 32], BF16); b_qTm = Buf("qTm")
        klm = A.alloc([128, 4, 256], BF16); b_klm = Buf("klm")
        sso = [A.alloc([128, 2], F32) for _ in range(2)]; b_sso = [Buf(f"sso{i}") for i in range(2)]
        srt = [A.alloc([128, 512], BF16) for _ in range(2)]; b_srt = [Buf(f"srt{i}") for i in range(2)]
        ogt = [A.alloc([128, 512], BF16) for _ in range(2)]; b_ogt = [Buf(f"ogt{i}") for i in range(2)]
        sqj = A.alloc([128, 512], BF16); b_sqj = Buf("sqj")
        print("phase01 arena used", A.lo, "of", ARENA)

        P_PROJ = (0, 1); P_TR = (2, 3); P_G = 4; P_BC = 5; P_O = 6; P_SU = 7
        b_p4a = b_p4b = b_p4c = bps[4]
        trstate = {"n": 0}

        def tr_bank():
            trstate["n"] += 1
            return P_TR[trstate["n"] % 2]

        def load_transpose(slots, rs, ss, b_rs, idx0=0):
            n = len(slots)
            for idx, (slot, rows, T) in enumerate(slots):
                f = (idx0 + idx) % 2
                S.dma("sp", xf[f][:T, :], rows, reads=[b_const], writes=[b_xf[f]])
                if idx == 1 and b_early.w is None:
                    b_early.w = b_xf[f].w
                ACT(junk[:T, :], xf[f][:T, :], AF.Square, [b_xf[f]], [b_junk, b_rs], accum_out=ss[:T, slot:slot + 1])
                ACT(xb[:T, :], xf[f][:T, :], AF.Copy, [b_xf[f]], [b_xb])
                c0 = slot * 128
                for half in range(2):
                    pb = tr_bank()
                    pv = psb[pb][:, 0:8 * T].rearrange("p (a b) -> p a b", b=T)
                    for kk in range(8):
                        k = half * 8 + kk
                        TR(pv[:, kk, :], xb[:T, k * 128:(k + 1) * 128], [b_xb], [bps[pb]], inc=(kk == 7))
                    gb = gvec[:, 0, half * 8:(half + 1) * 8].unsqueeze(2).broadcast_to([128, 8, T])
                    TT(xT[:, half * 8:(half + 1) * 8, c0:c0 + T], pv, gb, ALU.mult, [bps[pb], b_const], [b_xT[slot]])
            RSTD(rs[:, 0:10 if rs is rs1 else 8], ss[:, 0:10 if ss is ss1 else 8], [b_rs], [b_rs])

        def glow_for(slot, T):
            c0 = slot * 128
            pb = P_G
            for k in range(16):
                MM(psum[pb][0:16, 0:T], wglow[:, k, :], xT[:, k, c0:c0 + T], k == 0, k == 15, [b_wglow, b_xT[slot]], [b_p4a], inc=(k == 15))
            CP(glowT[0:16, c0:c0 + T], psum[pb][0:16, 0:T], [b_p4a], [b_glow[slot]])

        def proj(wt, bw, wcols, ncols, slots, xTt, b_x, evac):
            for slot, T in slots:
                c0 = slot * 128
                pb = P_PROJ[proj_bank()]
                PROJ16(pb, T, ncols, lambda k: xTt[:, k, c0:c0 + T], lambda k: wt[:, k, wcols:wcols + ncols], [b_x[slot], bw])
                evac(slot, T, psum[pb][:T, 0:ncols], bps[pb])

        LN16 = -2.772588722239781
        P_SUB = (7, 5)

        def head_passes(h, slots, rs, b_rs, full, wq_parts, wv_part, wr_part, post_pv=None, carry=None, last=True):
            n = len(slots)

            def G1(i):
                slot, T, sample = slots[i]
                c0 = slot * 128
                gi = i % NG
                pb = 4 + (i % 2)
                MM(psum[pb][:T, 0:256], glowT[0:16, c0:c0 + T], wgu[:, h * 256:(h + 1) * 256], True, True, [b_glow[slot], b_wgu], [bps[pb]], True)
                STT(nl[gi][:T, :], psum[pb][:T, 0:256], rs[:T, slot:slot + 1], bgate[:T, h * 256:(h + 1) * 256], ALU.mult, ALU.add,
                    [bps[pb], b_rs, b_const], [b_nl[gi]])
                ACT(nl[gi][:T, :], nl[gi][:T, :], AF.Exp, [b_nl[gi]], [b_nl[gi]], scale=-1.0)
                ACT(nl[gi][:T, :], nl[gi][:T, :], AF.Ln, [b_nl[gi]], [b_nl[gi]], bias=1.0)

            def G2(i):
                slot, T, sample = slots[i]
                gi = i % NG
                ci = 1 if sample else 0
                nseg = 4 if sample else 1
                pb = 6 + (i % 2)
                pg = 4 + (i % 2)
                if full:
                    MM(psum[pb][:T, 0:256], glac[:T, ci, 0, :T], nl[gi][:T, :], True, True, [b_nl[gi], b_const], [bps[pb]], False)
                MM(psum[pb][:T, 256:512], glac[:T, ci, 1, :T], nl[gi][:T, :], True, True, [b_nl[gi], b_const], [bps[pb]], True)
                for dc in range(2):
                    MM(psum[pg][:, 256 + dc * nseg:256 + (dc + 1) * nseg], nl[gi][:T, dc * 128:(dc + 1) * 128], segi[:T, ci, 0:nseg], True, True,
                       [b_nl[gi], b_const], [bps[pg]], dc == 1)
                if full:
                    ACT(eb[i % 2][:T, 0, :], psum[pb][:T, 0:256], AF.Exp, [bps[pb]], [b_eb[i % 2]], bias=LN16)
                    ACT(eb[i % 2][:T, 1, :], psum[pb][:T, 0:256], AF.Exp, [bps[pb]], [b_eb[i % 2]], scale=-1.0)
                ACT(eb[i % 2][:T, 2, :], psum[pb][:T, 256:512], AF.Exp, [bps[pb]], [b_eb[i % 2]])
                ACT(ebl[:, slot, 0:2 * nseg], psum[pg][:, 256:256 + 2 * nseg], AF.Exp, [bps[pg]], [b_ebl[slot]])

            p1bank = {}

            def P1_mm(i):
                slot, T, sample = slots[i]
                c0 = slot * 128
                pb = P_PROJ[proj_bank()]
                p1bank[i] = pb
                wc, nco = (0, 512) if full else (256, 256)
                PROJ16(pb, T, nco, lambda k: xT[:, k, c0:c0 + T], lambda k: wt_qk[:, k, wc:wc + nco], [b_xT[slot], bw_qk])

            def P1_ev(i):
                slot, T, sample = slots[i]
                pb = p1bank[i]
                rsc = rs[:T, slot:slot + 1]
                if full:
                    STT(qkd[:T, slot, 0, :], psum[pb][:T, 0:256], rsc, eb[i % 2][:T, 0, :], ALU.mult, ALU.mult, [bps[pb], b_rs, b_eb[i % 2]], [b_qkd[slot]])
                    STT(qkd[:T, slot, 1, :], psum[pb][:T, 256:512], rsc, eb[i % 2][:T, 1, :], ALU.mult, ALU.mult, [bps[pb], b_rs, b_eb[i % 2]], [b_qkd[slot]])
                    STT(qkd[:T, slot, 2, :], psum[pb][:T, 256:512], rsc, eb[i % 2][:T, 2, :], ALU.mult, ALU.mult, [bps[pb], b_rs, b_eb[i % 2]], [b_qkd[slot]])
                else:
                    STT(qkd[:T, slot, 2, :], psum[pb][:T, 0:256], rsc, eb[i % 2][:T, 2, :], ALU.mult, ALU.mult, [bps[pb], b_rs, b_eb[i % 2]], [b_qkd[slot]])

            wt_qk, bw_qk = WLOAD(wq_parts)
            G1(0)
            if n > 1:
                G1(1)
            P1_mm(0)
            if carry:
                for fn_ in carry:
                    fn_()
            G2(0)
            for i in range(n):
                if i + 2 < n:
                    G1(i + 2)
                if i + 1 < n:
                    G2(i + 1)
                P1_ev(i)
                if i + 1 < n:
                    P1_mm(i + 1)

            wt_v, bw_v = WLOAD([wv_part])

            def TRQ(i):
                slot, T, sample = slots[i]
                ki = i % 2
                pb = tr_bank()
                pv = psb[pb][:, 0:4 * T].rearrange("p (a b) -> p a b", b=T)
                for a in range(2):
                    for dc in range(2):
                        TR(pv[:, a * 2 + dc, :], qkd[:T, slot, a, dc * 128:(dc + 1) * 128], [b_qkd[slot]], [bps[pb]], inc=(a == 1 and dc == 1))
                CP(qdT[:, slot, :, :T], pv[:, 0:2, :], [bps[pb]], [b_qdT[slot]], eng="act")
                CP(kdT[ki][:, :, :T], pv[:, 2:4, :], [bps[pb]], [b_kdT[ki]], eng="dve")

            def SC(i):
                slot, T, sample = slots[i]
                ki = i % 2
                ci = 1 if sample else 0
                pb = 4 + (i % 2)
                for dc in range(2):
                    MM(psum[pb][:T, 0:T], kdT[ki][:, dc, :T], qdT[:, slot, dc, :T], dc == 0, dc == 1, [b_kdT[ki], b_qdT[slot]], [bps[pb]], dc == 1)
                TT(scm[:T, slot, :T], psum[pb][:T, 0:T], glac[:T, ci, 2, :T], ALU.mult, [bps[pb], b_const], [b_scm[slot]])

            def PV(i):
                slot, T, sample = slots[i]
                c0 = slot * 128
                pb = P_PROJ[proj_bank()]
                PROJ16(pb, T, 512, lambda k: xT[:, k, c0:c0 + T], lambda k: wt_v[:, k, :], [b_xT[slot], bw_v])
                SCALE(v_s[:T, slot, :], psum[pb][:T, :], rs[:T, slot:slot + 1], [bps[pb], b_rs], [b_v[slot]])

            if full:
                TRQ(0)
            for i in range(n):
                if full and i + 1 < n:
                    TRQ(i + 1)
                PV(i)
                if full:
                    SC(i)
                if post_pv is not None:
                    post_pv(i)

            if full:
                wt_r, bw_r = WLOAD([wr_part])

            def RP_mm(i):
                slot, T, sample = slots[i]
                c0 = slot * 128
                ti = i % 2
                pb = P_PROJ[proj_bank()]
                PROJ16(pb, T, 512, lambda k: xT[:, k, c0:c0 + T], lambda k: wt_r[:, k, :], [b_xT[slot], bw_r])
                ACT(srt[ti][:T, :], psum[pb][:T, :], AF.Silu, [bps[pb], b_rs], [b_srt[ti]], scale=rs[:T, slot:slot + 1])
                TT(ogt[ti][:T, :], v_s[:T, slot, :], srt[ti][:T, :], ALU.mult, [b_v[slot], b_srt[ti]], [b_ogt[ti]])

            def RP_tr(i):
                slot, T, sample = slots[i]
                c0 = slot * 128
                ti = i % 2
                pbt = tr_bank()
                pv = psb[pbt][:, 0:4 * T].rearrange("p (a b) -> p a b", b=T)
                for jj in range(4):
                    TR(pv[:, jj, :], ogt[ti][:T, jj * 128:(jj + 1) * 128], [b_ogt[ti]], [bps[pbt]], inc=(jj == 3))
                CP(ogT[:, 4 * h:4 * h + 4, c0:c0 + T], pv, [bps[pbt]], [b_ogT[slot]])

            def ST(i):
                slot, T, sample = slots[i]
                nseg = 4 if sample else 1
                si = i % 2
                if sample:
                    if full:
                        MEMSET(qTm[:], 0.0, [b_qTm])
                        for j in range(4):
                            S.op("dve", lambda e, j=j, slot=slot: e.tensor_copy(out=qTm[:, j, :, 8 * j:8 * j + 8], in_=qdT[:, slot, :, 8 * j:8 * j + 8]), [b_qdT[slot]], [b_qTm])
                    for j in range(4):
                        TS(klm[:T, j, :], qkd[:T, slot, 2, :], segm[:T, j:j + 1], ALU.mult, [b_qkd[slot], b_const], [b_klm])
                for j in range(nseg):
                    if sample:
                        sj = j % 2
                        CP(sSb[sj][:], sSf[sj][:], [b_sSf[sj]], [b_sSb[sj][0]], eng="act")
                        Sfj, b_Sfj, Sbj, b_Sbj = sSf[sj], b_sSf[sj], sSb[sj], b_sSb[sj]
                        kl_ap = klm[:T, j, :]
                        b_kl = b_klm
                    else:
                        Sfj, b_Sfj, Sbj, b_Sbj = Sf[:, h], b_Sf[h], Sb, b_Sb
                        kl_ap = qkd[:T, slot, 2, :]
                        b_kl = b_qkd[slot]
                    if full:
                        for dc in range(2):
                            lh = qTm[:, j, dc, :T] if sample else qdT[:, slot, dc, :T]
                            MM(psum[6][:T, :], lh, Sbj[:, dc, :], (j == 0 and dc == 0), False, [b_qTm if sample else b_qdT[slot], b_Sbj[dc]], [bps[6]], False)
                        if j == nseg - 1:
                            MM(psum[6][:T, :], scm[:T, slot, :T], v_s[:T, slot, :], False, True, [b_scm[slot], b_v[slot]], [bps[6]], True)
                    for dc in range(2):
                        pbs = P_SUB[dc]
                        MM(psum[pbs][:, :], kl_ap[:, dc * 128:(dc + 1) * 128], v_s[:T, slot, :], True, True, [b_kl, b_v[slot]], [bps[pbs]], True)
                        STT(Sfj[:, dc, :], Sfj[:, dc, :], ebl[:, slot, dc * nseg + j:dc * nseg + j + 1], psum[pbs][:, :], ALU.mult, ALU.add,
                            [bps[pbs], b_ebl[slot], b_Sfj], [b_Sfj])
                        if full and not sample:
                            S.op("dve", lambda e, dc=dc: e.tensor_copy(out=Sb[:, dc, :], in_=Sf[:, h, dc, :]), [b_Sf[h]], [b_Sb[dc]])
                    if sample:
                        S.dma("sp", ssamp_o[j, h].rearrange("(c p) e -> p c e", p=128), Sfj[:], reads=[b_Sfj])
                        if j + 2 < 4:
                            S.dma("sp", sSf[sj][:], st_in[j + 2, h].rearrange("(c p) e -> p c e", p=128), writes=[b_sSf[sj]])
                if full:
                    ACT(sqj[:T, :], psum[6][:T, :], AF.Square, [bps[6]], [b_sqj, b_sso[si]], accum_out=sso[si][:T, 0:1])
                    ACT(sso[si][:T, 1:2], sso[si][:T, 0:1], AF.Ln, [b_sso[si]], [b_sso[si]], scale=1.0 / 512, bias=EPS)
                    ACT(sso[si][:T, 1:2], sso[si][:T, 1:2], AF.Exp, [b_sso[si]], [b_sso[si]], scale=-0.5)
                    STT(v_s[:T, slot, :], psum[6][:T, :], sso[si][:T, 1:2], gon[:T, :], ALU.mult, ALU.mult, [bps[6], b_sso[si], b_gon], [b_v[slot]])

            if any(sm for _, _, sm in slots):
                for j in range(2):
                    S.dma("sp", sSf[j][:], st_in[j, h].rearrange("(c p) e -> p c e", p=128), writes=[b_sSf[j]])
            for i in range(n):
                ST(i)
                if full and i >= 1:
                    RP_mm(i - 1)
                if full and i >= 2:
                    RP_tr(i - 2)
            if full:
                RP_mm(n - 1)
                if last:
                    RP_tr(n - 2)
                    RP_tr(n - 1)
                    return []
                return [lambda: RP_tr(n - 2), lambda: RP_tr(n - 1)]
            return []

        try:
            MEMSET(ss0[:], 0.0, [b_rs0]); MEMSET(ss1[:], 0.0, [b_rs1]); MEMSET(ss2[:], 0.0, [b_rs2]); MEMSET(ss3[:], 0.0, [b_rs3])
            MEMSET(Sf[:], 0.0, b_Sf)
            for i_ in range(2):
                MEMSET(sso[i_][:], 0.0, [b_sso[i_]])
            pre = [(p, xw[p * 128:(p + 1) * 128, :], 128) for p in range(7)]
            load_transpose(pre, rs0, ss0, b_rs0)
            for p in range(7):
                glow_for(p, 128)
            pslots = [(p, 128, False) for p in range(7)]
            for h in range(4):
                if h == 3:
                    early = [(i, xrows(i), slotT(i)) for i in (7, 8, 9)]
                    load_transpose(early, rs1, ss1, b_rs1)
                    for i, _, T_ in early:
                        glow_for(i, T_)
                def main_tile_early(i):
                    load_transpose([(i, xrows(i), 128)], rs1, ss1, b_rs1, idx0=i + 1)
                    glow_for(i, 128)
                head_passes(h, pslots, rs0, b_rs0, False,
                            [(w_in_a[:, 1024 + 256 * h:1024 + 256 * (h + 1)], 256)],
                            (w_in_a[:, 2048 + 512 * h:2048 + 512 * (h + 1)], 0), None,
                            post_pv=main_tile_early if h == 3 else None)

            stop_check('p0')
            barrier()
            mslots = [(i, slotT(i)) for i in range(10)]
            mslots3 = [(i, slotT(i), i == 9) for i in range(10)]
            tail_ = []
            for h in range(4):
                S.dma("sp", gon[:], gon_d[:, h * 512:(h + 1) * 512], writes=[b_gon])
                for dc_ in range(2):
                    S.op("dve", lambda e, dc_=dc_, h=h: e.tensor_copy(out=Sb[:, dc_, :], in_=Sf[:, h, dc_, :]), [b_Sf[h]], [b_Sb[dc_]])
                tail_ = head_passes(h, mslots3, rs1, b_rs1, True,
                                    [(w_in_a[:, 256 * h:256 * (h + 1)], 0), (w_in_a[:, 1024 + 256 * h:1024 + 256 * (h + 1)], 256)],
                                    (w_in_a[:, 2048 + 512 * h:2048 + 512 * (h + 1)], 0),
                                    (w_in_a[:, 4096 + 512 * h:4096 + 512 * (h + 1)], 0),
                                    carry=tail_, last=(h == 3))
                S.dma("sp", sfin_o[h].rearrange("(c p) e -> p c e", p=128), Sf[:, h], reads=[b_Sf[h]])
            barrier()

            stop_check('p1')
            HB0 = ARENA - 33792
            KV0 = HB0 - 21 * KB
            hbT = A.at(HB0, [128, 16, 1056], BF16); b_hbT = [Buf(f"hbT{i}") for i in range(9)]
            W3 = KV0
            hkvT = A.at(R1, [128, 16, 1184], BF16); b_hkvT = [Buf(f"hkvT{i}") for i in range(10)]
            A.setrange(R3, W3)
            NP = 3
            xpc = [A.alloc([128, 512], F32) for _ in range(NP)]; b_xpc = [Buf(f"xpc{i}", f"xpc{i}") for i in range(NP)]
            h1p = [A.alloc([128, 512], F32) for _ in range(NP)]; b_h1p = [Buf(f"h1p{i}", f"h1p{i}") for i in range(NP)]
            h1b = [A.alloc([128, 512], BF16) for _ in range(2)]; b_h1b = [Buf(f"h1b{i}") for i in range(2)]
            junk2 = A.alloc([128, 512], BF16); b_junk2 = Buf("junk2")
            b_h1s = [Buf(f"h1s{i}") for i in range(9)]
            cnt = 0
            wnext = WLOAD([(w_out_a[:, 0:512], 0)])
            for blk in range(4):
                wt, bw = wnext
                wnext = WLOAD([(w_out_a[:, 512 * (blk + 1):512 * (blk + 2)], 0)]) if blk < 3 else WLOAD([(w_kv, 0)])

                def o_evac2(slot, T, bi, blk=blk):
                    pb = tr_bank()
                    pv = psb[pb][:, 0:4 * T].rearrange("p (a b) -> p a b", b=T)
                    for jj in range(4):
                        TR(pv[:, jj, :], h1b[bi][:T, jj * 128:(jj + 1) * 128], [b_h1b[bi]], [bps[pb]], inc=(jj == 3))
                    c0 = slot * 128
                    gk = gvec[:, 1, 4 * blk:4 * blk + 4].unsqueeze(2).broadcast_to([128, 4, T])
                    TT(hkvT[:, 4 * blk:4 * blk + 4, c0:c0 + T], pv, gk, ALU.mult, [bps[pb], b_const], [b_hkvT[slot]])
                    if slot >= 1:
                        c1 = (slot - 1) * 128
                        gb2 = gvec[:, 2, 4 * blk:4 * blk + 4].unsqueeze(2).broadcast_to([128, 4, T])
                        TT(hbT[:, 4 * blk:4 * blk + 4, c1:c1 + T], pv, gb2, ALU.mult, [bps[pb], b_const], [b_hbT[slot - 1]])

                def o_evac(slot, T, ps_ap, bp, blk=blk):
                    nonlocal cnt
                    pi = cnt % NP
                    bi = cnt % 2
                    cnt += 1
                    cs = slice(512 * blk, 512 * (blk + 1))
                    S.dma("sp", xpc[pi][:T, :], xrows(slot)[:, cs], writes=[b_xpc[pi]])
                    TT(h1p[pi][:T, :], ps_ap, xpc[pi][:T, :], ALU.add, [bp, b_xpc[pi]], [b_h1p[pi]])
                    if slot >= 1:
                        r0 = (slot - 1) * 128
                        S.dma("pool", h1s[r0:r0 + T, cs], h1p[pi][:T, :], reads=[b_h1p[pi]], writes=[b_h1s[slot - 1]])
                    ACT(junk2[:T, :], h1p[pi][:T, :], AF.Square, [b_h1p[pi]], [b_junk2, b_rs2], accum_out=ss2[:T, slot, blk:blk + 1])
                    CP(h1b[bi][:T, :], h1p[pi][:T, :], [b_h1p[pi]], [b_h1b[bi]], eng="act")
                    if pend2:
                        o_evac2(*pend2.pop())
                    pend2.append((slot, T, bi))
                pend2 = []
                proj(wt, bw, 0, 512, mslots, ogT, b_ogT, o_evac)
                o_evac2(*pend2.pop())
            S.op("dve", lambda e: e.reduce_sum(out=ss1[:, 0:10], in_=ss2[:, :, :], axis=AX.X), [b_rs2], [b_rs1])
            RSTD(rs2[:, 0:10], ss1[:, 0:10], [b_rs1], [b_rs2])
            TS(hrs2[:, 0:10], rs2[:, 0:10], 0.5, ALU.mult, [b_rs2], [b_rs2])
            barrier()

            stop_check('p2')
            kT = A.at(KV0, [128, 4, 1184], BF16); b_kT = [Buf(f"kT{i}") for i in range(10)]
            o1 = KV0 + 4 * 1184 * 2
            vaug = A.at(o1, [128, 10, 4, 65], BF16); b_va = [Buf(f"va{i}") for i in range(10)]
            o2 = (o1 + 10 * 4 * 65 * 2 + 63) // 64 * 64
            kTc = A.at(o2, [128, 4, 512], BF16); b_kTc = Buf("kTc")
            o3 = o2 + 4 * 512 * 2
            vcaug = A.at(o3, [128, 4, 4, 65], BF16); b_vca = Buf("vca")
            assert o3 + 4 * 4 * 65 * 2 <= HB0
            A.setrange(R2, W3)
            kvf = [A.alloc([128, 512], F32) for _ in range(2)]; b_kvf = [Buf(f"kvf{i}", f"kvf{i}") for i in range(2)]
            kdup = [A.alloc([128, 4, 2, 64], BF16) for _ in range(2)]; b_kdup = [Buf(f"kdup{i}") for i in range(2)]
            ckf = [A.alloc([128, 256], F32) for _ in range(2)]; b_ckf = [Buf(f"ckf{i}", f"ckf{i}") for i in range(2)]
            cvf = [A.alloc([128, 256], F32) for _ in range(2)]; b_cvf = [Buf(f"cvf{i}", f"cvf{i}") for i in range(2)]
            MEMSET(vaug[:], 1.0, b_va); MEMSET(vcaug[:], 1.0, [b_vca])
            assert A.lo <= R2 + 18432, A.lo
            bp_t = A.at(R2 + 18432, [128, 2, 1024], F32); bs_t = A.at(R2 + 18432 + 8192, [128, 5, 256], F32)
            b_bias = Buf("bias", "bias")
            S.dma("sp", bp_t[:], biasp_d[0], writes=[b_bias])
            S.dma("sp", bs_t[:], biass_d[0], writes=[b_bias])

            pendk = []

            def kforms2(T, ki, kT_out, b_kT_out):
                pb = tr_bank()
                pv = psb[pb][:, 0:4 * T].rearrange("p (a b) -> p a b", b=T)
                for g in range(4):
                    TR(pv[:, g, :], kdup[ki][:T, g].rearrange("p a b -> p (a b)"), [b_kdup[ki]], [bps[pb]], inc=(g == 3))
                CP(kT_out, pv, [bps[pb]], [b_kT_out])

            def kforms(kf_ap, vf_ap, T, R, ki, kT_out, b_kT_out, va_out, b_va_out):
                kv4 = kf_ap.rearrange("p (g d) -> p g d", d=64)
                for r in range(2):
                    S.op("dve", lambda e, r=r: e.tensor_copy(out=kdup[ki][:T, :, r, :], in_=kv4), R, [b_kdup[ki]])
                CP(va_out[:, :, 0:64], vf_ap.rearrange("p (g d) -> p g d", d=64), R, [b_va_out], eng="act")
                if pendk:
                    kforms2(*pendk.pop())
                pendk.append((T, ki, kT_out, b_kT_out))

            wt, bw = wnext

            def kv_evac(slot, T, ps_ap, bp):
                ki = slot % 2
                SCALE(kvf[ki][:T, :], ps_ap, rs2[:T, slot:slot + 1], [bp, b_rs2], [b_kvf[ki]])
                c0 = slot * 128
                kforms(kvf[ki][:T, 0:256], kvf[ki][:T, 256:512], T, [b_kvf[ki]], ki, kT[:, :, c0:c0 + T], b_kT[slot], vaug[:T, slot], b_va[slot])
                if slot == 8:
                    S.dma("sp", kwin_o, kvf[ki][:, 0:256], reads=[b_kvf[ki]])
                    S.dma("sp", vwin_o, kvf[ki][:, 256:512], reads=[b_kvf[ki]])
                if slot == 9:
                    for j in range(4):
                        S.dma("sp", kwins_o[j, 120:128, :], kvf[ki][8 * j:8 * j + 8, 0:256], reads=[b_kvf[ki]])
                        S.dma("sp", vwins_o[j, 120:128, :], kvf[ki][8 * j:8 * j + 8, 256:512], reads=[b_kvf[ki]])
            proj(wt, bw, 0, 512, mslots, hkvT, b_hkvT, kv_evac)
            b_dd = Buf("dd", "dd")
            S.dma("sp", kwins_o[:, 0:120, :], ck[:, 8:128, :], writes=[b_dd])
            S.dma("sp", vwins_o[:, 0:120, :], cv[:, 8:128, :], writes=[b_dd])
            for j in range(4):
                ci_ = j % 2
                S.dma("sp", ckf[ci_][:], ck[j], writes=[b_ckf[ci_]])
                S.dma("sp", cvf[ci_][:], cv[j], writes=[b_cvf[ci_]])
                kforms(ckf[ci_][:], cvf[ci_][:], 128, [b_ckf[ci_], b_cvf[ci_]], ci_, kTc[:, :, j * 128:(j + 1) * 128], b_kTc, vcaug[:, j], b_vca)
            kforms2(*pendk.pop())
            barrier()

            stop_check('p25')
            ozT = A.at(R1, [128, 16, 1056], BF16); b_ozT = [Buf(f"ozT{i}") for i in range(9)]
            A.setrange(R2, W3)
            q_s = A.alloc([128, 9, 512], BF16); b_q = [Buf(f"q{i}") for i in range(9)]
            oat = A.alloc([128, 9, 512], BF16); b_oat = [Buf(f"oat{i}") for i in range(9)]
            assert A.lo == R2 + 18432, A.lo
            _bp = A.alloc([128, 2, 1024], F32); _bs = A.alloc([128, 5, 256], F32)
            qT = [A.alloc([128, 2, 4, 128], BF16) for _ in range(2)]; b_qT = [Buf(f"qT{i}") for i in range(2)]
            scs = [A.alloc([128, 512], BF16) for _ in range(2)]; b_scs = [Buf(f"scs{i}") for i in range(2)]
            Ep = A.alloc([128, 2, 1024], BF16); Ep0 = A.alloc([128, 1024], BF16); Es = A.alloc([128, 5, 256], BF16); b_E = Buf("Etab")
            pT = [A.alloc([128, 4, 512], BF16) for _ in range(2)]; b_pT = [Buf(f"pT{i}") for i in range(2)]
            pTn = A.alloc([128, 2, 128], BF16); b_pTn = Buf("pTn")
            szt = [A.alloc([128, 512], F32) for _ in range(2)]; b_szt = [Buf(f"szt{i}") for i in range(2)]
            ozt = [A.alloc([128, 512], BF16) for _ in range(2)]; b_ozt = [Buf(f"ozt{i}") for i in range(2)]
            rden = [A.alloc([128, 8], F32) for _ in range(2)]; b_rden = [Buf(f"rden{i}") for i in range(2)]
            print("phase3 arena used", A.lo, "of", W3)
            P_SC = (4, 5); P_PV = (6, 7)
            for i_ in range(2):
                MEMSET(qT[i_][:], 0.0, [b_qT[i_]])
            hslots = [(i, 32 if i == 8 else 128) for i in range(9)]
            scn = {"n": 0}
            for g in range(4):
                if g > 0:
                    S.dma("sp", bp_t[:], biasp_d[g], writes=[b_bias])
                    S.dma("sp", bs_t[:], biass_d[g], writes=[b_bias])
                for half in range(2):
                    for j in range(4):
                        si = 8 * g + 2 * j + half
                        c = half * 512 + j * 128
                        TS(bp_t[:, :, c:c + 128], bp_t[:, :, c:c + 128], sink[:, si:si + 1], ALU.subtract, [b_bias, b_const], [b_bias])
                        c2 = half * 128 + j * 32
                        TS(bs_t[:, :, c2:c2 + 32], bs_t[:, :, c2:c2 + 32], sink[:, si:si + 1], ALU.subtract, [b_bias, b_const], [b_bias])
                ACT(Ep[:], bp_t[:], AF.Exp, [b_bias], [b_E])
                ACT(Ep0[:], bp_t[:, 0, :], AF.Exp, [b_bias, b_const], [b_E], bias=pmask[:, 0:1])
                ACT(Es[:], bs_t[:], AF.Exp, [b_bias], [b_E])
                wt, bw = WLOAD([(w_in_b[:, 512 * g:512 * (g + 1)], 0)])
                proj(wt, bw, 0, 512, hslots, hbT, b_hbT,
                     lambda slot, T, ps_ap, bp: SCALE(q_s[:T, slot, :], ps_ap, rs2[:T, slot + 1:slot + 2], [bp, b_rs2], [b_q[slot]], mul=0.125, eng="dve"))
                def S1(i):
                    slot, T = hslots[i]
                    qi = slot % 2
                    pb = tr_bank()
                    pv = psb[pb][:, 0:4 * T].rearrange("p (a b) -> p a b", b=T)
                    for jj in range(4):
                        TR(pv[:, jj, :], q_s[:T, slot, jj * 128:(jj + 1) * 128], [b_q[slot]], [bps[pb]], inc=(jj == 3))
                    CP(qT[qi][0:64, 0, :, :T], pv[0:64], [bps[pb]], [b_qT[qi]], eng="act")
                    CP(qT[qi][64:128, 1, :, :T], pv[64:128], [bps[pb]], [b_qT[qi]], eng="dve")

                def kvblocks(slot):
                    kvs = slot + 1
                    return [(kT[:, g, (kvs - 1) * 128:kvs * 128], b_kT[kvs - 1], vaug[:, kvs - 1, g, :], b_va[kvs - 1], (Ep0[:, :] if slot == 0 else Ep[:, 0, :])),
                            (kT[:, g, kvs * 128:(kvs + 1) * 128], b_kT[kvs], vaug[:, kvs, g, :], b_va[kvs], Ep[:, 1, :])]

                def S2(i, part=None):
                    slot, T = hslots[i]
                    qi = slot % 2
                    pti = slot % 2
                    if slot >= 8 and part == 1:
                        return
                    if slot < 8:
                        for bi_, (kt_ap, bkt, va_ap, bva, bias_ap) in enumerate(kvblocks(slot)):
                            if part is not None and bi_ != part:
                                continue
                            for half in range(2):
                                pbk = P_SC[half]
                                MM(psum[pbk][:, :], kt_ap[:, :], qT[qi][:, half, :, :].rearrange("p a b -> p (a b)"),
                                   True, True, [bkt, b_qT[qi]], [bps[pbk]], True)
                                si_ = scn["n"] % 2
                                scn["n"] += 1
                                ACT(scs[si_][:, :], psum[pbk][:, :], AF.Exp, [bps[pbk]], [b_scs[si_]])
                                TT(pT[pti][:, bi_ * 2 + half, :], scs[si_][:, :], bias_ap[:, half * 512:(half + 1) * 512], ALU.mult, [b_scs[si_], b_E], [b_pT[pti]])
                    else:
                        for half in range(2):
                            pbk = P_SC[half]
                            rhs_q = qT[qi][half * 64:(half + 1) * 64, half, :, :T]
                            for jb in range(4):
                                MM(psum[pbk][:, jb * 128:(jb + 1) * 128].rearrange("p (a b) -> p a b", b=T), kTc[half * 64:(half + 1) * 64, g, jb * 128:(jb + 1) * 128], rhs_q,
                                   True, True, [b_kTc, b_qT[qi]], [bps[pbk]], jb == 3)
                            si_ = scn["n"] % 2
                            scn["n"] += 1
                            bias_c = Es[:, 0:4, half * 128:(half + 1) * 128]
                            ACT(scs[si_][:, :], psum[pbk][:, :], AF.Exp, [bps[pbk]], [b_scs[si_]])
                            TT(pT[pti][:, half, :].rearrange("p (a b) -> p a b", b=128), scs[si_][:, :].rearrange("p (a b) -> p a b", b=128), bias_c, ALU.mult,
                               [b_scs[si_], b_E], [b_pT[pti]])
                        c9 = 9 * 128
                        for half in range(2):
                            pbn = P_TR[half]
                            MM(psum[pbn][:T, 0:128].rearrange("p (a b) -> p a b", b=T), kT[half * 64:(half + 1) * 64, g, c9:c9 + T],
                               qT[qi][half * 64:(half + 1) * 64, half, :, :T], True, True, [b_kT[9], b_qT[qi]], [bps[pbn]], True)
                            si_ = scn["n"] % 2
                            scn["n"] += 1
                            ACT(scs[si_][:T, 0:128], psum[pbn][:T, 0:128], AF.Exp, [bps[pbn]], [b_scs[si_]])
                            TT(pTn[:T, half, :], scs[si_][:T, 0:128], Es[:T, 4, half * 128:(half + 1) * 128], ALU.mult, [b_scs[si_], b_E], [b_pTn])

                def S3(i):
                    slot, T = hslots[i]
                    qi = slot % 2
                    pti = slot % 2
                    if slot < 8:
                        blocks = kvblocks(slot)
                        for half in range(2):
                            for j in range(4):
                                for bi_, (kt_ap, bkt, va_ap, bva, bias_ap) in enumerate(blocks):
                                    MM(psum[P_PV[half]][:T, j * 65:(j + 1) * 65], pT[pti][:, bi_ * 2 + half, j * 128:(j + 1) * 128], va_ap,
                                       bi_ == 0, bi_ == 1, [b_pT[pti], bva], [bps[P_PV[half]]], (bi_ == 1 and j == 3))
                    else:
                        for half in range(2):
                            for j in range(4):
                                for jb in range(4):
                                    MM(psum[P_PV[half]][:T, j * 65:(j + 1) * 65], pT[pti][:, half, jb * 128 + j * T:jb * 128 + (j + 1) * T], vcaug[:, jb, g, :],
                                       jb == 0, False, [b_pT[pti], b_vca], [bps[P_PV[half]]], False)
                                MM(psum[P_PV[half]][:T, j * 65:(j + 1) * 65], pTn[:T, half, j * T:(j + 1) * T], vaug[:T, 9, g, :],
                                   False, True, [b_pTn, b_va[9]], [bps[P_PV[half]]], j == 3)
                    for half in range(2):
                        pvv = psum[P_PV[half]][:T, 0:260].rearrange("p (a b) -> p a b", b=65)
                        TS(rden[qi][:T, half * 4:(half + 1) * 4], pvv[:, :, 64], 1.0, ALU.add, [bps[P_PV[half]]], [b_rden[qi]])
                        S.op("dve", lambda e, half=half, qi=qi, T=T: e.reciprocal(out=rden[qi][:T, half * 4:(half + 1) * 4], in_=rden[qi][:T, half * 4:(half + 1) * 4]),
                             [b_rden[qi]], [b_rden[qi]])
                        ov = oat[:T, slot, :].rearrange("p (j h d) -> p j h d", h=2, d=64)[:, :, half, :]
                        rb = rden[qi][:T, half * 4:(half + 1) * 4].unsqueeze(2).broadcast_to([T, 4, 64])
                        TT(ov, pvv[:, :, 0:64], rb, ALU.mult, [bps[P_PV[half]], b_rden[qi]], [b_oat[slot]])

                wtz, bwz = WLOAD([(w_in_b[:, 2048 + 512 * g:2048 + 512 * (g + 1)], 0)])

                def z_evac2(slot, T, zi, g=g):
                    pb = tr_bank()
                    pv = psb[pb][:, 0:4 * T].rearrange("p (a b) -> p a b", b=T)
                    for jj in range(4):
                        TR(pv[:, jj, :], ozt[zi][:T, jj * 128:(jj + 1) * 128], [b_ozt[zi]], [bps[pb]], inc=(jj == 3))
                    c0 = slot * 128
                    CP(ozT[:, 4 * g:4 * g + 4, c0:c0 + T], pv, [bps[pb]], [b_ozT[slot]])

                def z_evac(slot, T, ps_ap, bp, g=g):
                    zi = slot % 2
                    ACT(szt[zi][:T, :], ps_ap, AF.Tanh, [bp, b_rs2], [b_szt[zi]], scale=hrs2[:T, slot + 1:slot + 2])
                    STT(szt[zi][:T, :], szt[zi][:T, :], 1.0, oat[:T, slot, :], ALU.add, ALU.mult, [b_szt[zi], b_oat[slot]], [b_szt[zi]])
                    STT(ozt[zi][:T, :], ps_ap, hrs2[:T, slot + 1:slot + 2], szt[zi][:T, :], ALU.mult, ALU.mult, [bp, b_rs2, b_szt[zi]], [b_ozt[zi]])
                    if pendz:
                        z_evac2(*pendz.pop())
                    pendz.append((slot, T, zi))
                pendz = []

                nh = len(hslots)
                S1(0)
                S1(1)
                S2(0)
                for i in range(nh):
                    if i + 2 < nh:
                        S1(i + 2)
                    if i + 1 < nh:
                        S2(i + 1, 0)
                    S3(i)
                    if i + 1 < nh:
                        S2(i + 1, 1)
                    proj(wtz, bwz, 0, 512, [hslots[i]], hbT, b_hbT, z_evac)
                z_evac2(*pendz.pop())
            barrier()

            stop_check('p3')
            A.setrange(R2, ARENA)
            h2 = A.alloc([128, 9, D], F32); b_h2 = [Buf(f"h2{i}", f"h2") for i in range(9)]
            gfin = A.alloc([128, D], F32); b_gfin = Buf("gfin", "gfin")
            hpc = [A.alloc([128, 512], F32) for _ in range(NP)]; b_hpc = [Buf(f"hpc{i}", f"hpc{i}") for i in range(NP)]
            junk3 = A.alloc([128, 512], BF16); b_junk3 = Buf("junk3")
            S.dma("sp", gfin[:], gfin_d, writes=[b_gfin])
            cnt4 = {"n": 0}
            b_rs3s = [Buf(f"rs3s{i}") for i in range(9)]
            for i_ in range(9):
                b_rs3s[i_].w = b_rs3.w
            for blk in range(4):
                wt, bw = WLOAD([(w_out_b[:, 512 * blk:512 * (blk + 1)], 0)])

                def f_evac(slot, T, ps_ap, bp, blk=blk):
                    pi = cnt4["n"] % NP
                    cnt4["n"] += 1
                    cs = slice(512 * blk, 512 * (blk + 1))
                    r0 = slot * 128
                    S.dma("sp", hpc[pi][:T, :], h1s[r0:r0 + T, cs], reads=[b_h1s[slot]], writes=[b_hpc[pi]])
                    TT(h2[:T, slot, cs], ps_ap, hpc[pi][:T, :], ALU.add, [bp, b_hpc[pi]], [b_h2[slot]])
                    ACT(junk3[:T, :], h2[:T, slot, cs], AF.Square, [b_h2[slot]], [b_junk3, b_rs3s[slot]], accum_out=ss3[:T, slot, blk:blk + 1])
                    if blk == 3:
                        S.op("dve", lambda e, slot=slot, T=T: e.reduce_sum(out=rs3[:T, slot:slot + 1], in_=ss3[:T, slot, :], axis=AX.X), [b_rs3s[slot]], [b_rs3s[slot]])
                        RSTD(rs3[:T, slot:slot + 1], rs3[:T, slot:slot + 1], [b_rs3s[slot]], [b_rs3s[slot]])
                        STT(h2[:T, slot, :], h2[:T, slot, :], rs3[:T, slot:slot + 1], gfin[:T, :], ALU.mult, ALU.mult, [b_h2[slot], b_rs3s[slot], b_gfin], [b_h2[slot]])
                        if slot < 8:
                            S.dma("pool", y_o[slot * 128:(slot + 1) * 128, :], h2[:, slot, :], reads=[b_h2[slot]])
                        else:
                            S.dma("pool", ys_o, h2[:32, slot, :], reads=[b_h2[slot]])
                proj(wt, bw, 0, 512, hslots, ozT, b_ozT, f_evac)
        except _Stop:
            pass
        S.finish()
        print("instr counts", {k: len(v) for k, v in S.streams.items()}, "waits", S.nwaits, "sem counts", {k: v for k, v in S.cnt.items() if k.startswith("e_")})
        S.emit()
    return nc


_CACHE = {}


def _consts():
    import ml_dtypes
    if "c" in _CACHE:
        return _CACHE["c"]
    c = {}
    c["ident"] = np.eye(128, dtype=np.float32).astype(ml_dtypes.bfloat16)
    fm = np.zeros((128, 32), np.float32)
    for p_ in range(128):
        fm[p_, p_ % 32] = 1.0
    c["fold"] = fm.astype(ml_dtypes.bfloat16)
    s = np.arange(128)[:, None]
    t = np.arange(128)[None, :]
    glac = np.zeros((128, 2, 3, 128), np.float32)
    glac[:, 0, 0, :] = np.where(s <= t, -1.0 / 16, 0.0)
    glac[:, 0, 1, :] = np.where(s > t, -1.0 / 16, 0.0)
    glac[:, 0, 2, :] = np.where(s <= t, 1.0, 0.0)
    same = (s // 8 == t // 8) & (s < 32) & (t < 32)
    glac[:, 1, 0, :] = np.where(same & (s <= t), -1.0 / 16, 0.0)
    glac[:, 1, 1, :] = np.where(same & (s > t), -1.0 / 16, 0.0)
    glac[:, 1, 2, :] = np.where(same & (s <= t), 1.0, 0.0)
    c["glac"] = glac
    segi = np.zeros((128, 2, 4), np.float32)
    segi[:, 0, 0] = -1.0 / 16
    segm = np.zeros((128, 4), np.float32)
    for r in range(32):
        segi[r, 1, r // 8] = -1.0 / 16
        segm[r, r // 8] = 1.0
    c["segi"] = segi
    c["segm"] = segm
    slopes = np.exp2(-8.0 * np.arange(1, 33, dtype=np.float64) / 32.0)
    bp = np.full((4, 128, 2, 1024), NEG, np.float64)
    bs = np.full((4, 128, 5, 256), NEG, np.float64)
    sv = np.arange(128)[:, None]
    tv = np.arange(128)[None, :]
    t32 = np.arange(32)[None, :]
    samp = t32 // 8
    ii = t32 % 8
    for g in range(4):
        for half in range(2):
            for j in range(4):
                sl = slopes[8 * g + 2 * j + half]
                cc = half * 512 + j * 128
                dist = tv + 128 - sv
                bp[g, :, 0, cc:cc + 128] = np.where(sv >= tv, -sl * dist, NEG)
                dist = tv - sv
                bp[g, :, 1, cc:cc + 128] = np.where(dist >= 0, -sl * dist, NEG)
                c2 = half * 128 + j * 32
                for jb in range(4):
                    ok = (samp == jb) & (sv >= ii)
                    bs[g, :, jb, c2:c2 + 32] = np.where(ok, -sl * (128 + ii - sv), NEG)
                ss_ = sv // 8
                is_ = sv % 8
                ok = (ss_ == samp) & (is_ <= ii) & (sv < 32)
                bs[g, :, 4, c2:c2 + 32] = np.where(ok, -sl * (ii - is_), NEG)
    c["bias_p"] = bp.astype(np.float32)
    c["bias_s"] = bs.astype(np.float32)
    _CACHE["c"] = c
    return c


def kernel(x_prompt, x_sample, state_gla, cache_k_win, cache_v_win, g_norm_a, w_in_a, w_gate_up, b_gate,
           g_onorm_a, w_out_a, g_norm_kv, w_kv, g_norm_b, w_in_b, sinks, w_out_b, g_final):
    f = lambda a: np.ascontiguousarray(np.asarray(a, dtype=np.float32))
    x_prompt = f(x_prompt); x_sample = f(x_sample); state_gla = f(state_gla)
    cache_k_win = f(cache_k_win); cache_v_win = f(cache_v_win)
    w_in_a = f(w_in_a)[0]; w_out_a = f(w_out_a)[0]; w_kv = f(w_kv); w_in_b = f(w_in_b)[0]; w_out_b = f(w_out_b)[0]
    if "nc" not in _CACHE:
        _CACHE["nc"] = build_program()
    nc = _CACHE["nc"]
    c = _consts()
    pk = lambda v: np.ascontiguousarray(f(v).reshape(16, 128).T)
    gvec = np.ascontiguousarray(np.stack([pk(g_norm_a), pk(g_norm_kv), pk(g_norm_b)], axis=1))
    bc = lambda v, n: np.ascontiguousarray(np.broadcast_to(f(v).reshape(1, n), (128, n)))
    shared = dict(
        w_in_a=w_in_a, w_out_a=w_out_a, w_kv=w_kv, w_in_b=w_in_b, w_out_b=w_out_b,
        wgu=f(w_gate_up)[0], gon_bc=bc(g_onorm_a, 2048), gfin_bc=bc(g_final, 2048),
        bias_p=c["bias_p"], bias_s=c["bias_s"],
        cpack16=np.ascontiguousarray(np.concatenate([c["ident"], c["fold"]], axis=1)),
    )
    wglow_h = np.ascontiguousarray(w_in_a[:, 6144:6160].reshape(16, 128, 16).transpose(1, 0, 2)).reshape(128, 256)

    def cpack_for(pm):
        parts = [c["glac"].reshape(128, 768), c["segi"].reshape(128, 8), c["segm"], bc(b_gate, 1024), gvec.reshape(128, 48),
                 bc(sinks, 32), np.full((128, 1), pm, np.float32), wglow_h, np.zeros((128, 3), np.float32)]
        return np.ascontiguousarray(np.concatenate(parts, axis=1).astype(np.float32))
    in_maps = []
    for core in range(8):
        b, half = core // 2, core % 2
        if half == 0:
            xw = np.concatenate([np.zeros((1024, 2048), np.float32), x_prompt[b, :1024]], axis=0)
        else:
            xw = x_prompt[b]
        m = dict(shared)
        m["xw"] = np.ascontiguousarray(xw)
        m["xs"] = np.ascontiguousarray(x_sample[4 * core:4 * core + 4].reshape(32, 2048))
        m["st_in"] = np.ascontiguousarray(state_gla[0, 4 * core:4 * core + 4])
        m["ck"] = np.ascontiguousarray(cache_k_win[4 * core:4 * core + 4].reshape(4, 128, 256))
        m["cv"] = np.ascontiguousarray(cache_v_win[4 * core:4 * core + 4].reshape(4, 128, 256))
        m["cpack"] = cpack_for(NEG if half == 0 else 0.0)
        in_maps.append(m)
    if _CACHE.get('hook') is not None:
        return _CACHE['hook'](nc, in_maps)
    res = run_bass_kernel_spmd(nc, in_maps, core_ids=list(range(8)))
    R = res.results
    y_prompt = np.zeros((4, 2048, 2048), np.float32)
    y_sample = np.zeros((32, 8, 2048), np.float32)
    gla_prompt = np.zeros((1, 4, 4, 256, 512), np.float32)
    gla_sample = np.zeros((1, 32, 4, 256, 512), np.float32)
    kwp = np.zeros((4, 128, 4, 64), np.float32); vwp = np.zeros((4, 128, 4, 64), np.float32)
    kws = np.zeros((32, 128, 4, 64), np.float32); vws = np.zeros((32, 128, 4, 64), np.float32)
    for core in range(8):
        b, half = core // 2, core % 2
        r = R[core]
        y_prompt[b, half * 1024:(half + 1) * 1024] = r["y"]
        y_sample[4 * core:4 * core + 4] = np.asarray(r["ys"]).reshape(4, 8, 2048)
        gla_sample[0, 4 * core:4 * core + 4] = r["ssamp"]
        kws[4 * core:4 * core + 4] = np.asarray(r["kwin_s"]).reshape(4, 128, 4, 64)
        vws[4 * core:4 * core + 4] = np.asarray(r["vwin_s"]).reshape(4, 128, 4, 64)
        if half == 1:
            gla_prompt[0, b] = r["sfin"]
            kwp[b] = np.asarray(r["kwin"]).reshape(128, 4, 64)
            vwp[b] = np.asarray(r["vwin"]).reshape(128, 4, 64)
    return (y_prompt, y_sample, gla_prompt, gla_sample, kwp, vwp, kws, vws)
```

```python
import numpy as np
from contextlib import ExitStack
import concourse.bass as bass
import concourse.mybir as mybir
from concourse.bass_utils import run_bass_kernel_spmd

F32 = mybir.dt.float32
BF16 = mybir.dt.bfloat16
AF = mybir.ActivationFunctionType
ALU = mybir.AluOpType
AX = mybir.AxisListType


class Buf:
    __slots__ = ("name", "w", "r", "dsem", "excl")

    def __init__(self, name, dsem=None, excl=False):
        self.name = name
        self.w = None
        self.r = {}
        self.dsem = dsem
        self.excl = excl


class Sched:
    ENG = ("pe", "act", "dve", "pool", "sp")

    def __init__(self, nc, stack):
        self.nc = nc
        self.stack = stack
        self.sems = {}
        self.cnt = {}
        self.streams = {e: [] for e in self.ENG}
        self.waited = {e: {} for e in self.ENG}
        for e in self.ENG:
            self._sem("e_" + e)
        self.nwaits = 0

    def _sem(self, key):
        if key not in self.sems:
            self.sems[key] = self.stack.enter_context(self.nc.semaphore(key))
            self.cnt[key] = 0
        return key

    def _need(self, eng, tok, waits):
        key, val = tok
        if key == "e_pe" and eng == "pe":
            return
        if key == "e_sp" and eng == "sp":
            return
        if self.waited[eng].get(key, 0) >= val:
            return
        self.waited[eng][key] = val
        waits.append((key, val))

    def _deps(self, eng, reads, writes):
        waits = []
        for b in reads:
            if b.w is not None:
                self._need(eng, b.w, waits)
        for b in writes:
            if b.w is not None:
                self._need(eng, b.w, waits)
            for k, v in b.r.items():
                self._need(eng, (k, v), waits)
        self.nwaits += len(waits)
        return waits

    def _mark(self, tok, reads, writes):
        k, v = tok
        for b in reads:
            if b.r.get(k, 0) < v:
                b.r[k] = v
        for b in writes:
            b.w = tok
            b.r = {}

    def op(self, eng, fn, reads=(), writes=(), inc=True):
        ex = [b for b in reads if b.excl]
        if ex:
            writes = list(writes) + [b for b in ex if b not in writes]
            reads = [b for b in reads if not b.excl]
        waits = self._deps(eng, reads, writes)
        key = "e_" + eng
        if inc:
            self.cnt[key] += 1
            tok = (key, self.cnt[key])
        else:
            tok = (key, self.cnt[key] + 1)
        self._mark(tok, reads, writes)
        self.streams[eng].append((waits, fn, (key, 1) if inc else None))

    def dma(self, eng, out, in_, reads=(), writes=(), sem=None):
        waits = self._deps(eng, reads, writes)
        if sem is None:
            for b in list(writes) + list(reads):
                if b.dsem is not None:
                    sem = b.dsem
                    break
        assert sem is not None
        key = self._sem("d_" + sem)
        self.cnt[key] += 16
        tok = (key, self.cnt[key])
        self._mark(tok, reads, writes)
        self.streams[eng].append(
            (waits, lambda e, o=out, i=in_: e.dma_start(out=o, in_=i), (key, 16)))

    def finish(self, eng="sp"):
        waits = []
        for key, c in self.cnt.items():
            if key.startswith("d_") and c > 0:
                self._need(eng, (key, c), waits)
        self.streams[eng].append((waits, None, None))

    def emit(self):
        nc = self.nc
        streams = self.streams
        sems = self.sems

        def replay(name, e):
            for waits, fn, inc in streams[name]:
                for key, val in waits:
                    e.wait_ge(sems[key], val)
                if fn is None:
                    continue
                ins = fn(e)
                if inc is not None:
                    ins.then_inc(sems[inc[0]], inc[1])

        with nc.Block() as block:
            @block.tensor
            def _(e):
                replay("pe", e)

            @block.scalar
            def _(e):
                replay("act", e)

            @block.vector
            def _(e):
                replay("dve", e)

            @block.gpsimd
            def _(e):
                replay("pool", e)

            @block.sync
            def _(e):
                replay("sp", e)

D = 2048
NEG = -30000.0
EPS = 1e-6
KB = 1024


def _prod(xs):
    r = 1
    for v in xs:
        r *= v
    return r


class Arena:
    def __init__(self, nc, st, nbytes):
        self.n = nbytes
        self.t = st.enter_context(nc.sbuf_tensor("arena", [128, nbytes // 2], BF16))
        self.lo = 0
        self.hi = nbytes

    def at(self, off, shape, dt):
        esz = 4 if dt == F32 else 2
        n = _prod(shape[1:]) * esz
        assert off % 4 == 0 and off + n <= self.n, (off, n, self.n)
        a = self.t[:, off // 2:(off + n) // 2]
        if dt != BF16:
            a = a.bitcast(dt)
        if len(shape) == 3:
            a = a.rearrange("p (a b) -> p a b", b=shape[2])
        elif len(shape) == 4:
            a = a.rearrange("p (a b c) -> p a b c", b=shape[2], c=shape[3])
        elif len(shape) == 5:
            a = a.rearrange("p (a b c d) -> p a b c d", b=shape[2], c=shape[3], d=shape[4])
        return a

    def setrange(self, lo, hi):
        self.lo, self.hi = lo, hi

    def alloc(self, shape, dt):
        esz = 4 if dt == F32 else 2
        n = _prod(shape[1:]) * esz
        n = (n + 63) // 64 * 64
        assert self.lo + n <= self.hi, ("arena overflow", self.lo, n, self.hi)
        a = self.at(self.lo, shape, dt)
        self.lo += n
        return a


class _Stop(Exception):
    pass


def build_program():
    import os
    STOP = os.environ.get('MK_STOP', '')

    def stop_check(p):
        if STOP == p:
            raise _Stop()
    nc = bass.Bass("TRN2", target_bir_lowering=False)

    def din(name, shape, dt=F32):
        return nc.dram_tensor(name, list(shape), dt, kind="ExternalInput").ap()

    def dout(name, shape, dt=F32):
        return nc.dram_tensor(name, list(shape), dt, kind="ExternalOutput").ap()

    xw = din("xw", [2048, D]); xs = din("xs", [32, D])
    st_in = din("st_in", [4, 4, 256, 512])
    ck = din("ck", [4, 128, 256]); cv = din("cv", [4, 128, 256])
    w_in_a = din("w_in_a", [D, 6160]); w_out_a = din("w_out_a", [D, D])
    w_kv = din("w_kv", [D, 512]); w_in_b = din("w_in_b", [D, 4096]); w_out_b = din("w_out_b", [D, D])
    wgu_d = din("wgu", [16, 1024])
    gon_d = din("gon_bc", [128, D]); gfin_d = din("gfin_bc", [128, D])
    NCP = 768 + 8 + 4 + 1024 + 48 + 32 + 1 + 256 + 3
    cpack_d = din("cpack", [128, NCP]); cpack16_d = din("cpack16", [128, 160], BF16)
    biasp_d = din("bias_p", [4, 128, 2, 1024]); biass_d = din("bias_s", [4, 128, 5, 256])

    y_o = dout("y", [1024, D]); ys_o = dout("ys", [32, D])
    sfin_o = dout("sfin", [4, 256, 512]); ssamp_o = dout("ssamp", [4, 4, 256, 512])
    kwin_o = dout("kwin", [128, 256]); vwin_o = dout("vwin", [128, 256])
    kwins_o = dout("kwin_s", [4, 128, 256]); vwins_o = dout("vwin_s", [4, 128, 256])
    h1s = nc.dram_tensor("h1s", [1056, D], F32).ap()

    with ExitStack() as st:
        S = Sched(nc, st)
        sbt = lambda name, shape, dt: st.enter_context(nc.sbuf_tensor("sb_" + name, shape, dt))
        NW = 2
        wbuf = [sbt(f"wbuf{i}", [128, 16, 512], BF16) for i in range(NW)]
        bwbuf = [Buf(f"wbuf{i}", f"w{i}") for i in range(NW)]
        b_const = Buf("const", "cst")
        cpack = sbt("cpack", [128, NCP], F32); cpack16 = sbt("cpack16", [128, 160], BF16)
        ident = cpack16[:, 0:128]; foldm = cpack16[:, 128:160]
        sfold = [sbt(f"sfold{i}", [128, 512], BF16) for i in range(2)]; b_sfold = [Buf(f"sfold{i}") for i in range(2)]
        _o = 0
        glac = cpack[:, _o:_o + 768].rearrange("p (a b c) -> p a b c", a=2, b=3); _o += 768
        segi = cpack[:, _o:_o + 8].rearrange("p (a b) -> p a b", a=2); _o += 8
        segm = cpack[:, _o:_o + 4]; _o += 4
        bgate = cpack[:, _o:_o + 1024]; _o += 1024
        wgu = sbt("wgu", [16, 1024], BF16); b_wgu = Buf("wgu", "wgu")
        gvec = cpack[:, _o:_o + 48].rearrange("p (a b) -> p a b", a=3); _o += 48
        sink = cpack[:, _o:_o + 32]; _o += 32
        pmask = cpack[:, _o:_o + 1]; _o += 1
        wglow_f = cpack[:, _o:_o + 256].rearrange("p (a b) -> p a b", a=16); _o += 256
        wglow = sbt("wglow", [128, 16, 16], BF16); b_wglow = Buf("wglow")
        rs0 = sbt("rs0", [128, 8], F32); rs1 = sbt("rs1", [128, 10], F32)
        rs2 = sbt("rs2", [128, 10], F32); rs3 = sbt("rs3", [128, 9], F32)
        ss0 = sbt("ss0", [128, 8], F32); ss1 = sbt("ss1", [128, 10], F32)
        ss2 = sbt("ss2", [128, 10, 4], F32); ss3 = sbt("ss3", [128, 9, 4], F32)
        b_rs0 = Buf("rs0"); b_rs1 = Buf("rs1"); b_rs2 = Buf("rs2"); b_rs3 = Buf("rs3")
        small = sbt("small", [128, 64], F32); b_small = Buf("small")
        hrs2 = sbt("hrs2", [128, 10], F32)
        ARENA = nc.sbuf_bytes_remaining - 1024
        ARENA = ARENA // 64 * 64
        A = Arena(nc, st, ARENA)
        psum = [st.enter_context(nc.psum_tensor(f"ps{i}", [128, 512], F32)) for i in range(8)]
        psb = [p[:].bitcast(BF16) for p in psum]
        bps = [Buf(f"ps{i}", excl=True) for i in range(8)]

        S.dma("act", cpack[:], cpack_d, writes=[b_const])
        S.dma("sp", cpack16[:], cpack16_d, writes=[b_const])
        S.op("dve", lambda e: e.tensor_copy(out=wglow[:], in_=wglow_f[:]), reads=[b_const], writes=[b_wglow])
        S.dma("pool", wgu[:], wgu_d, writes=[b_wgu])

        rr = {"n": 0}

        def MM(out, lhsT, rhs, start, stop, R, W, inc, tp=None):
            if tp is None:
                S.op("pe", lambda e: e.matmul(out, lhsT=lhsT, rhs=rhs, start=start, stop=stop), R, W, inc)
            else:
                S.op("pe", lambda e: e.matmul(out, lhsT=lhsT, rhs=rhs, start=start, stop=stop, tile_position=tp), R, W, inc)

        foldn = {"n": 0}

        def PROJ16(pb, T, ncols, lhs_of_k, rhs_of_k, R):
            if T == 128:
                for k in range(16):
                    MM(psum[pb][:T, 0:ncols], lhs_of_k(k), rhs_of_k(k), k == 0, k == 15, R, [bps[pb]], inc=(k == 15))
                return
            groups = [[0, 4, 8, 12], [1, 5, 9, 13], [2, 6, 10, 14], [3, 7, 11, 15]]
            order = []
            for step in range(4):
                for j in range(4):
                    if step < len(groups[j]):
                        order.append((j, step))
            for n_, (j, step) in enumerate(order):
                k = groups[j][step]
                MM(psum[pb][32 * j:32 * j + 32, 0:ncols], lhs_of_k(k), rhs_of_k(k), step == 0, step == len(groups[j]) - 1, R, [bps[pb]],
                   inc=(n_ == len(order) - 1), tp=(0, 32 * j))
            fi = foldn["n"] % 2
            foldn["n"] += 1
            CP(sfold[fi][:, 0:ncols], psum[pb][:, 0:ncols], [bps[pb]], [b_sfold[fi]])
            MM(psum[pb][0:32, 0:ncols], foldm[:, :], sfold[fi][:, 0:ncols], True, True, [b_sfold[fi], b_const], [bps[pb]], True)

        def TR(out, in_, R, W, inc, idn=None):
            idn_ap = ident[:in_.shape[0], :in_.shape[0]]
            S.op("pe", lambda e: e.transpose(out=out, in_=in_, identity=idn_ap), list(R) + [b_const], W, inc)

        def ACT(out, in_, func, R, W, **kw):
            S.op("act", lambda e: e.activation(out=out, in_=in_, func=func, **kw), R, W)

        def TT(out, in0, in1, op, R, W, eng="dve"):
            S.op(eng, lambda e: e.tensor_tensor(out=out, in0=in0, in1=in1, op=op), R, W)

        def TS(out, in0, s1, op0, R, W, s2=None, op1=None, eng="dve", accum=None):
            kw = {}
            if op1 is not None:
                kw["op1"] = op1
            if accum is not None:
                kw["accum_out"] = accum
            S.op(eng, lambda e: e.tensor_scalar(out=out, in0=in0, scalar1=s1, scalar2=s2, op0=op0, **kw), R, W)

        def STT(out, in0, scalar, in1, op0, op1, R, W, accum=None):
            kw = {}
            if accum is not None:
                kw["accum_out"] = accum
            S.op("dve", lambda e: e.scalar_tensor_tensor(out=out, in0=in0, scalar=scalar, in1=in1, op0=op0, op1=op1, **kw), R, W)

        def CP(out, in_, R, W, eng=None):
            if eng is None:
                rr["n"] += 1
                eng = "act" if rr["n"] % 2 else "dve"
            if eng == "act":
                ACT(out, in_, AF.Copy, R, W)
            else:
                S.op(eng, lambda e: e.tensor_copy(out=out, in_=in_), R, W)

        def SCALE(out, in_, sc_ap, R, W, mul=None, eng=None):
            if eng is None:
                rr["n"] += 1
                eng = "act" if rr["n"] % 2 else "dve"
            if eng == "act" and mul is None:
                ACT(out, in_, AF.Copy, R, W, scale=sc_ap)
            else:
                if mul is None:
                    TS(out, in_, sc_ap, ALU.mult, R, W)
                else:
                    TS(out, in_, sc_ap, ALU.mult, R, W, s2=mul, op1=ALU.mult)

        def MEMSET(ap, val, W, eng="dve"):
            S.op(eng, lambda e: e.memset(ap, val), (), W)

        def barrier():
            engs = ("pe", "act", "dve", "sp")
            for e in engs:
                waits = []
                for k, c in S.cnt.items():
                    if c > 0 and k != "e_" + e and k != "e_sp" and k != "e_pool" and not k.startswith("d_w"):
                        S._need(e, (k, c), waits)
                S.streams[e].append((waits, None, None))

        def RSTD(rs_ap, ss_ap, R, W):
            ACT(rs_ap, ss_ap, AF.Ln, R, W, scale=1.0 / D, bias=EPS)
            ACT(rs_ap, rs_ap, AF.Exp, W, W, scale=-0.5)

        wstate = {"n": 0}
        b_early = Buf("early")
        wlist = [(wbuf[i], bwbuf[i]) for i in range(NW)]

        def WLOAD(parts):
            i = wstate["n"] % len(wlist)
            first = wstate["n"] == 0
            wstate["n"] += 1
            wb_, bwb_ = wlist[i]
            for ap, off in parts:
                nco = ap.shape[1]
                src = ap.rearrange("(k p) n -> p k n", p=128)
                for kh in range(2):
                    S.dma("pool", wb_[:, kh * 8:(kh + 1) * 8, off:off + nco], src[:, kh * 8:(kh + 1) * 8, :], reads=[b_const, b_early] if first else [b_const], writes=[bwb_])
            return wb_, bwb_

        def slotT(i):
            return 32 if i == 9 else 128

        def xrows(i):
            return xs if i == 9 else xw[(7 + i) * 128:(8 + i) * 128, :]

        pstate = {"n": 0}

        def proj_bank():
            pstate["n"] += 1
            return pstate["n"] % 2

        R1 = 0
        R2 = 37888
        R3 = 75776
        xT = A.at(R1, [128, 16, 1184], BF16); b_xT = [Buf(f"xT{i}") for i in range(10)]
        ogT = A.at(R2, [128, 16, 1184], BF16); b_ogT = [Buf(f"ogT{i}") for i in range(10)]
        A.setrange(R2, R3)
        xf = [A.alloc([128, D], F32) for _ in range(2)]; b_xf = [Buf(f"xf{i}", f"xf{i}") for i in range(2)]
        xb = A.alloc([128, D], BF16); b_xb = Buf("xb")
        junk = A.alloc([128, D], BF16); b_junk = Buf("junk")
        A.setrange(R3, ARENA)
        qkd = A.alloc([128, 10, 3, 256], BF16); b_qkd = [Buf(f"qkd{i}") for i in range(10)]
        qdT = A.alloc([128, 10, 2, 128], BF16); b_qdT = [Buf(f"qdT{i}") for i in range(10)]
        scm = A.alloc([128, 10, 128], BF16); b_scm = [Buf(f"scm{i}") for i in range(10)]
        v_s = A.alloc([128, 10, 512], BF16); b_v = [Buf(f"v{i}") for i in range(10)]
        glowT = A.alloc([128, 1184], BF16); b_glow = [Buf(f"glow{i}") for i in range(10)]
        Sf = A.alloc([128, 4, 2, 512], F32); b_Sf = [Buf(f"Sf{h}", "sfin") for h in range(4)]
        Sb = A.alloc([128, 2, 512], BF16); b_Sb = [Buf("Sb0"), Buf("Sb1")]
        sSf = [A.alloc([128, 2, 512], F32) for _ in range(2)]; b_sSf = [Buf(f"sSf{i}", f"sS{i}") for i in range(2)]
        sSb = [A.alloc([128, 2, 512], BF16)] * 2; b_sSb = [[Buf("sSb")] * 2] * 2
        gon = A.alloc([128, 512], F32); b_gon = Buf("gon", "gon")
        NG = 3
        nl = [A.alloc([128, 256], F32) for _ in range(NG)]; b_nl = [Buf(f"nl{i}") for i in range(NG)]
        eb = [A.alloc([128, 3, 256], F32) for _ in range(2)]; b_eb = [Buf(f"eb{i}") for i in range(2)]
        ebl = A.alloc([128, 10, 8], F32); b_ebl = [Buf(f"ebl{i}") for i in range(10)]
        kdT = [A.alloc([128, 2, 128], BF16) for _ in range(2)]; b_kdT = [Buf(f"kdT{i}") for i in range(2)]
        qTm = A.alloc([128, 4, 2, 32], BF16); b_qTm = Buf("qTm")
        klm = A.alloc([128, 4, 256], BF16); b_klm = Buf("klm")
        sso = [A.alloc([128, 2], F32) for _ in range(2)]; b_sso = [Buf(f"sso{i}") for i in range(2)]
        srt = [A.alloc([128, 512], BF16) for _ in range(2)]; b_srt = [Buf(f"srt{i}") for i in range(2)]
        ogt = [A.alloc([128, 512], BF16) for _ in range(2)]; b_ogt = [Buf(f"ogt{i}") for i in range(2)]
        sqj = A.alloc([128, 512], BF16); b_sqj = Buf("sqj")
        print("phase01 arena used", A.lo, "of", ARENA)

        P_PROJ = (0, 1); P_TR = (2, 3); P_G = 4; P_BC = 5; P_O = 6; P_SU = 7
        b_p4a = b_p4b = b_p4c = bps[4]
        trstate = {"n": 0}

        def tr_bank():
            trstate["n"] += 1
            return P_TR[trstate["n"] % 2]

        def load_transpose(slots, rs, ss, b_rs, idx0=0):
            n = len(slots)
            for idx, (slot, rows, T) in enumerate(slots):
                f = (idx0 + idx) % 2
                S.dma("sp", xf[f][:T, :], rows, reads=[b_const], writes=[b_xf[f]])
                if idx == 1 and b_early.w is None:
                    b_early.w = b_xf[f].w
                ACT(junk[:T, :], xf[f][:T, :], AF.Square, [b_xf[f]], [b_junk, b_rs], accum_out=ss[:T, slot:slot + 1])
                ACT(xb[:T, :], xf[f][:T, :], AF.Copy, [b_xf[f]], [b_xb])
                c0 = slot * 128
                for half in range(2):
                    pb = tr_bank()
                    pv = psb[pb][:, 0:8 * T].rearrange("p (a b) -> p a b", b=T)
                    for kk in range(8):
                        k = half * 8 + kk
                        TR(pv[:, kk, :], xb[:T, k * 128:(k + 1) * 128], [b_xb], [bps[pb]], inc=(kk == 7))
                    gb = gvec[:, 0, half * 8:(half + 1) * 8].unsqueeze(2).broadcast_to([128, 8, T])
                    TT(xT[:, half * 8:(half + 1) * 8, c0:c0 + T], pv, gb, ALU.mult, [bps[pb], b_const], [b_xT[slot]])
            RSTD(rs[:, 0:10 if rs is rs1 else 8], ss[:, 0:10 if ss is ss1 else 8], [b_rs], [b_rs])

        def glow_for(slot, T):
            c0 = slot * 128
            pb = P_G
            for k in range(16):
                MM(psum[pb][0:16, 0:T], wglow[:, k, :], xT[:, k, c0:c0 + T], k == 0, k == 15, [b_wglow, b_xT[slot]], [b_p4a], inc=(k == 15))
            CP(glowT[0:16, c0:c0 + T], psum[pb][0:16, 0:T], [b_p4a], [b_glow[slot]])

        def proj(wt, bw, wcols, ncols, slots, xTt, b_x, evac):
            for slot, T in slots:
                c0 = slot * 128
                pb = P_PROJ[proj_bank()]
                PROJ16(pb, T, ncols, lambda k: xTt[:, k, c0:c0 + T], lambda k: wt[:, k, wcols:wcols + ncols], [b_x[slot], bw])
                evac(slot, T, psum[pb][:T, 0:ncols], bps[pb])

        LN16 = -2.772588722239781
        P_SUB = (7, 5)

        def head_passes(h, slots, rs, b_rs, full, wq_parts, wv_part, wr_part, post_pv=None, carry=None, last=True):
            n = len(slots)

            def G1(i):
                slot, T, sample = slots[i]
                c0 = slot * 128
                gi = i % NG
                pb = 4 + (i % 2)
                MM(psum[pb][:T, 0:256], glowT[0:16, c0:c0 + T], wgu[:, h * 256:(h + 1) * 256], True, True, [b_glow[slot], b_wgu], [bps[pb]], True)
                STT(nl[gi][:T, :], psum[pb][:T, 0:256], rs[:T, slot:slot + 1], bgate[:T, h * 256:(h + 1) * 256], ALU.mult, ALU.add,
                    [bps[pb], b_rs, b_const], [b_nl[gi]])
                ACT(nl[gi][:T, :], nl[gi][:T, :], AF.Exp, [b_nl[gi]], [b_nl[gi]], scale=-1.0)
                ACT(nl[gi][:T, :], nl[gi][:T, :], AF.Ln, [b_nl[gi]], [b_nl[gi]], bias=1.0)

            def G2(i):
                slot, T, sample = slots[i]
                gi = i % NG
                ci = 1 if sample else 0
                nseg = 4 if sample else 1
                pb = 6 + (i % 2)
                pg = 4 + (i % 2)
                if full:
                    MM(psum[pb][:T, 0:256], glac[:T, ci, 0, :T], nl[gi][:T, :], True, True, [b_nl[gi], b_const], [bps[pb]], False)
                MM(psum[pb][:T, 256:512], glac[:T, ci, 1, :T], nl[gi][:T, :], True, True, [b_nl[gi], b_const], [bps[pb]], True)
                for dc in range(2):
                    MM(psum[pg][:, 256 + dc * nseg:256 + (dc + 1) * nseg], nl[gi][:T, dc * 128:(dc + 1) * 128], segi[:T, ci, 0:nseg], True, True,
                       [b_nl[gi], b_const], [bps[pg]], dc == 1)
                if full:
                    ACT(eb[i % 2][:T, 0, :], psum[pb][:T, 0:256], AF.Exp, [bps[pb]], [b_eb[i % 2]], bias=LN16)
                    ACT(eb[i % 2][:T, 1, :], psum[pb][:T, 0:256], AF.Exp, [bps[pb]], [b_eb[i % 2]], scale=-1.0)
                ACT(eb[i % 2][:T, 2, :], psum[pb][:T, 256:512], AF.Exp, [bps[pb]], [b_eb[i % 2]])
                ACT(ebl[:, slot, 0:2 * nseg], psum[pg][:, 256:256 + 2 * nseg], AF.Exp, [bps[pg]], [b_ebl[slot]])

            p1bank = {}

            def P1_mm(i):
                slot, T, sample = slots[i]
                c0 = slot * 128
                pb = P_PROJ[proj_bank()]
                p1bank[i] = pb
                wc, nco = (0, 512) if full else (256, 256)
                PROJ16(pb, T, nco, lambda k: xT[:, k, c0:c0 + T], lambda k: wt_qk[:, k, wc:wc + nco], [b_xT[slot], bw_qk])

            def P1_ev(i):
                slot, T, sample = slots[i]
                pb = p1bank[i]
                rsc = rs[:T, slot:slot + 1]
                if full:
                    STT(qkd[:T, slot, 0, :], psum[pb][:T, 0:256], rsc, eb[i % 2][:T, 0, :], ALU.mult, ALU.mult, [bps[pb], b_rs, b_eb[i % 2]], [b_qkd[slot]])
                    STT(qkd[:T, slot, 1, :], psum[pb][:T, 256:512], rsc, eb[i % 2][:T, 1, :], ALU.mult, ALU.mult, [bps[pb], b_rs, b_eb[i % 2]], [b_qkd[slot]])
                    STT(qkd[:T, slot, 2, :], psum[pb][:T, 256:512], rsc, eb[i % 2][:T, 2, :], ALU.mult, ALU.mult, [bps[pb], b_rs, b_eb[i % 2]], [b_qkd[slot]])
                else:
                    STT(qkd[:T, slot, 2, :], psum[pb][:T, 0:256], rsc, eb[i % 2][:T, 2, :], ALU.mult, ALU.mult, [bps[pb], b_rs, b_eb[i % 2]], [b_qkd[slot]])

            wt_qk, bw_qk = WLOAD(wq_parts)
            G1(0)
            if n > 1:
                G1(1)
            P1_mm(0)
            if carry:
                for fn_ in carry:
                    fn_()
            G2(0)
            for i in range(n):
                if i + 2 < n:
                    G1(i + 2)
                if i + 1 < n:
                    G2(i + 1)
                P1_ev(i)
                if i + 1 < n:
                    P1_mm(i + 1)

            wt_v, bw_v = WLOAD([wv_part])

            def TRQ(i):
                slot, T, sample = slots[i]
                ki = i % 2
                pb = tr_bank()
                pv = psb[pb][:, 0:4 * T].rearrange("p (a b) -> p a b", b=T)
                for a in range(2):
                    for dc in range(2):
                        TR(pv[:, a * 2 + dc, :], qkd[:T, slot, a, dc * 128:(dc + 1) * 128], [b_qkd[slot]], [bps[pb]], inc=(a == 1 and dc == 1))
                CP(qdT[:, slot, :, :T], pv[:, 0:2, :], [bps[pb]], [b_qdT[slot]], eng="act")
                CP(kdT[ki][:, :, :T], pv[:, 2:4, :], [bps[pb]], [b_kdT[ki]], eng="dve")

            def SC(i):
                slot, T, sample = slots[i]
                ki = i % 2
                ci = 1 if sample else 0
                pb = 4 + (i % 2)
                for dc in range(2):
                    MM(psum[pb][:T, 0:T], kdT[ki][:, dc, :T], qdT[:, slot, dc, :T], dc == 0, dc == 1, [b_kdT[ki], b_qdT[slot]], [bps[pb]], dc == 1)
                TT(scm[:T, slot, :T], psum[pb][:T, 0:T], glac[:T, ci, 2, :T], ALU.mult, [bps[pb], b_const], [b_scm[slot]])

            def PV(i):
                slot, T, sample = slots[i]
                c0 = slot * 128
                pb = P_PROJ[proj_bank()]
                PROJ16(pb, T, 512, lambda k: xT[:, k, c0:c0 + T], lambda k: wt_v[:, k, :], [b_xT[slot], bw_v])
                SCALE(v_s[:T, slot, :], psum[pb][:T, :], rs[:T, slot:slot + 1], [bps[pb], b_rs], [b_v[slot]])

            if full:
                TRQ(0)
            for i in range(n):
                if full and i + 1 < n:
                    TRQ(i + 1)
                PV(i)
                if full:
                    SC(i)
                if post_pv is not None:
                    post_pv(i)

            if full:
                wt_r, bw_r = WLOAD([wr_part])

            def RP_mm(i):
                slot, T, sample = slots[i]
                c0 = slot * 128
                ti = i % 2
                pb = P_PROJ[proj_bank()]
                PROJ16(pb, T, 512, lambda k: xT[:, k, c0:c0 + T], lambda k: wt_r[:, k, :], [b_xT[slot], bw_r])
                ACT(srt[ti][:T, :], psum[pb][:T, :], AF.Silu, [bps[pb], b_rs], [b_srt[ti]], scale=rs[:T, slot:slot + 1])
                TT(ogt[ti][:T, :], v_s[:T, slot, :], srt[ti][:T, :], ALU.mult, [b_v[slot], b_srt[ti]], [b_ogt[ti]])

            def RP_tr(i):
                slot, T, sample = slots[i]
                c0 = slot * 128
                ti = i % 2
                pbt = tr_bank()
                pv = psb[pbt][:, 0:4 * T].rearrange("p (a b) -> p a b", b=T)
                for jj in range(4):
                    TR(pv[:, jj, :], ogt[ti][:T, jj * 128:(jj + 1) * 128], [b_ogt[ti]], [bps[pbt]], inc=(jj == 3))
                CP(ogT[:, 4 * h:4 * h + 4, c0:c0 + T], pv, [bps[pbt]], [b_ogT[slot]])

            def ST(i):
                slot, T, sample = slots[i]
                nseg = 4 if sample else 1
                si = i % 2
                if sample:
                    if full:
                        MEMSET(qTm[:], 0.0, [b_qTm])
                        for j in range(4):
                            S.op("dve", lambda e, j=j, slot=slot: e.tensor_copy(out=qTm[:, j, :, 8 * j:8 * j + 8], in_=qdT[:, slot, :, 8 * j:8 * j + 8]), [b_qdT[slot]], [b_qTm])
                    for j in range(4):
                        TS(klm[:T, j, :], qkd[:T, slot, 2, :], segm[:T, j:j + 1], ALU.mult, [b_qkd[slot], b_const], [b_klm])
                for j in range(nseg):
                    if sample:
                        sj = j % 2
                        CP(sSb[sj][:], sSf[sj][:], [b_sSf[sj]], [b_sSb[sj][0]], eng="act")
                        Sfj, b_Sfj, Sbj, b_Sbj = sSf[sj], b_sSf[sj], sSb[sj], b_sSb[sj]
                        kl_ap = klm[:T, j, :]
                        b_kl = b_klm
                    else:
                        Sfj, b_Sfj, Sbj, b_Sbj = Sf[:, h], b_Sf[h], Sb, b_Sb
                        kl_ap = qkd[:T, slot, 2, :]
                        b_kl = b_qkd[slot]
                    if full:
                        for dc in range(2):
                            lh = qTm[:, j, dc, :T] if sample else qdT[:, slot, dc, :T]
                            MM(psum[6][:T, :], lh, Sbj[:, dc, :], (j == 0 and dc == 0), False, [b_qTm if sample else b_qdT[slot], b_Sbj[dc]], [bps[6]], False)
                        if j == nseg - 1:
                            MM(psum[6][:T, :], scm[:T, slot, :T], v_s[:T, slot, :], False, True, [b_scm[slot], b_v[slot]], [bps[6]], True)
                    for dc in range(2):
                        pbs = P_SUB[dc]
                        MM(psum[pbs][:, :], kl_ap[:, dc * 128:(dc + 1) * 128], v_s[:T, slot, :], True, True, [b_kl, b_v[slot]], [bps[pbs]], True)
                        STT(Sfj[:, dc, :], Sfj[:, dc, :], ebl[:, slot, dc * nseg + j:dc * nseg + j + 1], psum[pbs][:, :], ALU.mult, ALU.add,
                            [bps[pbs], b_ebl[slot], b_Sfj], [b_Sfj])
                        if full and not sample:
                            S.op("dve", lambda e, dc=dc: e.tensor_copy(out=Sb[:, dc, :], in_=Sf[:, h, dc, :]), [b_Sf[h]], [b_Sb[dc]])
                    if sample:
                        S.dma("sp", ssamp_o[j, h].rearrange("(c p) e -> p c e", p=128), Sfj[:], reads=[b_Sfj])
                        if j + 2 < 4:
                            S.dma("sp", sSf[sj][:], st_in[j + 2, h].rearrange("(c p) e -> p c e", p=128), writes=[b_sSf[sj]])
                if full:
                    ACT(sqj[:T, :], psum[6][:T, :], AF.Square, [bps[6]], [b_sqj, b_sso[si]], accum_out=sso[si][:T, 0:1])
                    ACT(sso[si][:T, 1:2], sso[si][:T, 0:1], AF.Ln, [b_sso[si]], [b_sso[si]], scale=1.0 / 512, bias=EPS)
                    ACT(sso[si][:T, 1:2], sso[si][:T, 1:2], AF.Exp, [b_sso[si]], [b_sso[si]], scale=-0.5)
                    STT(v_s[:T, slot, :], psum[6][:T, :], sso[si][:T, 1:2], gon[:T, :], ALU.mult, ALU.mult, [bps[6], b_sso[si], b_gon], [b_v[slot]])

            if any(sm for _, _, sm in slots):
                for j in range(2):
                    S.dma("sp", sSf[j][:], st_in[j, h].rearrange("(c p) e -> p c e", p=128), writes=[b_sSf[j]])
            for i in range(n):
                ST(i)
                if full and i >= 1:
                    RP_mm(i - 1)
                if full and i >= 2:
                    RP_tr(i - 2)
            if full:
                RP_mm(n - 1)
                if last:
                    RP_tr(n - 2)
                    RP_tr(n - 1)
                    return []
                return [lambda: RP_tr(n - 2), lambda: RP_tr(n - 1)]
            return []

        try:
            MEMSET(ss0[:], 0.0, [b_rs0]); MEMSET(ss1[:], 0.0, [b_rs1]); MEMSET(ss2[:], 0.0, [b_rs2]); MEMSET(ss3[:], 0.0, [b_rs3])
            MEMSET(Sf[:], 0.0, b_Sf)
            for i_ in range(2):
                MEMSET(sso[i_][:], 0.0, [b_sso[i_]])
            pre = [(p, xw[p * 128:(p + 1) * 128, :], 128) for p in range(7)]
            load_transpose(pre, rs0, ss0, b_rs0)
            for p in range(7):
                glow_for(p, 128)
            pslots = [(p, 128, False) for p in range(7)]
            for h in range(4):
                if h == 3:
                    early = [(i, xrows(i), slotT(i)) for i in (7, 8, 9)]
                    load_transpose(early, rs1, ss1, b_rs1)
                    for i, _, T_ in early:
                        glow_for(i, T_)
                def main_tile_early(i):
                    load_transpose([(i, xrows(i), 128)], rs1, ss1, b_rs1, idx0=i + 1)
                    glow_for(i, 128)
                head_passes(h, pslots, rs0, b_rs0, False,
                            [(w_in_a[:, 1024 + 256 * h:1024 + 256 * (h + 1)], 256)],
                            (w_in_a[:, 2048 + 512 * h:2048 + 512 * (h + 1)], 0), None,
                            post_pv=main_tile_early if h == 3 else None)

            stop_check('p0')
            barrier()
            mslots = [(i, slotT(i)) for i in range(10)]
            mslots3 = [(i, slotT(i), i == 9) for i in range(10)]
            tail_ = []
            for h in range(4):
                S.dma("sp", gon[:], gon_d[:, h * 512:(h + 1) * 512], writes=[b_gon])
                for dc_ in range(2):
                    S.op("dve", lambda e, dc_=dc_, h=h: e.tensor_copy(out=Sb[:, dc_, :], in_=Sf[:, h, dc_, :]), [b_Sf[h]], [b_Sb[dc_]])
                tail_ = head_passes(h, mslots3, rs1, b_rs1, True,
                                    [(w_in_a[:, 256 * h:256 * (h + 1)], 0), (w_in_a[:, 1024 + 256 * h:1024 + 256 * (h + 1)], 256)],
                                    (w_in_a[:, 2048 + 512 * h:2048 + 512 * (h + 1)], 0),
                                    (w_in_a[:, 4096 + 512 * h:4096 + 512 * (h + 1)], 0),
                                    carry=tail_, last=(h == 3))
                S.dma("sp", sfin_o[h].rearrange("(c p) e -> p c e", p=128), Sf[:, h], reads=[b_Sf[h]])
            barrier()

            stop_check('p1')
            HB0 = ARENA - 33792
            KV0 = HB0 - 21 * KB
            hbT = A.at(HB0, [128, 16, 1056], BF16); b_hbT = [Buf(f"hbT{i}") for i in range(9)]
            W3 = KV0
            hkvT = A.at(R1, [128, 16, 1184], BF16); b_hkvT = [Buf(f"hkvT{i}") for i in range(10)]
            A.setrange(R3, W3)
            NP = 3
            xpc = [A.alloc([128, 512], F32) for _ in range(NP)]; b_xpc = [Buf(f"xpc{i}", f"xpc{i}") for i in range(NP)]
            h1p = [A.alloc([128, 512], F32) for _ in range(NP)]; b_h1p = [Buf(f"h1p{i}", f"h1p{i}") for i in range(NP)]
            h1b = [A.alloc([128, 512], BF16) for _ in range(2)]; b_h1b = [Buf(f"h1b{i}") for i in range(2)]
            junk2 = A.alloc([128, 512], BF16); b_junk2 = Buf("junk2")
            b_h1s = [Buf(f"h1s{i}") for i in range(9)]
            cnt = 0
            wnext = WLOAD([(w_out_a[:, 0:512], 0)])
            pend2 = []
            for blk in range(4):
                wt, bw = wnext
                wnext = WLOAD([(w_out_a[:, 512 * (blk + 1):512 * (blk + 2)], 0)]) if blk < 3 else WLOAD([(w_kv, 0)])

                def o_evac2(slot, T, bi, blk=blk):
                    pb = tr_bank()
                    pv = psb[pb][:, 0:4 * T].rearrange("p (a b) -> p a b", b=T)
                    for jj in range(4):
                        TR(pv[:, jj, :], h1b[bi][:T, jj * 128:(jj + 1) * 128], [b_h1b[bi]], [bps[pb]], inc=(jj == 3))
                    c0 = slot * 128
                    gk = gvec[:, 1, 4 * blk:4 * blk + 4].unsqueeze(2).broadcast_to([128, 4, T])
                    TT(hkvT[:, 4 * blk:4 * blk + 4, c0:c0 + T], pv, gk, ALU.mult, [bps[pb], b_const], [b_hkvT[slot]])
                    if slot >= 1:
                        c1 = (slot - 1) * 128
                        gb2 = gvec[:, 2, 4 * blk:4 * blk + 4].unsqueeze(2).broadcast_to([128, 4, T])
                        TT(hbT[:, 4 * blk:4 * blk + 4, c1:c1 + T], pv, gb2, ALU.mult, [bps[pb], b_const], [b_hbT[slot - 1]])

                def o_evac(slot, T, ps_ap, bp, blk=blk):
                    nonlocal cnt
                    pi = cnt % NP
                    bi = cnt % 2
                    cnt += 1
                    cs = slice(512 * blk, 512 * (blk + 1))
                    S.dma("sp", xpc[pi][:T, :], xrows(slot)[:, cs], writes=[b_xpc[pi]])
                    TT(h1p[pi][:T, :], ps_ap, xpc[pi][:T, :], ALU.add, [bp, b_xpc[pi]], [b_h1p[pi]])
                    if slot >= 1:
                        r0 = (slot - 1) * 128
                        S.dma("pool", h1s[r0:r0 + T, cs], h1p[pi][:T, :], reads=[b_h1p[pi]], writes=[b_h1s[slot - 1]])
                    ACT(junk2[:T, :], h1p[pi][:T, :], AF.Square, [b_h1p[pi]], [b_junk2, b_rs2], accum_out=ss2[:T, slot, blk:blk + 1])
                    CP(h1b[bi][:T, :], h1p[pi][:T, :], [b_h1p[pi]], [b_h1b[bi]], eng="act")
                    if pend2:
                        pend2.pop()()
                    pend2.append(lambda slot=slot, T=T, bi=bi, f_=o_evac2: f_(slot, T, bi))
                proj(wt, bw, 0, 512, mslots, ogT, b_ogT, o_evac)
            pend2.pop()()
            S.op("dve", lambda e: e.reduce_sum(out=ss1[:, 0:10], in_=ss2[:, :, :], axis=AX.X), [b_rs2], [b_rs1])
            RSTD(rs2[:, 0:10], ss1[:, 0:10], [b_rs1], [b_rs2])
            TS(hrs2[:, 0:10], rs2[:, 0:10], 0.5, ALU.mult, [b_rs2], [b_rs2])
            barrier()

            stop_check('p2')
            kT = A.at(KV0, [128, 4, 1184], BF16); b_kT = [Buf(f"kT{i}") for i in range(10)]
            o1 = KV0 + 4 * 1184 * 2
            vaug = A.at(o1, [128, 10, 4, 65], BF16); b_va = [Buf(f"va{i}") for i in range(10)]
            o2 = (o1 + 10 * 4 * 65 * 2 + 63) // 64 * 64
            kTc = A.at(o2, [128, 4, 512], BF16); b_kTc = Buf("kTc")
            o3 = o2 + 4 * 512 * 2
            vcaug = A.at(o3, [128, 4, 4, 65], BF16); b_vca = Buf("vca")
            assert o3 + 4 * 4 * 65 * 2 <= HB0
            A.setrange(R2, W3)
            kvf = [A.alloc([128, 512], F32) for _ in range(2)]; b_kvf = [Buf(f"kvf{i}", f"kvf{i}") for i in range(2)]
            kdup = [A.alloc([128, 4, 2, 64], BF16) for _ in range(2)]; b_kdup = [Buf(f"kdup{i}") for i in range(2)]
            ckf = [A.alloc([128, 256], F32) for _ in range(2)]; b_ckf = [Buf(f"ckf{i}", f"ckf{i}") for i in range(2)]
            cvf = [A.alloc([128, 256], F32) for _ in range(2)]; b_cvf = [Buf(f"cvf{i}", f"cvf{i}") for i in range(2)]
            MEMSET(vaug[:], 1.0, b_va); MEMSET(vcaug[:], 1.0, [b_vca])
            assert A.lo <= R2 + 18432, A.lo
            bp_t = A.at(R2 + 18432, [128, 2, 1024], F32); bs_t = A.at(R2 + 18432 + 8192, [128, 5, 256], F32)
            b_bias = Buf("bias", "bias")
            S.dma("sp", bp_t[:], biasp_d[0], writes=[b_bias])
            S.dma("sp", bs_t[:], biass_d[0], writes=[b_bias])

            pendk = []

            def kforms2(T, ki, kT_out, b_kT_out):
                pb = tr_bank()
                pv = psb[pb][:, 0:4 * T].rearrange("p (a b) -> p a b", b=T)
                for g in range(4):
                    TR(pv[:, g, :], kdup[ki][:T, g].rearrange("p a b -> p (a b)"), [b_kdup[ki]], [bps[pb]], inc=(g == 3))
                CP(kT_out, pv, [bps[pb]], [b_kT_out])

            def kforms(kf_ap, vf_ap, T, R, ki, kT_out, b_kT_out, va_out, b_va_out):
                kv4 = kf_ap.rearrange("p (g d) -> p g d", d=64)
                for r in range(2):
                    S.op("dve", lambda e, r=r: e.tensor_copy(out=kdup[ki][:T, :, r, :], in_=kv4), R, [b_kdup[ki]])
                CP(va_out[:, :, 0:64], vf_ap.rearrange("p (g d) -> p g d", d=64), R, [b_va_out], eng="act")
                if pendk:
                    kforms2(*pendk.pop())
                pendk.append((T, ki, kT_out, b_kT_out))

            wt, bw = wnext

            def kv_evac(slot, T, ps_ap, bp):
                ki = slot % 2
                SCALE(kvf[ki][:T, :], ps_ap, rs2[:T, slot:slot + 1], [bp, b_rs2], [b_kvf[ki]])
                c0 = slot * 128
                kforms(kvf[ki][:T, 0:256], kvf[ki][:T, 256:512], T, [b_kvf[ki]], ki, kT[:, :, c0:c0 + T], b_kT[slot], vaug[:T, slot], b_va[slot])
                if slot == 8:
                    S.dma("sp", kwin_o, kvf[ki][:, 0:256], reads=[b_kvf[ki]])
                    S.dma("sp", vwin_o, kvf[ki][:, 256:512], reads=[b_kvf[ki]])
                if slot == 9:
                    for j in range(4):
                        S.dma("sp", kwins_o[j, 120:128, :], kvf[ki][8 * j:8 * j + 8, 0:256], reads=[b_kvf[ki]])
                        S.dma("sp", vwins_o[j, 120:128, :], kvf[ki][8 * j:8 * j + 8, 256:512], reads=[b_kvf[ki]])
            proj(wt, bw, 0, 512, mslots, hkvT, b_hkvT, kv_evac)
            b_dd = Buf("dd", "dd")
            S.dma("sp", kwins_o[:, 0:120, :], ck[:, 8:128, :], writes=[b_dd])
            S.dma("sp", vwins_o[:, 0:120, :], cv[:, 8:128, :], writes=[b_dd])
            for j in range(4):
                ci_ = j % 2
                S.dma("sp", ckf[ci_][:], ck[j], writes=[b_ckf[ci_]])
                S.dma("sp", cvf[ci_][:], cv[j], writes=[b_cvf[ci_]])
                kforms(ckf[ci_][:], cvf[ci_][:], 128, [b_ckf[ci_], b_cvf[ci_]], ci_, kTc[:, :, j * 128:(j + 1) * 128], b_kTc, vcaug[:, j], b_vca)
            kforms2(*pendk.pop())
            barrier()

            stop_check('p25')
            ozT = A.at(R1, [128, 16, 1056], BF16); b_ozT = [Buf(f"ozT{i}") for i in range(9)]
            A.setrange(R2, W3)
            q_s = A.alloc([128, 9, 512], BF16); b_q = [Buf(f"q{i}") for i in range(9)]
            oat = A.alloc([128, 9, 512], BF16); b_oat = [Buf(f"oat{i}") for i in range(9)]
            assert A.lo == R2 + 18432, A.lo
            _bp = A.alloc([128, 2, 1024], F32); _bs = A.alloc([128, 5, 256], F32)
            qT = [A.alloc([128, 2, 4, 128], BF16) for _ in range(2)]; b_qT = [Buf(f"qT{i}") for i in range(2)]
            scs = [A.alloc([128, 512], BF16) for _ in range(2)]; b_scs = [Buf(f"scs{i}") for i in range(2)]
            Ep = A.alloc([128, 2, 1024], BF16); Ep0 = A.alloc([128, 1024], BF16); Es = A.alloc([128, 5, 256], BF16); b_E = Buf("Etab")
            pT = [A.alloc([128, 4, 512], BF16) for _ in range(2)]; b_pT = [Buf(f"pT{i}") for i in range(2)]
            pTn = A.alloc([128, 2, 128], BF16); b_pTn = Buf("pTn")
            szt = [A.alloc([128, 512], F32) for _ in range(2)]; b_szt = [Buf(f"szt{i}") for i in range(2)]
            ozt = [A.alloc([128, 512], BF16) for _ in range(2)]; b_ozt = [Buf(f"ozt{i}") for i in range(2)]
            rden = [A.alloc([128, 8], F32) for _ in range(2)]; b_rden = [Buf(f"rden{i}") for i in range(2)]
            print("phase3 arena used", A.lo, "of", W3)
            P_SC = (4, 5); P_PV = (6, 7)
            for i_ in range(2):
                MEMSET(qT[i_][:], 0.0, [b_qT[i_]])
            hslots = [(i, 32 if i == 8 else 128) for i in range(9)]
            scn = {"n": 0}
            pendz = []
            zcnt = {"n": 0}
            for g in range(4):
                if g > 0:
                    S.dma("sp", bp_t[:], biasp_d[g], writes=[b_bias])
                    S.dma("sp", bs_t[:], biass_d[g], writes=[b_bias])
                for half in range(2):
                    for j in range(4):
                        si = 8 * g + 2 * j + half
                        c = half * 512 + j * 128
                        TS(bp_t[:, :, c:c + 128], bp_t[:, :, c:c + 128], sink[:, si:si + 1], ALU.subtract, [b_bias, b_const], [b_bias])
                        c2 = half * 128 + j * 32
                        TS(bs_t[:, :, c2:c2 + 32], bs_t[:, :, c2:c2 + 32], sink[:, si:si + 1], ALU.subtract, [b_bias, b_const], [b_bias])
                ACT(Ep[:], bp_t[:], AF.Exp, [b_bias], [b_E])
                ACT(Ep0[:], bp_t[:, 0, :], AF.Exp, [b_bias, b_const], [b_E], bias=pmask[:, 0:1])
                ACT(Es[:], bs_t[:], AF.Exp, [b_bias], [b_E])
                wt, bw = WLOAD([(w_in_b[:, 512 * g:512 * (g + 1)], 0)])
                proj(wt, bw, 0, 512, hslots, hbT, b_hbT,
                     lambda slot, T, ps_ap, bp: SCALE(q_s[:T, slot, :], ps_ap, rs2[:T, slot + 1:slot + 2], [bp, b_rs2], [b_q[slot]], mul=0.125, eng="dve"))
                def S1(i):
                    slot, T = hslots[i]
                    qi = slot % 2
                    pb = tr_bank()
                    pv = psb[pb][:, 0:4 * T].rearrange("p (a b) -> p a b", b=T)
                    for jj in range(4):
                        TR(pv[:, jj, :], q_s[:T, slot, jj * 128:(jj + 1) * 128], [b_q[slot]], [bps[pb]], inc=(jj == 3))
                    CP(qT[qi][0:64, 0, :, :T], pv[0:64], [bps[pb]], [b_qT[qi]], eng="act")
                    CP(qT[qi][64:128, 1, :, :T], pv[64:128], [bps[pb]], [b_qT[qi]], eng="dve")

                def kvblocks(slot):
                    kvs = slot + 1
                    return [(kT[:, g, (kvs - 1) * 128:kvs * 128], b_kT[kvs - 1], vaug[:, kvs - 1, g, :], b_va[kvs - 1], (Ep0[:, :] if slot == 0 else Ep[:, 0, :])),
                            (kT[:, g, kvs * 128:(kvs + 1) * 128], b_kT[kvs], vaug[:, kvs, g, :], b_va[kvs], Ep[:, 1, :])]

                def S2(i, part=None):
                    slot, T = hslots[i]
                    qi = slot % 2
                    pti = slot % 2
                    if slot >= 8 and part == 1:
                        return
                    if slot < 8:
                        for bi_, (kt_ap, bkt, va_ap, bva, bias_ap) in enumerate(kvblocks(slot)):
                            if part is not None and bi_ != part:
                                continue
                            for half in range(2):
                                pbk = P_SC[half]
                                MM(psum[pbk][:, :], kt_ap[:, :], qT[qi][:, half, :, :].rearrange("p a b -> p (a b)"),
                                   True, True, [bkt, b_qT[qi]], [bps[pbk]], True)
                                si_ = scn["n"] % 2
                                scn["n"] += 1
                                ACT(scs[si_][:, :], psum[pbk][:, :], AF.Exp, [bps[pbk]], [b_scs[si_]])
                                TT(pT[pti][:, bi_ * 2 + half, :], scs[si_][:, :], bias_ap[:, half * 512:(half + 1) * 512], ALU.mult, [b_scs[si_], b_E], [b_pT[pti]])
                    else:
                        for half in range(2):
                            pbk = P_SC[half]
                            rhs_q = qT[qi][half * 64:(half + 1) * 64, half, :, :T]
                            for jb in range(4):
                                MM(psum[pbk][:, jb * 128:(jb + 1) * 128].rearrange("p (a b) -> p a b", b=T), kTc[half * 64:(half + 1) * 64, g, jb * 128:(jb + 1) * 128], rhs_q,
                                   True, True, [b_kTc, b_qT[qi]], [bps[pbk]], jb == 3)
                            si_ = scn["n"] % 2
                            scn["n"] += 1
                            bias_c = Es[:, 0:4, half * 128:(half + 1) * 128]
                            ACT(scs[si_][:, :], psum[pbk][:, :], AF.Exp, [bps[pbk]], [b_scs[si_]])
                            TT(pT[pti][:, half, :].rearrange("p (a b) -> p a b", b=128), scs[si_][:, :].rearrange("p (a b) -> p a b", b=128), bias_c, ALU.mult,
                               [b_scs[si_], b_E], [b_pT[pti]])
                        c9 = 9 * 128
                        for half in range(2):
                            pbn = P_TR[half]
                            MM(psum[pbn][:T, 0:128].rearrange("p (a b) -> p a b", b=T), kT[half * 64:(half + 1) * 64, g, c9:c9 + T],
                               qT[qi][half * 64:(half + 1) * 64, half, :, :T], True, True, [b_kT[9], b_qT[qi]], [bps[pbn]], True)
                            si_ = scn["n"] % 2
                            scn["n"] += 1
                            ACT(scs[si_][:T, 0:128], psum[pbn][:T, 0:128], AF.Exp, [bps[pbn]], [b_scs[si_]])
                            TT(pTn[:T, half, :], scs[si_][:T, 0:128], Es[:T, 4, half * 128:(half + 1) * 128], ALU.mult, [b_scs[si_], b_E], [b_pTn])

                def S3(i):
                    slot, T = hslots[i]
                    qi = slot % 2
                    pti = slot % 2
                    if slot < 8:
                        blocks = kvblocks(slot)
                        for half in range(2):
                            for j in range(4):
                                for bi_, (kt_ap, bkt, va_ap, bva, bias_ap) in enumerate(blocks):
                                    MM(psum[P_PV[half]][:T, j * 65:(j + 1) * 65], pT[pti][:, bi_ * 2 + half, j * 128:(j + 1) * 128], va_ap,
                                       bi_ == 0, bi_ == 1, [b_pT[pti], bva], [bps[P_PV[half]]], (bi_ == 1 and j == 3))
                    else:
                        for half in range(2):
                            for j in range(4):
                                for jb in range(4):
                                    MM(psum[P_PV[half]][:T, j * 65:(j + 1) * 65], pT[pti][:, half, jb * 128 + j * T:jb * 128 + (j + 1) * T], vcaug[:, jb, g, :],
                                       jb == 0, False, [b_pT[pti], b_vca], [bps[P_PV[half]]], False)
                                MM(psum[P_PV[half]][:T, j * 65:(j + 1) * 65], pTn[:T, half, j * T:(j + 1) * T], vaug[:T, 9, g, :],
                                   False, True, [b_pTn, b_va[9]], [bps[P_PV[half]]], j == 3)
                    for half in range(2):
                        pvv = psum[P_PV[half]][:T, 0:260].rearrange("p (a b) -> p a b", b=65)
                        TS(rden[qi][:T, half * 4:(half + 1) * 4], pvv[:, :, 64], 1.0, ALU.add, [bps[P_PV[half]]], [b_rden[qi]])
                        S.op("dve", lambda e, half=half, qi=qi, T=T: e.reciprocal(out=rden[qi][:T, half * 4:(half + 1) * 4], in_=rden[qi][:T, half * 4:(half + 1) * 4]),
                             [b_rden[qi]], [b_rden[qi]])
                        ov = oat[:T, slot, :].rearrange("p (j h d) -> p j h d", h=2, d=64)[:, :, half, :]
                        rb = rden[qi][:T, half * 4:(half + 1) * 4].unsqueeze(2).broadcast_to([T, 4, 64])
                        TT(ov, pvv[:, :, 0:64], rb, ALU.mult, [bps[P_PV[half]], b_rden[qi]], [b_oat[slot]])

                wtz, bwz = WLOAD([(w_in_b[:, 2048 + 512 * g:2048 + 512 * (g + 1)], 0)])

                def z_evac2(slot, T, zi, g=g):
                    pb = tr_bank()
                    pv = psb[pb][:, 0:4 * T].rearrange("p (a b) -> p a b", b=T)
                    for jj in range(4):
                        TR(pv[:, jj, :], ozt[zi][:T, jj * 128:(jj + 1) * 128], [b_ozt[zi]], [bps[pb]], inc=(jj == 3))
                    c0 = slot * 128
                    CP(ozT[:, 4 * g:4 * g + 4, c0:c0 + T], pv, [bps[pb]], [b_ozT[slot]])

                def z_evac(slot, T, ps_ap, bp, g=g):
                    zi = zcnt["n"] % 2
                    zcnt["n"] += 1
                    ACT(szt[zi][:T, :], ps_ap, AF.Tanh, [bp, b_rs2], [b_szt[zi]], scale=hrs2[:T, slot + 1:slot + 2])
                    STT(szt[zi][:T, :], szt[zi][:T, :], 1.0, oat[:T, slot, :], ALU.add, ALU.mult, [b_szt[zi], b_oat[slot]], [b_szt[zi]])
                    STT(ozt[zi][:T, :], ps_ap, hrs2[:T, slot + 1:slot + 2], szt[zi][:T, :], ALU.mult, ALU.mult, [bp, b_rs2, b_szt[zi]], [b_ozt[zi]])
                    if pendz:
                        pendz.pop()()
                    pendz.append(lambda slot=slot, T=T, zi=zi, f_=z_evac2: f_(slot, T, zi))

                nh = len(hslots)
                S1(0)
                S1(1)
                S2(0)
                for i in range(nh):
                    if i + 2 < nh:
                        S1(i + 2)
                    if i + 1 < nh:
                        S2(i + 1, 0)
                    S3(i)
                    if i + 1 < nh:
                        S2(i + 1, 1)
                    proj(wtz, bwz, 0, 512, [hslots[i]], hbT, b_hbT, z_evac)
            pendz.pop()()
            barrier()

            stop_check('p3')
            A.setrange(R2, ARENA)
            h2 = A.alloc([128, 9, D], F32); b_h2 = [Buf(f"h2{i}", f"h2") for i in range(9)]
            gfin = A.alloc([128, D], F32); b_gfin = Buf("gfin", "gfin")
            hpc = [A.alloc([128, 512], F32) for _ in range(NP)]; b_hpc = [Buf(f"hpc{i}", f"hpc{i}") for i in range(NP)]
            junk3 = A.alloc([128, 512], BF16); b_junk3 = Buf("junk3")
            S.dma("sp", gfin[:], gfin_d, writes=[b_gfin])
            cnt4 = {"n": 0}
            b_rs3s = [Buf(f"rs3s{i}") for i in range(9)]
            for i_ in range(9):
                b_rs3s[i_].w = b_rs3.w
            for blk in range(4):
                wt, bw = WLOAD([(w_out_b[:, 512 * blk:512 * (blk + 1)], 0)])

                def f_evac(slot, T, ps_ap, bp, blk=blk):
                    pi = cnt4["n"] % NP
                    cnt4["n"] += 1
                    cs = slice(512 * blk, 512 * (blk + 1))
                    r0 = slot * 128
                    S.dma("sp", hpc[pi][:T, :], h1s[r0:r0 + T, cs], reads=[b_h1s[slot]], writes=[b_hpc[pi]])
                    TT(h2[:T, slot, cs], ps_ap, hpc[pi][:T, :], ALU.add, [bp, b_hpc[pi]], [b_h2[slot]])
                    ACT(junk3[:T, :], h2[:T, slot, cs], AF.Square, [b_h2[slot]], [b_junk3, b_rs3s[slot]], accum_out=ss3[:T, slot, blk:blk + 1])
                    if blk == 3:
                        S.op("dve", lambda e, slot=slot, T=T: e.reduce_sum(out=rs3[:T, slot:slot + 1], in_=ss3[:T, slot, :], axis=AX.X), [b_rs3s[slot]], [b_rs3s[slot]])
                        RSTD(rs3[:T, slot:slot + 1], rs3[:T, slot:slot + 1], [b_rs3s[slot]], [b_rs3s[slot]])
                        STT(h2[:T, slot, :], h2[:T, slot, :], rs3[:T, slot:slot + 1], gfin[:T, :], ALU.mult, ALU.mult, [b_h2[slot], b_rs3s[slot], b_gfin], [b_h2[slot]])
                        if slot < 8:
                            S.dma("pool", y_o[slot * 128:(slot + 1) * 128, :], h2[:, slot, :], reads=[b_h2[slot]])
                        else:
                            S.dma("pool", ys_o, h2[:32, slot, :], reads=[b_h2[slot]])
                proj(wt, bw, 0, 512, hslots, ozT, b_ozT, f_evac)
        except _Stop:
            pass
        S.finish()
        print("instr counts", {k: len(v) for k, v in S.streams.items()}, "waits", S.nwaits, "sem counts", {k: v for k, v in S.cnt.items() if k.startswith("e_")})
        S.emit()
    return nc


_CACHE = {}


def _consts():
    import ml_dtypes
    if "c" in _CACHE:
        return _CACHE["c"]
    c = {}
    c["ident"] = np.eye(128, dtype=np.float32).astype(ml_dtypes.bfloat16)
    fm = np.zeros((128, 32), np.float32)
    for p_ in range(128):
        fm[p_, p_ % 32] = 1.0
    c["fold"] = fm.astype(ml_dtypes.bfloat16)
    s = np.arange(128)[:, None]
    t = np.arange(128)[None, :]
    glac = np.zeros((128, 2, 3, 128), np.float32)
    glac[:, 0, 0, :] = np.where(s <= t, -1.0 / 16, 0.0)
    glac[:, 0, 1, :] = np.where(s > t, -1.0 / 16, 0.0)
    glac[:, 0, 2, :] = np.where(s <= t, 1.0, 0.0)
    same = (s // 8 == t // 8) & (s < 32) & (t < 32)
    glac[:, 1, 0, :] = np.where(same & (s <= t), -1.0 / 16, 0.0)
    glac[:, 1, 1, :] = np.where(same & (s > t), -1.0 / 16, 0.0)
    glac[:, 1, 2, :] = np.where(same & (s <= t), 1.0, 0.0)
    c["glac"] = glac
    segi = np.zeros((128, 2, 4), np.float32)
    segi[:, 0, 0] = -1.0 / 16
    segm = np.zeros((128, 4), np.float32)
    for r in range(32):
        segi[r, 1, r // 8] = -1.0 / 16
        segm[r, r // 8] = 1.0
    c["segi"] = segi
    c["segm"] = segm
    slopes = np.exp2(-8.0 * np.arange(1, 33, dtype=np.float64) / 32.0)
    bp = np.full((4, 128, 2, 1024), NEG, np.float64)
    bs = np.full((4, 128, 5, 256), NEG, np.float64)
    sv = np.arange(128)[:, None]
    tv = np.arange(128)[None, :]
    t32 = np.arange(32)[None, :]
    samp = t32 // 8
    ii = t32 % 8
    for g in range(4):
        for half in range(2):
            for j in range(4):
                sl = slopes[8 * g + 2 * j + half]
                cc = half * 512 + j * 128
                dist = tv + 128 - sv
                bp[g, :, 0, cc:cc + 128] = np.where(sv >= tv, -sl * dist, NEG)
                dist = tv - sv
                bp[g, :, 1, cc:cc + 128] = np.where(dist >= 0, -sl * dist, NEG)
                c2 = half * 128 + j * 32
                for jb in range(4):
                    ok = (samp == jb) & (sv >= ii)
                    bs[g, :, jb, c2:c2 + 32] = np.where(ok, -sl * (128 + ii - sv), NEG)
                ss_ = sv // 8
                is_ = sv % 8
                ok = (ss_ == samp) & (is_ <= ii) & (sv < 32)
                bs[g, :, 4, c2:c2 + 32] = np.where(ok, -sl * (ii - is_), NEG)
    c["bias_p"] = bp.astype(np.float32)
    c["bias_s"] = bs.astype(np.float32)
    _CACHE["c"] = c
    return c


def kernel(x_prompt, x_sample, state_gla, cache_k_win, cache_v_win, g_norm_a, w_in_a, w_gate_up, b_gate,
           g_onorm_a, w_out_a, g_norm_kv, w_kv, g_norm_b, w_in_b, sinks, w_out_b, g_final):
    f = lambda a: np.ascontiguousarray(np.asarray(a, dtype=np.float32))
    x_prompt = f(x_prompt); x_sample = f(x_sample); state_gla = f(state_gla)
    cache_k_win = f(cache_k_win); cache_v_win = f(cache_v_win)
    w_in_a = f(w_in_a)[0]; w_out_a = f(w_out_a)[0]; w_kv = f(w_kv); w_in_b = f(w_in_b)[0]; w_out_b = f(w_out_b)[0]
    if "nc" not in _CACHE:
        _CACHE["nc"] = build_program()
    nc = _CACHE["nc"]
    c = _consts()
    pk = lambda v: np.ascontiguousarray(f(v).reshape(16, 128).T)
    gvec = np.ascontiguousarray(np.stack([pk(g_norm_a), pk(g_norm_kv), pk(g_norm_b)], axis=1))
    bc = lambda v, n: np.ascontiguousarray(np.broadcast_to(f(v).reshape(1, n), (128, n)))
    shared = dict(
        w_in_a=w_in_a, w_out_a=w_out_a, w_kv=w_kv, w_in_b=w_in_b, w_out_b=w_out_b,
        wgu=f(w_gate_up)[0], gon_bc=bc(g_onorm_a, 2048), gfin_bc=bc(g_final, 2048),
        bias_p=c["bias_p"], bias_s=c["bias_s"],
        cpack16=np.ascontiguousarray(np.concatenate([c["ident"], c["fold"]], axis=1)),
    )
    wglow_h = np.ascontiguousarray(w_in_a[:, 6144:6160].reshape(16, 128, 16).transpose(1, 0, 2)).reshape(128, 256)

    def cpack_for(pm):
        parts = [c["glac"].reshape(128, 768), c["segi"].reshape(128, 8), c["segm"], bc(b_gate, 1024), gvec.reshape(128, 48),
                 bc(sinks, 32), np.full((128, 1), pm, np.float32), wglow_h, np.zeros((128, 3), np.float32)]
        return np.ascontiguousarray(np.concatenate(parts, axis=1).astype(np.float32))
    in_maps = []
    for core in range(8):
        b, half = core // 2, core % 2
        if half == 0:
            xw = np.concatenate([np.zeros((1024, 2048), np.float32), x_prompt[b, :1024]], axis=0)
        else:
            xw = x_prompt[b]
        m = dict(shared)
        m["xw"] = np.ascontiguousarray(xw)
        m["xs"] = np.ascontiguousarray(x_sample[4 * core:4 * core + 4].reshape(32, 2048))
        m["st_in"] = np.ascontiguousarray(state_gla[0, 4 * core:4 * core + 4])
        m["ck"] = np.ascontiguousarray(cache_k_win[4 * core:4 * core + 4].reshape(4, 128, 256))
        m["cv"] = np.ascontiguousarray(cache_v_win[4 * core:4 * core + 4].reshape(4, 128, 256))
        m["cpack"] = cpack_for(NEG if half == 0 else 0.0)
        in_maps.append(m)
    if _CACHE.get('hook') is not None:
        return _CACHE['hook'](nc, in_maps)
    res = run_bass_kernel_spmd(nc, in_maps, core_ids=list(range(8)))
    R = res.results
    y_prompt = np.zeros((4, 2048, 2048), np.float32)
    y_sample = np.zeros((32, 8, 2048), np.float32)
    gla_prompt = np.zeros((1, 4, 4, 256, 512), np.float32)
    gla_sample = np.zeros((1, 32, 4, 256, 512), np.float32)
    kwp = np.zeros((4, 128, 4, 64), np.float32); vwp = np.zeros((4, 128, 4, 64), np.float32)
    kws = np.zeros((32, 128, 4, 64), np.float32); vws = np.zeros((32, 128, 4, 64), np.float32)
    for core in range(8):
        b, half = core // 2, core % 2
        r = R[core]
        y_prompt[b, half * 1024:(half + 1) * 1024] = r["y"]
        y_sample[4 * core:4 * core + 4] = np.asarray(r["ys"]).reshape(4, 8, 2048)
        gla_sample[0, 4 * core:4 * core + 4] = r["ssamp"]
        kws[4 * core:4 * core + 4] = np.asarray(r["kwin_s"]).reshape(4, 128, 4, 64)
        vws[4 * core:4 * core + 4] = np.asarray(r["vwin_s"]).reshape(4, 128, 4, 64)
        if half == 1:
            gla_prompt[0, b] = r["sfin"]
            kwp[b] = np.asarray(r["kwin"]).reshape(128, 4, 64)
            vwp[b] = np.asarray(r["vwin"]).reshape(128, 4, 64)
    return (y_prompt, y_sample, gla_prompt, gla_sample, kwp, vwp, kws, vws)
```
